# Optimizing a Trainium2 kernel written in Bass

```python
import math
import jax, jax.numpy as jnp
from jax import lax
import numpy as np

D_MODEL = 1024
BATCH = 16
SEQ = 4096
DEPTH = 4

N_MIXERS = 3
N_CONV_LAYERS = (DEPTH + 2) // 3
N_POOL_LAYERS = (DEPTH + 1) // 3
N_NSA_LAYERS = DEPTH // 3

CONV_WIDTH = 3
D_FF = 2816

POOL_WINDOWS = (2, 4, 8, 16)
N_POOL_GROUPS = 4
POOL_GROUP = D_MODEL // N_POOL_GROUPS

N_HEADS = 16
HEAD_DIM = D_MODEL // N_HEADS
N_KV_GROUPS = 4
HEADS_PER_GROUP = N_HEADS // N_KV_GROUPS
KV_WIDTH = N_KV_GROUPS * HEAD_DIM
CMP_BLOCK = 32
CMP_STRIDE = 16
CMP_HIDDEN = 256
SEL_BLOCK = 64
N_SELECT = 16
WINDOW = 512
Q_BLOCK = 128
N_BRANCH = 3
NSA_IN = D_MODEL + 6 * KV_WIDTH + N_BRANCH * N_HEADS

ALPHA = (2.0 * DEPTH) ** 0.25
BETA = (8.0 * DEPTH) ** -0.25
LN_EPS = 1e-5
NEG_INF = -1e30
FORCE_SCORE = 1e4

kernel_name = "hybrid_conv_pool_nsa_deepnorm_adaln"


def layer_norm(x, g, b):
    xf = x.astype(jnp.float32)
    mu = jnp.mean(xf, -1, keepdims=True)
    var = jnp.mean(jnp.square(xf - mu), -1, keepdims=True)
    y = (xf - mu) * lax.rsqrt(var + LN_EPS)
    return (y * g.astype(jnp.float32) + b.astype(jnp.float32)).astype(x.dtype)


def causal_dwconv(x, w):
    k = w.shape[0]
    return lax.conv_general_dilated(
        x, w[:, None, :].astype(x.dtype), window_strides=(1,), padding=((k - 1, 0),),
        dimension_numbers=("NWC", "WIO", "NWC"), feature_group_count=x.shape[-1])


def short_conv_mixer(h, w_in, w_conv, w_out):
    b_gate, c_gate, u = jnp.split(h @ w_in, 3, axis=-1)
    return (b_gate * causal_dwconv(c_gate * u, w_conv)) @ w_out


def causal_window_mean(u, w):
    t = u.shape[1]
    cs = jnp.cumsum(u.astype(jnp.float32), axis=1)
    prev = jnp.pad(cs, ((0, 0), (w, 0), (0, 0)))[:, :t]
    cnt = jnp.minimum(jnp.arange(1, t + 1, dtype=jnp.float32), float(w))
    return ((cs - prev) / cnt[None, :, None]).astype(u.dtype)


def pool_mixer(h, w_in, w_grp, scale, w_out):
    bsz, t, _ = h.shape
    u = (h @ w_in).reshape(bsz, t, N_POOL_GROUPS, POOL_GROUP)
    pooled = jnp.stack([causal_window_mean(u[:, :, g], w) - u[:, :, g]
                        for g, w in enumerate(POOL_WINDOWS)], axis=2)
    z = jnp.einsum("btgc,gce->btge", pooled, w_grp).reshape(bsz, t, D_MODEL)
    return (z * scale) @ w_out


def alibi_slopes():
    hh = jnp.arange(1, N_HEADS + 1, dtype=jnp.float32)
    return jnp.exp2(-8.0 * hh / N_HEADS).reshape(N_KV_GROUPS, HEADS_PER_GROUP)


def nsa_mixer(h, w_in, cmp_pos_k, cmp_pos_v, cmp_k_w1, cmp_k_w2, cmp_v_w1, cmp_v_w2, w_out):
    bsz, t, _ = h.shape
    dt = h.dtype
    f32 = jnp.float32
    G, R, HD = N_KV_GROUPS, HEADS_PER_GROUP, HEAD_DIM
    splits = np.cumsum([D_MODEL] + [KV_WIDTH] * 6).tolist()
    q, kc, vc, ks, vs, kw, vw, gl = jnp.split(h @ w_in, splits, axis=-1)
    q = q.reshape(bsz, t, G, R, HD)
    kc, vc, ks, vs, kw, vw = [a.reshape(bsz, t, G, HD) for a in (kc, vc, ks, vs, kw, vw)]
    gates = jax.nn.sigmoid(gl).reshape(bsz, t, G, R, N_BRANCH)

    n_cmp = (t - CMP_BLOCK) // CMP_STRIDE + 1
    blk_idx = jnp.arange(n_cmp)[:, None] * CMP_STRIDE + jnp.arange(CMP_BLOCK)[None, :]

    def compress(a, pos, w1, w2):
        blocks = a[:, blk_idx] + pos[None, None, :, None, :]
        hid = jax.nn.gelu(jnp.einsum("bnlgd,lde->bnge", blocks, w1))
        return jnp.einsum("bnge,ed->bngd", hid, w2)

    k_cmp = compress(kc, cmp_pos_k, cmp_k_w1, cmp_k_w2)
    v_cmp = compress(vc, cmp_pos_v, cmp_v_w1, cmp_v_w2)
    cmp_start = jnp.arange(n_cmp, dtype=jnp.int32) * CMP_STRIDE
    cmp_end = cmp_start + (CMP_BLOCK - 1)
    cmp_centre = cmp_start.astype(f32) + 0.5 * (CMP_BLOCK - 1)

    n_sel_blocks = t // SEL_BLOCK
    n_sel = min(N_SELECT, n_sel_blocks)
    sel_start = jnp.arange(n_sel_blocks, dtype=jnp.int32) * SEL_BLOCK
    overlap = ((cmp_start[:, None] <= sel_start[None, :] + SEL_BLOCK - 1)
               & (cmp_end[:, None] >= sel_start[None, :])).astype(f32)
    k_sel = ks.reshape(bsz, n_sel_blocks, SEL_BLOCK, G, HD).transpose(0, 3, 1, 2, 4)
    v_sel = vs.reshape(bsz, n_sel_blocks, SEL_BLOCK, G, HD).transpose(0, 3, 1, 2, 4)

    k_win = jnp.pad(kw, ((0, 0), (WINDOW, 0), (0, 0), (0, 0)))
    v_win = jnp.pad(vw, ((0, 0), (WINDOW, 0), (0, 0), (0, 0)))

    slopes = alibi_slopes()
    scale = HEAD_DIM ** -0.5
    n_qb = t // Q_BLOCK
    g_ar = jnp.arange(G)[:, None, None]
    sel_off = jnp.arange(SEL_BLOCK, dtype=jnp.int32)
    win_off = jnp.arange(Q_BLOCK + WINDOW, dtype=jnp.int32) - WINDOW

    def step(args):
        b, q0, qb, gb = args
        tq = q0 + jnp.arange(Q_BLOCK, dtype=jnp.int32)
        tf = tq.astype(f32)

        kc_b = lax.dynamic_index_in_dim(k_cmp, b, 0, keepdims=False)
        vc_b = lax.dynamic_index_in_dim(v_cmp, b, 0, keepdims=False)
        s_c = jnp.einsum("qgrd,ngd->grqn", qb, kc_b).astype(f32) * scale
        dist_c = tf[:, None] - cmp_centre[None, :]
        valid_c = cmp_end[None, :] <= tq[:, None]
        s_c = jnp.where(valid_c, s_c - slopes[:, :, None, None] * dist_c, NEG_INF)
        has_c = (tq >= CMP_BLOCK - 1).astype(f32)[None, None, :, None]
        p_c = jax.nn.softmax(s_c, axis=-1) * has_c
        o_c = jnp.einsum("grqn,ngd->qgrd", p_c.astype(dt), vc_b)

        imp = jnp.einsum("grqn,nj->gqj", p_c, overlap)
        bt = tq // SEL_BLOCK
        jb = jnp.arange(n_sel_blocks, dtype=jnp.int32)[None, :]
        forced = (jb == 0) | (jb == bt[:, None]) | (jb == bt[:, None] - 1)
        causal_b = jb <= bt[:, None]
        sel_score = jnp.where(forced[None], FORCE_SCORE, jnp.where(causal_b[None], imp, -FORCE_SCORE))
        _, idx = lax.top_k(sel_score, n_sel)

        ks_b = lax.dynamic_index_in_dim(k_sel, b, 0, keepdims=False)
        vs_b = lax.dynamic_index_in_dim(v_sel, b, 0, keepdims=False)
        kg = ks_b[g_ar, idx]
        vg = vs_b[g_ar, idx]
        s_s = jnp.einsum("qgrd,gqksd->grqks", qb, kg).astype(f32) * scale
        dist_s = tq[None, :, None, None] - (idx[..., None] * SEL_BLOCK + sel_off)
        s_s = jnp.where((dist_s >= 0)[:, None],
                        s_s - slopes[:, :, None, None, None] * dist_s.astype(f32)[:, None], NEG_INF)
        p_s = jax.nn.softmax(s_s.reshape(G, R, Q_BLOCK, n_sel * SEL_BLOCK), axis=-1)
        p_s = p_s.reshape(G, R, Q_BLOCK, n_sel, SEL_BLOCK)
        o_s = jnp.einsum("grqks,gqksd->qgrd", p_s.astype(dt), vg)

        kw_b = lax.dynamic_slice(k_win, (b, q0, 0, 0), (1, Q_BLOCK + WINDOW, G, HD))[0]
        vw_b = lax.dynamic_slice(v_win, (b, q0, 0, 0), (1, Q_BLOCK + WINDOW, G, HD))[0]
        s_pos = q0 + win_off
        dist_w = tq[:, None] - s_pos[None, :]
        valid_w = (dist_w >= 0) & (dist_w < WINDOW) & (s_pos[None, :] >= 0)
        s_w = jnp.einsum("qgrd,kgd->grqk", qb, kw_b).astype(f32) * scale
        s_w = jnp.where(valid_w, s_w - slopes[:, :, None, None] * dist_w.astype(f32), NEG_INF)
        p_w = jax.nn.softmax(s_w, axis=-1)
        o_w = jnp.einsum("grqk,kgd->qgrd", p_w.astype(dt), vw_b)

        o = gb[..., 0:1] * o_c + gb[..., 1:2] * o_s + gb[..., 2:3] * o_w
        return o.reshape(Q_BLOCK, D_MODEL)

    b_idx = jnp.repeat(jnp.arange(bsz, dtype=jnp.int32), n_qb)
    q0s = jnp.tile(jnp.arange(n_qb, dtype=jnp.int32) * Q_BLOCK, bsz)
    q_x = q.reshape(bsz * n_qb, Q_BLOCK, G, R, HD)
    g_x = gates.reshape(bsz * n_qb, Q_BLOCK, G, R, N_BRANCH)
    o = lax.map(step, (b_idx, q0s, q_x, g_x)).reshape(bsz, t, D_MODEL)
    return o @ w_out


def conv_ffn(h, w_in, w_conv, w_out):
    a, v = jnp.split(h @ w_in, 2, axis=-1)
    return (jax.nn.gelu(causal_dwconv(a, w_conv)) * v) @ w_out


def setup_inputs(seed: int = 0) -> dict:
    key = jax.random.key(seed)
    keys = iter(jax.random.split(key, 32))
    D = D_MODEL

    def nrm(shape, s):
        return jax.random.normal(next(keys), shape, jnp.float32) * s

    return {
        "x": nrm((BATCH, SEQ, D), 1.0),
        "c": nrm((BATCH, D), 1.0),
        "ada_w": nrm((DEPTH, D, 6 * D), 0.1 * D ** -0.5),
        "ada_b": nrm((DEPTH, 6 * D), 0.02),
        "ln1_g": 1.0 + nrm((DEPTH, D), 0.02),
        "ln1_b": nrm((DEPTH, D), 0.02),
        "ln2_g": 1.0 + nrm((DEPTH, D), 0.02),
        "ln2_b": nrm((DEPTH, D), 0.02),
        "ffn_w_in": nrm((DEPTH, D, 2 * D_FF), D ** -0.5),
        "ffn_conv": nrm((DEPTH, CONV_WIDTH, D_FF), CONV_WIDTH ** -0.5),
        "ffn_w_out": nrm((DEPTH, D_FF, D), BETA * D_FF ** -0.5),
        "conv_w_in": nrm((N_CONV_LAYERS, D, 3 * D), D ** -0.5),
        "conv_w": nrm((N_CONV_LAYERS, CONV_WIDTH, D), CONV_WIDTH ** -0.5),
        "conv_w_out": nrm((N_CONV_LAYERS, D, D), BETA * D ** -0.5),
        "pool_w_in": nrm((N_POOL_LAYERS, D, D), D ** -0.5),
        "pool_w_grp": nrm((N_POOL_LAYERS, N_POOL_GROUPS, POOL_GROUP, POOL_GROUP), POOL_GROUP ** -0.5),
        "pool_scale": 1.0 + nrm((N_POOL_LAYERS, D), 0.02),
        "pool_w_out": nrm((N_POOL_LAYERS, D, D), BETA * D ** -0.5),
        "nsa_w_in": nrm((N_NSA_LAYERS, D, NSA_IN), D ** -0.5),
        "nsa_cmp_pos_k": nrm((N_NSA_LAYERS, CMP_BLOCK, HEAD_DIM), 0.1),
        "nsa_cmp_pos_v": nrm((N_NSA_LAYERS, CMP_BLOCK, HEAD_DIM), 0.1),
        "nsa_cmp_k_w1": nrm((N_NSA_LAYERS, CMP_BLOCK, HEAD_DIM, CMP_HIDDEN), (CMP_BLOCK * HEAD_DIM) ** -0.5),
        "nsa_cmp_k_w2": nrm((N_NSA_LAYERS, CMP_HIDDEN, HEAD_DIM), CMP_HIDDEN ** -0.5),
        "nsa_cmp_v_w1": nrm((N_NSA_LAYERS, CMP_BLOCK, HEAD_DIM, CMP_HIDDEN), (CMP_BLOCK * HEAD_DIM) ** -0.5),
        "nsa_cmp_v_w2": nrm((N_NSA_LAYERS, CMP_HIDDEN, HEAD_DIM), CMP_HIDDEN ** -0.5),
        "nsa_w_out": nrm((N_NSA_LAYERS, D, D), BETA * D ** -0.5),
    }


def reference(x, c, ada_w, ada_b, ln1_g, ln1_b, ln2_g, ln2_b, ffn_w_in, ffn_conv, ffn_w_out,
              conv_w_in, conv_w, conv_w_out, pool_w_in, pool_w_grp, pool_scale, pool_w_out,
              nsa_w_in, nsa_cmp_pos_k, nsa_cmp_pos_v, nsa_cmp_k_w1, nsa_cmp_k_w2,
              nsa_cmp_v_w1, nsa_cmp_v_w2, nsa_w_out):
    cond = jax.nn.silu(c)
    for i in range(DEPTH):
        mod = (cond @ ada_w[i] + ada_b[i])[:, None, :]
        sh1, sc1, g1, sh2, sc2, g2 = jnp.split(mod, 6, axis=-1)
        m, j = i % N_MIXERS, i // N_MIXERS
        h = x * (1 + sc1) + sh1
        if m == 0:
            y = short_conv_mixer(h, conv_w_in[j], conv_w[j], conv_w_out[j])
        elif m == 1:
            y = pool_mixer(h, pool_w_in[j], pool_w_grp[j], pool_scale[j], pool_w_out[j])
        else:
            y = nsa_mixer(h, nsa_w_in[j], nsa_cmp_pos_k[j], nsa_cmp_pos_v[j], nsa_cmp_k_w1[j],
                          nsa_cmp_k_w2[j], nsa_cmp_v_w1[j], nsa_cmp_v_w2[j], nsa_w_out[j])
        x = layer_norm(ALPHA * x + (1 + g1) * y, ln1_g[i], ln1_b[i])
        h = x * (1 + sc2) + sh2
        x = layer_norm(ALPHA * x + (1 + g2) * conv_ffn(h, ffn_w_in[i], ffn_conv[i], ffn_w_out[i]),
                       ln2_g[i], ln2_b[i])
    return x
```

```python
import math
from contextlib import ExitStack
import numpy as np
import concourse.bass as bass
import concourse.mybir as mybir
from concourse.bass_utils import run_bass_kernel_spmd

F32 = mybir.dt.float32
BF16 = mybir.dt.bfloat16
AF = mybir.ActivationFunctionType
ALU = mybir.AluOpType

D = 1024
T = 4096
NB = 2
DEPTH = 4
DFF = 2816
KD = D // 128
KF = DFF // 128
ALPHA = (2.0 * DEPTH) ** 0.25
LN_EPS = 1e-5
EPS_P = LN_EPS / (ALPHA * ALPHA)
NSA_IN = 2608
POOL_W = (2, 4, 8, 16)

VOFF = {}
_nv = 0


def _valloc(name, n):
    global _nv
    VOFF[name] = _nv
    _nv += n


_valloc("ada_b", DEPTH * 48)
_valloc("ln1_g", DEPTH * 8)
_valloc("ln1_b", DEPTH * 8)
_valloc("ln2_g", DEPTH * 8)
_valloc("ln2_b", DEPTH * 8)
_valloc("ffn_conv", DEPTH * 3 * KF)
_valloc("conv_w", 2 * 3 * KD)
_valloc("pool_scale", KD)
_valloc("pool_rc", 4 * 16)
NV = _nv


def _pm(v):
    v = np.asarray(v, np.float32)
    return np.ascontiguousarray(v.reshape(-1, 128).T)


class Tok:
    __slots__ = ("key", "sem", "val")

    def __init__(self, key, sem, val):
        self.key, self.sem, self.val = key, sem, val


class Ctx:
    def __init__(self, nc):
        self.nc = nc
        self.eng = {"pe": nc.tensor, "act": nc.scalar, "dve": nc.vector, "pool": nc.gpsimd, "sp": nc.sync}
        self.sem = {e: nc.alloc_semaphore(name=f"s_{e}") for e in self.eng}
        self.cnt = {e: 0 for e in self.eng}
        self.seen = {e: {} for e in self.eng}
        self.ndma = 0

    def sig(self, e, ins):
        ins.then_inc(self.sem[e], 1)
        self.cnt[e] += 1
        return Tok(e, self.sem[e], self.cnt[e])

    def wait(self, e, *toks):
        for t in toks:
            if t is None:
                continue
            if isinstance(t, (list, tuple)):
                self.wait(e, *t)
                continue
            if t.key == e:
                continue
            if self.seen[e].get(t.key, 0) >= t.val:
                continue
            self.eng[e].wait_ge(t.sem, t.val)
            self.seen[e][t.key] = t.val

    def dsem(self, name):
        self.ndma += 1
        return [f"d{self.ndma}_{name}", self.nc.alloc_semaphore(name=f"d{self.ndma}_{name}"), 0]

    def dma(self, q, ds, out, in_):
        ins = self.eng[q].dma_start(out=out, in_=in_)
        ins.then_inc(ds[1], 16)
        ds[2] += 16
        return Tok(ds[0], ds[1], ds[2])

    def barrier(self, extra=()):
        toks = []
        for e in self.eng:
            self.wait(e, *extra)
        for e in ("pe", "act", "dve", "pool", "sp"):
            toks.append(self.sig(e, self.eng[e].drain()))
        for e in self.eng:
            self.wait(e, *toks)


class Ring:
    def __init__(self, items):
        self.items = list(items)
        self.free = [[] for _ in self.items]
        self.i = -1

    def next(self):
        self.i = (self.i + 1) % len(self.items)
        fr = self.free[self.i]
        self.free[self.i] = []
        return self.i, self.items[self.i], fr

    def release(self, idx, *toks):
        self.free[idx].extend(t for t in toks if t is not None)


def _mm_group(cx, out, pairs, last_sig=True):
    n = len(pairs)
    tok = None
    for i, (l, r) in enumerate(pairs):
        ins = cx.nc.tensor.matmul(out, l, r, start=(i == 0), stop=(i == n - 1))
        if i == n - 1 and last_sig:
            tok = cx.sig("pe", ins)
    return tok


def build_program(layers=(0, 1, 2, 3), ntiles_dbg=None, dbg=False):
    nc = bass.Bass("TRN2", target_bir_lowering=False)
    cx = Ctx(nc)
    pe, act, dve, pool, sp = nc.tensor, nc.scalar, nc.vector, nc.gpsimd, nc.sync

    def din(name, shape, dt=F32):
        return nc.dram_tensor(name, list(shape), dt, kind="ExternalInput").ap()

    xT = din("xT", [NB, D, T])
    condT = din("condT", [128, KD, NB])
    vecs_d = din("vecs", [128, NV])
    ada_w = din("ada_w", [DEPTH, D, 6 * D])
    ffn_w_in = din("ffn_w_in", [DEPTH, D, 2 * DFF])
    ffn_w_out = din("ffn_w_out", [DEPTH, DFF, D])
    conv_w_in = din("conv_w_in", [2, D, 3 * D])
    conv_w_out = din("conv_w_out", [2, D, D])
    pool_w_in = din("pool_w_in", [1, D, D])
    pool_w_grp = din("pool_w_grp", [1, 4, 256, 256])
    pool_w_out = din("pool_w_out", [1, D, D])
    nsa_w_in = din("nsa_w_in", [1, D, NSA_IN])
    nsa_posT = din("nsa_posT", [2, 64, 32])
    nsa_cmp_k_w1 = din("nsa_cmp_k_w1", [1, 32, 64, 256])
    nsa_cmp_k_w2 = din("nsa_cmp_k_w2", [1, 256, 64])
    nsa_cmp_v_w1 = din("nsa_cmp_v_w1", [1, 32, 64, 256])
    nsa_cmp_v_w2 = din("nsa_cmp_v_w2", [1, 256, 64])
    nsa_w_out = din("nsa_w_out", [1, D, D])
    c_ident = din("c_ident", [128, 128])
    c_negtri = din("c_negtri", [128, 2, 512])
    c_cmask = din("c_cmask", [128, 17, 512])
    c_ovl = din("c_ovl", [128, 2, 64])
    c_cms = din("c_cms", [128, 128])
    c_ads = din("c_ads", [128, 128])
    c_btab = din("c_btab", [128, 32, 16])
    c_bC = din("c_bC", [128, 2, 32, 16])
    c_ind = din("c_ind", [64, T])
    c_lx = din("c_lx", [7, 32, 128])
    c_lxc = din("c_lxc", [7, 2, 128])
    c_ext = din("c_ext", [7, 32, 2, 16])

    def dscr(name, shape, dt):
        return nc.dram_tensor(name, list(shape), dt, kind=("ExternalOutput" if dbg else "Internal")).ap()

    q_d = dscr("q_d", [NB, 16, 64, T], BF16)
    kc_d = dscr("kc_d", [NB, 4, 64, T], BF16)
    vc_d = dscr("vc_d", [NB, 4, 64, T], BF16)
    ks_d = dscr("ks_d", [NB, 4, 64, T], BF16)
    kw_d = dscr("kw_d", [NB, 4, 64, T], BF16)
    vs_d = dscr("vs_d", [NB, T, 260], BF16)
    vw_d = dscr("vw_d", [NB, T, 260], BF16)
    gate_d = dscr("gate_d", [NB, T, 48], F32)
    o_d = dscr("o_d", [NB, D, T], BF16)
    w2bf_d = nc.dram_tensor("w2bf_d", [KD, 128, KF, 128], BF16, kind="Internal").ap()
    outT = nc.dram_tensor("outT", [NB, D, T], F32, kind="ExternalOutput").ap()
    actA = nc.dram_tensor("actA", [NB, D, T], F32, kind=("ExternalOutput" if dbg else "Internal")).ap()
    actB = nc.dram_tensor("actB", [NB, D, T], F32, kind=("ExternalOutput" if dbg else "Internal")).ap()

    def sb(name, shape, dt=F32):
        return nc.sbuf_tensor(name, list(shape), dt).__enter__()

    def ps(name, shape=(128, 512), dt=F32):
        return nc.psum_tensor(name, list(shape), dt).__enter__()

    vecs = sb("vecs_sb", [128, NV])
    modT = sb("modT", [128, DEPTH, 48, NB])
    gsc = sb("gsc", [128, DEPTH, 2, KD, NB])
    ones_bf = sb("ones_bf", [128, 128], BF16)
    eps_col = sb("eps_col", [128, 1])
    banks = [ps(f"bank{i}") for i in range(8)]

    d_const = cx.dsem("const")
    t_vecs = cx.dma("sp", d_const, vecs[:], vecs_d[:, :])
    cond = sb("cond", [128, KD, NB])
    t_cond = cx.dma("sp", d_const, cond[:], condT[:, :, :])
    t_const = t_cond

    cx.wait("dve", t_const)
    dve.memset(ones_bf[:], 1.0 / 1024.0)
    t_c1 = cx.sig("dve", dve.memset(eps_col[:], EPS_P))

    cx.wait("act", t_const)
    t_silu = cx.sig("act", act.activation(out=cond[:], in_=cond[:], func=AF.Silu))
    adw_cm = [nc.sbuf_tensor(f"adw{i}", [128, KD, 768], F32) for i in range(2)]
    adw = [c_.__enter__() for c_ in adw_cm]
    adw_ring = Ring(adw)
    d_adw = [cx.dsem("adw0"), cx.dsem("adw1")]
    mod_ps = banks[0]
    tok_mod_evac = None
    for l in range(DEPTH):
        for piece in range(8):
            i, wt, fr = adw_ring.next()
            cx.wait("sp", *fr)
            tl = None
            for k in range(KD):
                tl = cx.dma("sp", d_adw[i], wt[:, k, :], ada_w[l, k * 128:(k + 1) * 128, piece * 768:(piece + 1) * 768])
            cx.wait("pe", tl, t_silu, tok_mod_evac)
            tk = None
            for j in range(6):
                f = piece * 6 + j
                tk = _mm_group(cx, mod_ps[:, f * NB:(f + 1) * NB],
                               [(wt[:, k, j * 128:(j + 1) * 128], cond[:, k, :]) for k in range(KD)],
                               last_sig=(j == 5))
            adw_ring.release(i, tk)
        cx.wait("dve", tk, t_const)
        tok_mod_evac = cx.sig("dve", dve.tensor_tensor(
            out=modT[:, l, :, :],
            in0=mod_ps[:, 0:48 * NB].rearrange("p (f b) -> p f b", b=NB),
            in1=vecs[:, VOFF["ada_b"] + l * 48: VOFF["ada_b"] + (l + 1) * 48].unsqueeze(2).to_broadcast([128, 48, NB]),
            op=ALU.add))
    for l in range(DEPTH):
        dve.tensor_scalar(out=modT[:, l, 8:16, :], in0=modT[:, l, 8:16, :], scalar1=1.0, scalar2=None, op0=ALU.add)
        dve.tensor_scalar(out=modT[:, l, 32:40, :], in0=modT[:, l, 32:40, :], scalar1=1.0, scalar2=None, op0=ALU.add)
        dve.tensor_scalar(out=gsc[:, l, 0, :, :], in0=modT[:, l, 16:24, :], scalar1=1.0, scalar2=1.0 / ALPHA,
                          op0=ALU.add, op1=ALU.mult)
        tok_mod = cx.sig("dve", dve.tensor_scalar(out=gsc[:, l, 1, :, :], in0=modT[:, l, 40:48, :], scalar1=1.0,
                                                   scalar2=1.0 / ALPHA, op0=ALU.add, op1=ALU.mult))
    cx.barrier()
    for c_ in reversed(adw_cm):
        c_.__exit__(None, None, None)

    def vcol(name, idx):
        o = VOFF[name] + idx
        return vecs[:, o:o + 1]

    class Epi:
        def __init__(self):
            self.TN = 512
            self.zb_t = [sb(f"zb{i}", [128, 512], BF16) for i in range(2)]
            self.zq_t = [sb(f"zq{i}", [128, 512], BF16) for i in range(2)]
            self.m2_t = sb("m2", [128, 512])
            self.rstd_t = sb("rstd", [128, 512])
            self.nmr_t = sb("nmr", [128, 512])
            self.stat_free = []
            self.set_tn(512)

        def set_tn(self, TN):
            self.TN = TN
            fz = getattr(self, "zb", None)
            self.zb = Ring([t[:, 0:TN] for t in self.zb_t])
            self.zq = Ring([t[:, 0:TN] for t in self.zq_t])
            if fz is not None:
                pass
            self.m2 = self.m2_t[:, 0:TN]
            self.rstd = self.rstd_t[:, 0:TN]
            self.nmr = self.nmr_t[:, 0:TN]
            return self

    ep_glob = Epi()

    def epilogue(ep, l, which, b, xs, y_ring, ymm, lng, lnb, x_ready, yrel=None):
        TN = ep.TN
        s1, s2 = banks[6], banks[7]
        cx.wait("pe", *ep.stat_free)
        ep_free_tmp = ep.stat_free
        ep.stat_free = []
        pend = None
        z_toks = []
        for m in range(KD):
            yi, ybank, fr = y_ring.next()
            cx.wait("pe", *fr)
            ty = _mm_group(cx, ybank[:, 0:TN], ymm(m))
            if yrel is not None:
                yrel(m, ty)
            if pend is not None:
                pm, tzb, tzq, zi, qi, zbt, zqt = pend
                cx.wait("pe", tzb, tzq)
                pe.matmul(s1[:, 0:TN], ones_bf[:], zbt, start=(pm == 0), stop=(pm == KD - 1))
                tst = cx.sig("pe", pe.matmul(s2[:, 0:TN], ones_bf[:], zqt, start=(pm == 0), stop=(pm == KD - 1)))
                ep.zb.release(zi, tst)
                ep.zq.release(qi, tst)
            cx.wait("dve", ty, x_ready, *ep_free_tmp)
            tz = cx.sig("dve", dve.scalar_tensor_tensor(out=xs[:, m, :], in0=ybank[:, 0:TN], scalar=gsc[:, l, which, m, b:b + 1],
                                                        in1=xs[:, m, :], op0=ALU.mult, op1=ALU.add))
            y_ring.release(yi, tz)
            z_toks.append(tz)
            zi, zbt, fr1 = ep.zb.next()
            qi, zqt, fr2 = ep.zq.next()
            cx.wait("act", tz, *fr1, *fr2)
            tzb = cx.sig("act", act.activation(out=zbt, in_=xs[:, m, :], func=AF.Identity))
            tzq = cx.sig("act", act.activation(out=zqt, in_=xs[:, m, :], func=AF.Square))
            pend = (m, tzb, tzq, zi, qi, zbt, zqt)
        pm, tzb, tzq, zi, qi, zbt, zqt = pend
        cx.wait("pe", tzb, tzq)
        pe.matmul(s1[:, 0:TN], ones_bf[:], zbt, start=False, stop=True)
        tst = cx.sig("pe", pe.matmul(s2[:, 0:TN], ones_bf[:], zqt, start=False, stop=True))
        ep.zb.release(zi, tst)
        ep.zq.release(qi, tst)
        cx.wait("act", tst)
        tm2 = cx.sig("act", act.activation(out=ep.m2, in_=s1[:, 0:TN], func=AF.Square))
        cx.wait("dve", tm2, tst)
        tv = cx.sig("dve", dve.scalar_tensor_tensor(out=ep.rstd, in0=s2[:, 0:TN], scalar=eps_col[:, 0:1], in1=ep.m2,
                                                    op0=ALU.add, op1=ALU.subtract))
        cx.wait("act", tv)
        tsd = cx.sig("act", act.activation(out=ep.rstd, in_=ep.rstd, func=AF.Sqrt))
        cx.wait("dve", tsd)
        dve.reciprocal(out=ep.rstd, in_=ep.rstd)
        tn = cx.sig("dve", dve.scalar_tensor_tensor(out=ep.nmr, in0=s1[:, 0:TN], scalar=-1.0, in1=ep.rstd,
                                                    op0=ALU.mult, op1=ALU.mult))
        ep.stat_free.append(tn)
        tx = None
        for m in range(KD):
            dve.tensor_tensor(out=xs[:, m, :], in0=xs[:, m, :], in1=ep.rstd, op=ALU.mult)
            tt = cx.sig("dve", dve.tensor_tensor(out=xs[:, m, :], in0=xs[:, m, :], in1=ep.nmr, op=ALU.add))
            cx.wait("act", tt)
            tx = cx.sig("act", act.activation(out=xs[:, m, :], in_=xs[:, m, :], func=AF.Identity,
                                              scale=vcol(lng, l * 8 + m), bias=vcol(lnb, l * 8 + m)))
        ep.stat_free.append(tt)
        return tx

    def load_w(ds, dst, src, nk, after=()):
        P = dst.shape[0]
        cx.wait("pool", *after)
        tk = None
        for k in range(nk):
            tk = cx.dma("pool", ds, dst[:, k, :], src[k * P:(k + 1) * P, :])
        return tk

    d_ffnw = cx.dsem("ffnw")
    ffn_w_free = []


    def make_h(dst, xs, l, sc0, sh0, b, x_ready, free):
        cx.wait("pool", x_ready, tok_mod, *free)
        tk = None
        for m in range(KD):
            tk = cx.sig("pool", pool.tensor_scalar(out=dst[:, m, :], in0=xs[:, m, :], scalar1=modT[:, l, sc0 + m, b:b + 1],
                                                   scalar2=modT[:, l, sh0 + m, b:b + 1], op0=ALU.mult, op1=ALU.add))
        return tk

    def ffn_pass(l, src, dst):
        TN = 512
        NT = T // TN
        cw = VOFF["ffn_conv"] + l * 3 * KF
        d_s1 = cx.dsem("w2s1")
        d_s2 = cx.dsem("w2s2")
        with nc.sbuf_tensor(f"w2stage{l}", [128, KF, D], BF16) as w2st:
            t1 = load_w(d_s1, w2st, ffn_w_out[l], KF)
            cx.wait("sp", t1)
            t_w2bf = None
            for m in range(KD):
                t_w2bf = cx.dma("sp", d_s2, w2bf_d[m], w2st[:, :, m * 128:(m + 1) * 128])
        with ExitStack() as es1:
            w1s = es1.enter_context(nc.sbuf_tensor(f"ffn_w1_{l}", [128, KD, 2 * DFF], BF16))
            w2r = es1.enter_context(nc.sbuf_tensor(f"ffn_w2r_{l}", [128, 4, KF, 128], BF16))
            xbuf = es1.enter_context(nc.sbuf_tensor(f"fx{l}", [128, 2, KD, TN], F32))
            hbuf = es1.enter_context(nc.sbuf_tensor(f"fh{l}", [128, 1, KD, TN], BF16))
            hid = es1.enter_context(nc.sbuf_tensor(f"fhid{l}", [128, KF, TN], BF16))
            abuf = es1.enter_context(nc.sbuf_tensor(f"fab{l}", [128, 2, TN + 2], F32))
            accb = es1.enter_context(nc.sbuf_tensor(f"facc{l}", [128, 2, TN], F32))
            glb = es1.enter_context(nc.sbuf_tensor(f"fgl{l}", [128, 2, TN], F32))
            carry = es1.enter_context(nc.sbuf_tensor(f"fcar{l}", [128, KF, 2], F32))
            for e_ in ("pool", "sp", "act", "dve", "pe"):
                cx.wait(e_, t_w2bf)
            t_w = load_w(d_ffnw, w1s, ffn_w_in[l], KD)
            w2ring = Ring([w2r[:, i] for i in range(4)])
            d_w2 = [cx.dsem(f"w2r{i}") for i in range(4)]
            w2tok = {}

            def issue_w2(m):
                wi_, wt_, fr = w2ring.next()
                cx.wait("sp", *fr)
                w2tok[m] = (wi_, wt_, cx.dma("sp", d_w2[wi_], wt_.rearrange("p c n -> p (c n)"), w2bf_d[m].rearrange("p c n -> p (c n)")))

            def ymm_ffn(m):
                wi_, wt_, tk_ = w2tok[m]
                cx.wait("pe", tk_)
                return [(wt_[:, c, :], hid[:, c, :]) for c in range(KF)]

            def yrel_ffn(m, ty):
                wi_, wt_, tk_ = w2tok[m]
                w2ring.release(wi_, ty)
                if m + 4 < KD:
                    issue_w2(m + 4)
            ep = ep_glob.set_tn(TN)
            xring = Ring([xbuf[:, i] for i in range(2)])
            hring = Ring([hbuf[:, i] for i in range(1)])
            p1 = Ring(banks[0:4])
            yring = Ring(banks[4:6])
            aring = Ring([abuf[:, i] for i in range(2)])
            cring = Ring([accb[:, i] for i in range(2)])
            gring = Ring([glb[:, i] for i in range(2)])
            d_x = [cx.dsem(f"fx{i}") for i in range(2)]
            d_st = cx.dsem("fst")
            tiles = [(b, t) for b in range(NB) for t in range(NT)]
            if ntiles_dbg:
                tiles = [(b, t) for b in range(NB) for t in range(ntiles_dbg)]
            loads = {}

            def issue_load(idx):
                b, t = tiles[idx]
                xi, xs, fr = xring.next()
                cx.wait("sp", *fr)
                tk = None
                for m in range(KD):
                    tk = cx.dma("sp", d_x[xi], xs[:, m, :], src[b, m * 128:(m + 1) * 128, t * TN:(t + 1) * TN])
                loads[idx] = (xi, xs, tk)

            issue_load(0)
            hid_free = []
            hinfo = {}

            def issue_h(idx):
                b, t = tiles[idx]
                xi, xs, tl = loads[idx]
                hi, hs, fr = hring.next()
                th = make_h(hs, xs, l, 32, 24, b, tl, fr)
                hinfo[idx] = (hi, hs, th)

            issue_h(0)
            t_store = None
            for idx, (b, t) in enumerate(tiles):
                xi, xs, tl = loads[idx]
                hi, hs, th = hinfo[idx]
                if idx + 1 < len(tiles):
                    issue_load(idx + 1)
                for m_ in range(4):
                    issue_w2(m_)
                if t == 0:
                    cx.wait("act", *hid_free)
                    act.memzero(carry[:])
                cx.wait("pe", th, t_w)
                last_hid = None
                for c in range(KF):
                    ai, ab_, fra = p1.next()
                    cx.wait("pe", *fra)
                    ta = _mm_group(cx, ab_[:, 0:TN], [(w1s[:, k, c * 128:(c + 1) * 128], hs[:, k, :]) for k in range(KD)])
                    vi, vb_, frv = p1.next()
                    cx.wait("pe", *frv)
                    tv = _mm_group(cx, vb_[:, 0:TN], [(w1s[:, k, DFF + c * 128:DFF + (c + 1) * 128], hs[:, k, :]) for k in range(KD)])
                    bi, A, frb = aring.next()
                    cx.wait("act", ta, *frb)
                    act.copy(out=A[:, 0:2], in_=carry[:, c, :])
                    act.copy(out=A[:, 2:TN + 2], in_=ab_[:, 0:TN])
                    tcp = cx.sig("act", act.copy(out=carry[:, c, :], in_=ab_[:, TN - 2:TN]))
                    p1.release(ai, tcp)
                    ci, acc, frc = cring.next()
                    cx.wait("dve", tcp, *frc)
                    dve.tensor_scalar(out=acc, in0=A[:, 2:TN + 2], scalar1=vecs[:, cw + 2 * KF + c:cw + 2 * KF + c + 1], scalar2=None, op0=ALU.mult)
                    dve.scalar_tensor_tensor(out=acc, in0=A[:, 1:TN + 1], scalar=vecs[:, cw + KF + c:cw + KF + c + 1], in1=acc, op0=ALU.mult, op1=ALU.add)
                    tcv = cx.sig("dve", dve.scalar_tensor_tensor(out=acc, in0=A[:, 0:TN], scalar=vecs[:, cw + c:cw + c + 1], in1=acc, op0=ALU.mult, op1=ALU.add))
                    aring.release(bi, tcv)
                    gi, gl, frg = gring.next()
                    cx.wait("act", tcv, *frg)
                    tg = cx.sig("act", act.activation(out=gl, in_=acc, func=AF.Gelu_apprx_tanh))
                    cring.release(ci, tg)
                    cx.wait("dve", tg, tv, *hid_free)
                    thd = cx.sig("dve", dve.tensor_tensor(out=hid[:, c, :], in0=gl, in1=vb_[:, 0:TN], op=ALU.mult))
                    p1.release(vi, thd)
                    gring.release(gi, thd)
                    last_hid = thd
                hid_free = []
                hring.release(hi, ta, tv)
                if idx + 1 < len(tiles):
                    issue_h(idx + 1)
                cx.wait("pe", last_hid)
                tx = epilogue(ep, l, 1, b, xs, yring, ymm_ffn, "ln2_g", "ln2_b", tl, yrel=yrel_ffn)
                hid_free = [Tok("pe", cx.sem["pe"], cx.cnt["pe"])]
                cx.wait("sp", tx)
                ts = None
                for m in range(KD):
                    ts = cx.dma("sp", d_st, dst[b, m * 128:(m + 1) * 128, t * TN:(t + 1) * TN], xs[:, m, :])
                xring.release(xi, ts)
                t_store = ts
            ffn_w_free.clear()
            ffn_w_free.append(Tok("pe", cx.sem["pe"], cx.cnt["pe"]))
            cx.barrier(extra=[t_store])

    def conv_pass(l, j, src, dst):
        TN = 512
        NT = T // TN
        cwo = VOFF["conv_w"] + j * 3 * KD
        with ExitStack() as es2:
            wi = es2.enter_context(nc.sbuf_tensor(f"cwi{l}", [128, KD, 3 * D], BF16))
            wo = es2.enter_context(nc.sbuf_tensor(f"cwo{l}", [128, KD, D], BF16))
            xbuf = es2.enter_context(nc.sbuf_tensor(f"cx{l}", [128, 3, KD, TN], F32))
            hbuf = es2.enter_context(nc.sbuf_tensor(f"ch{l}", [128, 2, KD, TN], BF16))
            hid = es2.enter_context(nc.sbuf_tensor(f"chid{l}", [128, KD, TN], BF16))
            ubuf = es2.enter_context(nc.sbuf_tensor(f"cu{l}", [128, 2, TN], F32))
            cubuf = es2.enter_context(nc.sbuf_tensor(f"ccu{l}", [128, 2, TN + 2], F32))
            accb = es2.enter_context(nc.sbuf_tensor(f"cacc{l}", [128, 2, TN], F32))
            carry = es2.enter_context(nc.sbuf_tensor(f"ccar{l}", [128, KD, 2], F32))
            d_w = cx.dsem("cw")
            load_w(d_w, wi, conv_w_in[j], KD)
            t_w = load_w(d_w, wo, conv_w_out[j], KD)
            ep = ep_glob.set_tn(TN)
            xring = Ring([xbuf[:, i] for i in range(3)])
            hring = Ring([hbuf[:, i] for i in range(2)])
            p1 = Ring(banks[0:4])
            yring = Ring(banks[4:6])
            uring = Ring([ubuf[:, i] for i in range(2)])
            curing = Ring([cubuf[:, i] for i in range(2)])
            cring = Ring([accb[:, i] for i in range(2)])
            d_x = [cx.dsem(f"cx{i}") for i in range(3)]
            d_st = cx.dsem("cst")
            tiles = [(b, t) for b in range(NB) for t in range(ntiles_dbg or NT)]
            loads, hinfo = {}, {}

            def issue_load(idx):
                b, t = tiles[idx]
                xi, xs, fr = xring.next()
                cx.wait("sp", *fr)
                tk = None
                for m in range(KD):
                    tk = cx.dma("sp", d_x[xi], xs[:, m, :], src[b, m * 128:(m + 1) * 128, t * TN:(t + 1) * TN])
                loads[idx] = (xi, xs, tk)

            def issue_h(idx):
                b, t = tiles[idx]
                xi, xs, tl = loads[idx]
                hi, hs, fr = hring.next()
                hinfo[idx] = (hi, hs, make_h(hs, xs, l, 8, 0, b, tl, fr))

            issue_load(0)
            issue_h(0)
            hid_free = []
            t_store = None
            for idx, (b, t) in enumerate(tiles):
                xi, xs, tl = loads[idx]
                hi, hs, th = hinfo[idx]
                if idx + 1 < len(tiles):
                    issue_load(idx + 1)
                if t == 0:
                    cx.wait("act", *hid_free)
                    act.memzero(carry[:])
                cx.wait("pe", th, t_w)
                last_hid = None
                for m in range(KD):
                    def grp(off):
                        return [(wi[:, k, off + m * 128:off + (m + 1) * 128], hs[:, k, :]) for k in range(KD)]
                    ui_, ub_, fr = p1.next()
                    cx.wait("pe", *fr)
                    tu = _mm_group(cx, ub_[:, 0:TN], grp(2 * D))
                    gi_, gb_, fr = p1.next()
                    cx.wait("pe", *fr)
                    tcg = _mm_group(cx, gb_[:, 0:TN], grp(D))
                    bi_, bb_, fr = p1.next()
                    cx.wait("pe", *fr)
                    tbg = _mm_group(cx, bb_[:, 0:TN], grp(0))
                    si, us, fr = uring.next()
                    cx.wait("act", tu, *fr)
                    tus = cx.sig("act", act.copy(out=us, in_=ub_[:, 0:TN]))
                    p1.release(ui_, tus)
                    qi, CU, fr = curing.next()
                    cx.wait("dve", tus, tcg, *fr)
                    tcu = cx.sig("dve", dve.tensor_tensor(out=CU[:, 2:TN + 2], in0=us, in1=gb_[:, 0:TN], op=ALU.mult))
                    p1.release(gi_, tcu)
                    uring.release(si, tcu)
                    cx.wait("act", tcu)
                    act.copy(out=CU[:, 0:2], in_=carry[:, m, :])
                    tcar = cx.sig("act", act.copy(out=carry[:, m, :], in_=CU[:, TN:TN + 2]))
                    ci, acc, fr = cring.next()
                    cx.wait("dve", tcar, *fr)
                    dve.tensor_scalar(out=acc, in0=CU[:, 2:TN + 2], scalar1=vecs[:, cwo + 2 * KD + m:cwo + 2 * KD + m + 1], scalar2=None, op0=ALU.mult)
                    dve.scalar_tensor_tensor(out=acc, in0=CU[:, 1:TN + 1], scalar=vecs[:, cwo + KD + m:cwo + KD + m + 1], in1=acc, op0=ALU.mult, op1=ALU.add)
                    dve.scalar_tensor_tensor(out=acc, in0=CU[:, 0:TN], scalar=vecs[:, cwo + m:cwo + m + 1], in1=acc, op0=ALU.mult, op1=ALU.add)
                    cx.wait("dve", tbg, *hid_free)
                    thd = cx.sig("dve", dve.tensor_tensor(out=hid[:, m, :], in0=acc, in1=bb_[:, 0:TN], op=ALU.mult))
                    curing.release(qi, thd)
                    cring.release(ci, thd)
                    p1.release(bi_, thd)
                    last_hid = thd
                hid_free = []
                hring.release(hi, tbg)
                if idx + 1 < len(tiles):
                    issue_h(idx + 1)
                cx.wait("pe", last_hid)
                tx = epilogue(ep, l, 0, b, xs, yring,
                              lambda m: [(wo[:, k, m * 128:(m + 1) * 128], hid[:, k, :]) for k in range(KD)],
                              "ln1_g", "ln1_b", tl)
                hid_free = [Tok("pe", cx.sem["pe"], cx.cnt["pe"])]
                cx.wait("sp", tx)
                ts = None
                for m in range(KD):
                    ts = cx.dma("sp", d_st, dst[b, m * 128:(m + 1) * 128, t * TN:(t + 1) * TN], xs[:, m, :])
                xring.release(xi, ts)
                t_store = ts
            cx.barrier(extra=[t_store])

    def pool_pass(l, src, dst):
        TN = 512
        NT = T // TN
        H = 16
        with ExitStack() as es3:
            wi = es3.enter_context(nc.sbuf_tensor(f"pwi{l}", [128, KD, D], BF16))
            wg = es3.enter_context(nc.sbuf_tensor(f"pwg{l}", [128, 4, 2, 256], BF16))
            wo = es3.enter_context(nc.sbuf_tensor(f"pwo{l}", [128, KD, D], BF16))
            xbuf = es3.enter_context(nc.sbuf_tensor(f"px{l}", [128, 3, KD, TN], F32))
            hbuf = es3.enter_context(nc.sbuf_tensor(f"ph{l}", [128, 2, KD, TN], BF16))
            pooled = es3.enter_context(nc.sbuf_tensor(f"ppl{l}", [128, KD, TN], BF16))
            zs = es3.enter_context(nc.sbuf_tensor(f"pz{l}", [128, KD, TN], BF16))
            ubuf = es3.enter_context(nc.sbuf_tensor(f"pu{l}", [128, 2, TN + H], F32))
            sbuf2 = es3.enter_context(nc.sbuf_tensor(f"ps{l}", [128, 2, 2, TN + H], F32))
            carry = es3.enter_context(nc.sbuf_tensor(f"pcar{l}", [128, KD, H], F32))
            d_w = cx.dsem("pw")
            load_w(d_w, wi, pool_w_in[0], KD)
            tk = None
            for g in range(4):
                for k in range(2):
                    tk = cx.dma("pool", d_w, wg[:, g, k, :], pool_w_grp[0, g, k * 128:(k + 1) * 128, :])
            t_w = load_w(d_w, wo, pool_w_out[0], KD)
            ep = ep_glob.set_tn(TN)
            xring = Ring([xbuf[:, i] for i in range(3)])
            hring = Ring([hbuf[:, i] for i in range(2)])
            p1 = Ring(banks[0:4])
            yring = Ring(banks[4:6])
            uring = Ring([ubuf[:, i] for i in range(2)])
            sring = Ring([sbuf2[:, i] for i in range(2)])
            d_x = [cx.dsem(f"px{i}") for i in range(3)]
            d_st = cx.dsem("pst")
            tiles = [(b, t) for b in range(NB) for t in range(ntiles_dbg or NT)]
            loads, hinfo = {}, {}

            def issue_load(idx):
                b, t = tiles[idx]
                xi, xs, fr = xring.next()
                cx.wait("sp", *fr)
                tk = None
                for m in range(KD):
                    tk = cx.dma("sp", d_x[xi], xs[:, m, :], src[b, m * 128:(m + 1) * 128, t * TN:(t + 1) * TN])
                loads[idx] = (xi, xs, tk)

            def issue_h(idx):
                b, t = tiles[idx]
                xi, xs, tl = loads[idx]
                hi, hs, fr = hring.next()
                hinfo[idx] = (hi, hs, make_h(hs, xs, l, 8, 0, b, tl, fr))

            issue_load(0)
            issue_h(0)
            pooled_free, z_free = [], []
            t_store = None
            rc0 = VOFF["pool_rc"]
            for idx, (b, t) in enumerate(tiles):
                xi, xs, tl = loads[idx]
                hi, hs, th = hinfo[idx]
                if idx + 1 < len(tiles):
                    issue_load(idx + 1)
                if t == 0:
                    act.memzero(carry[:])
                cx.wait("pe", th, t_w)
                last_p = None
                for m in range(KD):
                    g = m // 2
                    w = POOL_W[g]
                    ui_, ub_, fr = p1.next()
                    cx.wait("pe", *fr)
                    tu = _mm_group(cx, ub_[:, 0:TN], [(wi[:, k, m * 128:(m + 1) * 128], hs[:, k, :]) for k in range(KD)])
                    si, U, fr = uring.next()
                    cx.wait("act", tu, *fr)
                    act.copy(out=U[:, 0:H], in_=carry[:, m, :])
                    act.copy(out=U[:, H:H + TN], in_=ub_[:, 0:TN])
                    tcar = cx.sig("act", act.copy(out=carry[:, m, :], in_=ub_[:, TN - H:TN]))
                    p1.release(ui_, tcar)
                    ri, S, fr = sring.next()
                    cx.wait("dve", tcar, *fr)
                    L = TN + H
                    cur = U
                    step = 1
                    n = 0
                    while step < w:
                        nxt = S[:, n % 2]
                        dve.tensor_tensor(out=nxt[:, step:L], in0=cur[:, step:L], in1=cur[:, 0:L - step], op=ALU.add)
                        cur = nxt
                        step *= 2
                        n += 1
                    cx.wait("dve", *pooled_free)
                    tp = cx.sig("dve", dve.scalar_tensor_tensor(out=pooled[:, m, :], in0=cur[:, H:H + TN], scalar=1.0 / w,
                                                                in1=U[:, H:H + TN], op0=ALU.mult, op1=ALU.subtract))
                    if t == 0:
                        tfx = cx.sig("dve", dve.tensor_tensor(out=cur[:, H:H + 16], in0=cur[:, H:H + 16], in1=vecs[:, rc0 + g * 16:rc0 + (g + 1) * 16], op=ALU.mult))
                        dve.wait_ge(tfx.sem, tfx.val)
                        tp = cx.sig("dve", dve.tensor_tensor(out=pooled[:, m, 0:16], in0=cur[:, H:H + 16], in1=U[:, H:H + 16], op=ALU.subtract))
                    uring.release(si, tp)
                    sring.release(ri, tp)
                    last_p = tp
                pooled_free = []
                hring.release(hi, tu)
                if idx + 1 < len(tiles):
                    issue_h(idx + 1)
                cx.wait("pe", last_p)
                last_z = None
                for m in range(KD):
                    g, e = m // 2, m % 2
                    zi_, zb_, fr = p1.next()
                    cx.wait("pe", *fr)
                    tzm = _mm_group(cx, zb_[:, 0:TN], [(wg[:, g, k, e * 128:(e + 1) * 128], pooled[:, 2 * g + k, :]) for k in range(2)])
                    cx.wait("act", tzm, *z_free)
                    tze = cx.sig("act", act.activation(out=zs[:, m, :], in_=zb_[:, 0:TN], func=AF.Identity,
                                                       scale=vecs[:, VOFF["pool_scale"] + m:VOFF["pool_scale"] + m + 1]))
                    p1.release(zi_, tze)
                    last_z = tze
                z_free = []
                pooled_free = [Tok("pe", cx.sem["pe"], cx.cnt["pe"])]
                cx.wait("pe", last_z)
                tx = epilogue(ep, l, 0, b, xs, yring,
                              lambda m: [(wo[:, k, m * 128:(m + 1) * 128], zs[:, k, :]) for k in range(KD)],
                              "ln1_g", "ln1_b", tl)
                z_free = [Tok("pe", cx.sem["pe"], cx.cnt["pe"])]
                cx.wait("sp", tx)
                ts = None
                for m in range(KD):
                    ts = cx.dma("sp", d_st, dst[b, m * 128:(m + 1) * 128, t * TN:(t + 1) * TN], xs[:, m, :])
                xring.release(xi, ts)
                t_store = ts
            cx.barrier(extra=[t_store])

    NH, HD, NG = 16, 64, 4
    NEG = -30000.0

    def dsync(ins):
        t = cx.sig("dve", ins)
        dve.wait_ge(t.sem, t.val)
        return t

    def asyncw(ins):
        t = cx.sig("act", ins)
        act.wait_ge(t.sem, t.val)
        return t

    def nsa_proj_pass(l, src):
        TN = 512
        NT = T // TN
        with ExitStack() as es4:
            wi = es4.enter_context(nc.sbuf_tensor("nwi", [128, KD, NSA_IN], BF16))
            xbuf = es4.enter_context(nc.sbuf_tensor("nx", [128, 2, KD, TN], F32))
            hbuf = es4.enter_context(nc.sbuf_tensor("nh", [128, 2, KD, TN], BF16))
            evb = es4.enter_context(nc.sbuf_tensor("nev", [128, 4, TN], BF16))
            evb2 = es4.enter_context(nc.sbuf_tensor("nev2", [128, 2, 2, 4, 65], BF16))
            gtb = es4.enter_context(nc.sbuf_tensor("ngt", [128, 2, 48], F32))
            d_w = cx.dsem("nw")
            t_w = load_w(d_w, wi, nsa_w_in[0], KD)
            xring = Ring([xbuf[:, i] for i in range(2)])
            hring = Ring([hbuf[:, i] for i in range(2)])
            ering = Ring([evb[:, i] for i in range(4)])
            e2ring = Ring([evb2[:, i] for i in range(2)])
            t_e2 = cx.sig("dve", dve.memset(evb2[:], 1.0))
            gring = Ring([gtb[:, i] for i in range(2)])
            p1 = Ring(banks[0:4])
            p2 = Ring(banks[4:6])
            p3 = Ring(banks[6:8])
            d_x = [cx.dsem(f"nx{i}") for i in range(2)]
            d_st = cx.dsem("nst")
            tiles = [(b, t) for b in range(NB) for t in range(ntiles_dbg or NT)]
            loads, hinfo = {}, {}

            def issue_load(idx):
                b, t = tiles[idx]
                xi, xs, fr = xring.next()
                cx.wait("sp", *fr)
                tk = None
                for m in range(KD):
                    tk = cx.dma("sp", d_x[xi], xs[:, m, :], src[b, m * 128:(m + 1) * 128, t * TN:(t + 1) * TN])
                loads[idx] = (xi, xs, tk)

            def issue_h(idx):
                b, t = tiles[idx]
                xi, xs, tl = loads[idx]
                hi, hs, fr = hring.next()
                th = make_h(hs, xs, l, 8, 0, b, tl, fr)
                xring.release(xi, th)
                hinfo[idx] = (hi, hs, th)

            issue_load(0)
            issue_h(0)
            t_store = None
            fm = [(m * 128, ("q", m), 0.125) for m in range(8)]
            for nm, c0 in (("kc", 1024), ("vc", 1280), ("ks", 1536), ("kw", 2048)):
                fm += [(c0, (nm, 0), 1.0), (c0 + 128, (nm, 1), 1.0)]
            dsts = {"q": q_d, "kc": kc_d, "vc": vc_d, "ks": ks_d, "kw": kw_d}
            for idx, (b, t) in enumerate(tiles):
                hi, hs, th = hinfo[idx]
                if idx + 1 < len(tiles):
                    issue_load(idx + 1)
                cx.wait("pe", th, t_w)
                tlast = None
                for n, (c0, (nm, ci), scl) in enumerate(fm):
                    pi, pb, fr = p1.next()
                    cx.wait("pe", *fr)
                    tm = _mm_group(cx, pb[:, 0:TN], [(wi[:, k, c0:c0 + 128], hs[:, k, :]) for k in range(KD)])
                    tlast = tm
                    ei, eb, fr = ering.next()
                    if n % 2 == 0:
                        cx.wait("act", tm, *fr)
                        te = cx.sig("act", act.activation(out=eb, in_=pb[:, 0:TN], func=AF.Copy, scale=scl))
                    else:
                        cx.wait("dve", tm, *fr)
                        te = cx.sig("dve", dve.tensor_scalar(out=eb, in0=pb[:, 0:TN], scalar1=scl, scalar2=None, op0=ALU.mult))
                    p1.release(pi, te)
                    cx.wait("sp", te)
                    dd = dsts[nm]
                    ts = cx.dma("sp", d_st, dd[b, 2 * ci:2 * ci + 2, :, t * TN:(t + 1) * TN].rearrange("h d t -> (h d) t"), eb)
                    ering.release(ei, ts)
                    t_store = ts
                for tb in range(TN // 128):
                    tok0 = t * TN + tb * 128
                    pi, pb, fr = p2.next()
                    cx.wait("pe", *fr)
                    lh = [hs[:, k, tb * 128:(tb + 1) * 128] for k in range(KD)]
                    _mm_group(cx, pb[:, 0:256], [(lh[k], wi[:, k, 1792:2048]) for k in range(KD)], last_sig=False)
                    tv = _mm_group(cx, pb[:, 256:512], [(lh[k], wi[:, k, 2304:2560]) for k in range(KD)])
                    gi, gb, fr2 = p3.next()
                    cx.wait("pe", *fr2)
                    tg = _mm_group(cx, gb[:, 0:48], [(lh[k], wi[:, k, 2560:2608]) for k in range(KD)])
                    tlast = tg
                    ei, eb, fr = e2ring.next()
                    cx.wait("dve", tv, *fr)
                    te = cx.sig("dve", dve.tensor_copy(out=eb[:, :, :, 0:64], in_=pb[:, 0:512].rearrange("p (s g d) -> p s g d", s=2, g=4)))
                    p2.release(pi, te)
                    cx.wait("sp", te)
                    cx.dma("sp", d_st, vs_d[b, tok0:tok0 + 128, :], eb[:, 0].rearrange("p g d -> p (g d)"))
                    ts = cx.dma("sp", d_st, vw_d[b, tok0:tok0 + 128, :], eb[:, 1].rearrange("p g d -> p (g d)"))
                    e2ring.release(ei, ts)
                    si, sgt, fr = gring.next()
                    cx.wait("act", tg, *fr)
                    tsg = cx.sig("act", act.activation(out=sgt, in_=gb[:, 0:48], func=AF.Sigmoid))
                    p3.release(gi, tsg)
                    cx.wait("sp", tsg)
                    ts = cx.dma("sp", d_st, gate_d[b, tok0:tok0 + 128, :], sgt)
                    gring.release(si, ts)
                    t_store = ts
                hring.release(hi, tlast)
                if idx + 1 < len(tiles):
                    issue_h(idx + 1)
            cx.barrier(extra=[t_store])

    def nsa_attn_pass():
        NQB = T // 128
        with ExitStack() as es5:
            ident = es5.enter_context(nc.sbuf_tensor("a_ident", [128, 128], BF16))
            negtri = es5.enter_context(nc.sbuf_tensor("a_negtri", [128, 2, 512], BF16))
            cmask = es5.enter_context(nc.sbuf_tensor("a_cmask", [128, 17, 512], BF16))
            cms = es5.enter_context(nc.sbuf_tensor("a_cms", [128, 128], F32))
            ads = es5.enter_context(nc.sbuf_tensor("a_ads", [128, 128], F32))
            ovl = es5.enter_context(nc.sbuf_tensor("a_ovl", [128, 2, 64], BF16))
            kcmpT = es5.enter_context(nc.sbuf_tensor("a_kcmp", [64, NB, NG, 256], BF16))
            vcmp = es5.enter_context(nc.sbuf_tensor("a_vcmp", [128, NB, 2, NG, 65], BF16))
            w2sb = es5.enter_context(nc.sbuf_tensor("a_w2", [128, 2, 2, 64], BF16))
            posT = es5.enter_context(nc.sbuf_tensor("a_posT", [64, 2, 32], BF16))
            LX = es5.enter_context(nc.sbuf_tensor("a_lx", [7, 32, 128], BF16))
            LXC = es5.enter_context(nc.sbuf_tensor("a_lxc", [7, 2, 128], BF16))
            ext_sb = es5.enter_context(nc.sbuf_tensor("a_ext", [7, 32, 2, 16], F32))
            EXb = es5.enter_context(nc.sbuf_tensor("a_exb", [7, 2, 2, 16 * 128], BF16))
            d_c = cx.dsem("ac")
            cx.dma("pool", d_c, ident[:], c_ident[:, :])
            cx.dma("pool", d_c, negtri[:], c_negtri[:, :, :])
            cx.dma("pool", d_c, cmask[:], c_cmask[:, :, :])
            cx.dma("pool", d_c, ovl[:], c_ovl[:, :, :])
            cx.dma("pool", d_c, posT[:, 0, :], nsa_posT[0])
            cx.dma("pool", d_c, posT[:, 1, :], nsa_posT[1])
            for kv, wsrc in enumerate((nsa_cmp_k_w2, nsa_cmp_v_w2)):
                for ec in range(2):
                    cx.dma("pool", d_c, w2sb[:, kv, ec, :], wsrc[0, ec * 128:(ec + 1) * 128, :])
            cx.dma("pool", d_c, LX[:], c_lx[:, :, :])
            cx.dma("pool", d_c, LXC[:], c_lxc[:, :, :])
            cx.dma("pool", d_c, ext_sb[:], c_ext[:, :, :, :])
            t_c1 = cx.dma("pool", d_c, cms[:], c_cms[:, :])
            t_c2 = cx.dma("sp", d_c, ads[:], c_ads[:, :])
            t_consts = t_c2
            cx.wait("dve", t_consts)
            dve.memset(kcmpT[:], 0.0)
            dve.memset(vcmp[:], 0.0)
            t_z = dsync(dve.memset(vcmp[:, :, :, :, 64:65], 1.0))

            with ExitStack() as es6:
                w1sb = es6.enter_context(nc.sbuf_tensor("c_w1", [64, 32, 256], BF16))
                csrc = es6.enter_context(nc.sbuf_tensor("c_src", [64, NG, T], BF16))
                hidT = es6.enter_context(nc.sbuf_tensor("c_hid", [128, 2, 256], BF16))
                posb = es6.enter_context(nc.sbuf_tensor("c_posb", [128, 2], F32))
                d_w1 = cx.dsem("cw1")
                d_src = cx.dsem("csrc")
                src_free = []
                w1_free = []
                t_hz = dsync(dve.memset(hidT[:], 0.0))
                hid_free = []
                for kv, (w1src, sdram) in enumerate(((nsa_cmp_k_w1, kc_d), (nsa_cmp_v_w1, vc_d))):
                    cx.wait("pool", *w1_free)
                    tw1 = None
                    for l0 in range(0, 32, 8):
                        tw1 = cx.dma("pool", d_w1, w1sb[:, l0:l0 + 8, :], w1src[0, l0:l0 + 8].rearrange("l d e -> d l e"))
                    cx.wait("pe", tw1, t_consts, *hid_free)
                    pbk = banks[7]
                    for ec in range(2):
                        tpb = _mm_group(cx, pbk[:, ec:ec + 1], [(w1sb[:, l_, ec * 128:(ec + 1) * 128], posT[:, kv, l_:l_ + 1]) for l_ in range(32)])
                    cx.wait("dve", tpb)
                    t_posb = dsync(dve.tensor_copy(out=posb[:], in_=pbk[:, 0:2]))
                    for b in range(NB):
                        cx.wait("sp", *src_free)
                        tsrc = cx.dma("sp", d_src, csrc[:], sdram[b].rearrange("g d t -> d g t"))
                        src_free = []
                        cx.wait("pe", tsrc, t_posb)
                        for g in range(NG):
                            hb = [banks[0], banks[1]]
                            thid = []
                            for ec in range(2):
                                cx.wait("pe", *hid_free)
                                th_ = _mm_group(cx, hb[ec][:, 0:255],
                                                [(w1sb[:, l_, ec * 128:(ec + 1) * 128], csrc[:, g, l_:l_ + 16 * 254 + 1:16]) for l_ in range(32)])
                                cx.wait("act", th_, t_posb, t_hz, *hid_free)
                                thid.append(cx.sig("act", act.activation(out=hidT[:, ec, 0:255], in_=hb[ec][:, 0:255], func=AF.Gelu_apprx_tanh,
                                                                         bias=posb[:, ec:ec + 1], scale=1.0)))
                            hid_free = []
                            cx.wait("pe", *thid)
                            if kv == 0:
                                ob = banks[2]
                                to = _mm_group(cx, ob[0:64, 0:255], [(w2sb[:, 0, ec, :], hidT[:, ec, 0:255]) for ec in range(2)])
                                cx.wait("dve", to, t_z)
                                te = cx.sig("dve", dve.tensor_copy(out=kcmpT[:, b, g, 0:255], in_=ob[0:64, 0:255]))
                            else:
                                ob = banks[2]
                                for nt in range(2):
                                    to = _mm_group(cx, ob[:, nt * 64:(nt + 1) * 64], [(hidT[:, ec, nt * 128:(nt + 1) * 128], w2sb[:, 1, ec, :]) for ec in range(2)])
                                cx.wait("dve", to, t_z)
                                te = cx.sig("dve", dve.tensor_copy(out=vcmp[:, b, :, g, 0:64], in_=ob[:, 0:128].rearrange("p (n d) -> p n d", n=2)))
                            hid_free = [te, Tok("pe", cx.sem["pe"], cx.cnt["pe"])]
                        src_free = [Tok("pe", cx.sem["pe"], cx.cnt["pe"])]
                    w1_free = [Tok("pe", cx.sem["pe"], cx.cnt["pe"])]
                cx.barrier()

            with ExitStack() as es7:
                KA = es7.enter_context(nc.sbuf_tensor("KA", [128, NG, T], BF16))
                KW = es7.enter_context(nc.sbuf_tensor("KW", [64, NG, T], BF16))
                VS = es7.enter_context(nc.sbuf_tensor("VS", [128, 32, NG, 65], BF16))
                VW = es7.enter_context(nc.sbuf_tensor("VW", [128, 32, NG, 65], BF16))
                QP = es7.enter_context(nc.sbuf_tensor("QP", [128, 2, NH * 128], BF16))
                GT = es7.enter_context(nc.sbuf_tensor("GT", [128, 2, 48], F32))
                PTb = es7.enter_context(nc.sbuf_tensor("PT", [128, 4, 512], BF16))
                otok = es7.enter_context(nc.sbuf_tensor("otok", [128, 2, D], F32))
                otb = es7.enter_context(nc.sbuf_tensor("otb", [128, D], BF16))
                oTs = es7.enter_context(nc.sbuf_tensor("oTs", [128, 2, KD, 128], BF16))
                OTS = es7.enter_context(nc.sbuf_tensor("OTS", [128, 2, 512], F32))
                vcx = es7.enter_context(nc.sbuf_tensor("vcx", [128, NB, 2, NG, 128], BF16))
                identf = es7.enter_context(nc.sbuf_tensor("identf", [128, 128], F32))
                with ExitStack() as es8:
                    rz = es8.enter_context(nc.sbuf_tensor("rz", [128, 3, 4], F32))
                    fac = es8.enter_context(nc.sbuf_tensor("fac", [128, 3, 4], F32))
                    imp = es8.enter_context(nc.sbuf_tensor("imp", [128, 64], F32))
                    sc = es8.enter_context(nc.sbuf_tensor("sc", [128, 64], F32))
                    sc2 = es8.enter_context(nc.sbuf_tensor("sc2", [128, 64], F32))
                    m8 = es8.enter_context(nc.sbuf_tensor("m8", [128, 2, 8], F32))
                    MB = es8.enter_context(nc.sbuf_tensor("MB", [128, 128], BF16))
                    d_kv = cx.dsem("kv")
                    d_q = [cx.dsem("q0"), cx.dsem("q1")]
                    d_o = cx.dsem("ost")
                    cx.wait("dve", t_consts)
                    t_ones = dsync(dve.memset(MB[:], 0.0))
                    tpb_bf = banks[6][:, :].bitcast(BF16)
                    otp_bf = banks[7][:, :].bitcast(BF16)
                    otp_free = []
                    fr_ = {"cmpacc": [], "winacc": [], "selacc": [], "tm": [], "mb": [], "tp": []}
                    sel_ready = {}
                    t_ind = None
                    for g in range(NG):
                        t_ind = cx.dma("pool", d_c, KA[64:128, g, :], c_ind[:, :])
                    qring = Ring([(QP[:, i], GT[:, i]) for i in range(2)])
                    exring = Ring([EXb[:, i] for i in range(2)])
                    otsring = Ring([OTS[:, i] for i in range(2)])
                    cx.wait("dve", t_consts)
                    dve.memset(imp[:], 0.0)
                    for b_ in range(NB):
                        for nt_ in range(2):
                            dve.tensor_copy(out=vcx[:, b_, nt_, :, 0:65], in_=vcmp[:, b_, nt_, :, :])
                            t_vcx = cx.sig("dve", dve.tensor_copy(out=vcx[:, b_, nt_, :, 65:128],
                                                                  in_=ovl[:, nt_, 1:64].unsqueeze(1).to_broadcast([128, 4, 63])))
                    t_idf = cx.dma("sp", d_c, identf[:], c_ident[:, :])
                    sring = Ring(banks[0:2])
                    ptring = Ring([PTb[:, i] for i in range(4)])
                    oring = Ring([otok[:, i] for i in range(2)])
                    otring = Ring([oTs[:, i] for i in range(2)])
                    kv_free = []
                    t_ostore = None
                    otb_free = []
                    for b in range(NB):
                        cx.wait("sp", *kv_free, t_ones)
                        cx.dma("sp", d_kv, KA[0:64, :, :], ks_d[b].rearrange("g d t -> d g t"))
                        cx.dma("sp", d_kv, KW[:], kw_d[b].rearrange("g d t -> d g t"))
                        for k0 in range(0, 32, 8):
                            cx.dma("sp", d_kv, VS[:, k0:k0 + 8].rearrange("p k g d -> p k (g d)"),
                                   vs_d[b, k0 * 128:(k0 + 8) * 128, :].rearrange("(k p) c -> p k c", p=128))
                            t_kv = cx.dma("sp", d_kv, VW[:, k0:k0 + 8].rearrange("p k g d -> p k (g d)"),
                                          vw_d[b, k0 * 128:(k0 + 8) * 128, :].rearrange("(k p) c -> p k c", p=128))
                        qinfo = {}

                        def issue_q(qb):
                            qi, (qp, gt), fr = qring.next()
                            cx.wait("sp", *fr)
                            cx.dma("sp", d_q[qi], qp[0:64, :].rearrange("p (h t) -> p h t", t=128), q_d[b, :, :, qb * 128:(qb + 1) * 128].rearrange("h d t -> d h t"))
                            tq = cx.dma("sp", d_q[qi], gt, gate_d[b, qb * 128:(qb + 1) * 128, :])
                            qinfo[qb] = (qi, qp, gt, tq)

                        nqb = NQB if not ntiles_dbg else ntiles_dbg * 4
                        issue_q(0)
                        for qb in range(nqb):
                            qi, qp, gt, tq = qinfo[qb]
                            if qb + 1 < nqb:
                                issue_q(qb + 1)
                            oi, ot, ofr = oring.next()
                            cx.wait("dve", *ofr)
                            q_last_readers = []

                            def qk_tile(lhsT, rhs, extra=None, ex=None):
                                si, sbk, fr = sring.next()
                                cx.wait("pe", *fr)
                                prs = [(lhsT, rhs)]
                                if extra is not None:
                                    prs.append((ident[:], extra))
                                prs.append(ex)
                                return si, sbk, _mm_group(cx, sbk[:, :], prs)

                            def exp_tile(si, sbk, ts, bias_of_r):
                                pi, pt, fr = ptring.next()
                                cx.wait("act", ts, *fr)
                                te = cx.sig("act", act.activation(out=pt[:, 0:512], in_=sbk[:, 0:512], func=AF.Exp))
                                sring.release(si, te)
                                return pi, pt, te


                            pend = []
                            st = {"tl": None}

                            def flush():
                                while pend:
                                    pend.pop(0)()

                            def issue(lhsT, rhs, extra, ex, pv_fn):
                                si, sbk, ts = qk_tile(lhsT, rhs, extra, ex)
                                pi, pt, te = exp_tile(si, sbk, ts, None)
                                flush()
                                pend.append(lambda: pv_fn(pi, pt, te))

                            exi, exq, exfr = exring.next()
                            cx.wait("pool", t_consts, *exfr)
                            tex = None
                            for v_ in range(2):
                                tex = cx.sig("pool", pool.tensor_copy(
                                    out=exq[:, v_, :].rearrange("p (h t) -> p h t", t=128),
                                    in_=ext_sb[:, qb, v_, :].unsqueeze(2).to_broadcast([7, 16, 128])))
                            cx.wait("pe", tq, t_kv, t_ind, tex, t_vcx, t_idf)


                            def evac_T(g, tpv, accb, M, kind):
                                oi_, ots, fr = otsring.next()
                                cx.wait("dve", tpv, *fr)
                                tcp_ = cx.sig("dve", dve.tensor_copy(out=ots[0:M, :], in_=accb[0:M, 0:512]))
                                fr_[kind] = [tcp_]
                                cx.wait("pe", tcp_, *fr_["tm"])
                                ttr = None
                                for r in range(4):
                                    ttr = cx.sig("pe", pe.transpose(banks[5][:, r * M:(r + 1) * M], ots[0:M, r * 128:(r + 1) * 128], identf[0:M, 0:M]))
                                otsring.release(oi_, ttr)
                                cx.wait("dve", ttr, tq)
                                return banks[5][:, 0:4 * M].rearrange("p (r c) -> p r c", c=M)

                            def evac_cmp(g, tpv):
                                tv_ = evac_T(g, tpv, banks[2], 128, "cmpacc")
                                dsync(dve.tensor_scalar(out=rz[:, 0, :], in0=tv_[:, :, 64], scalar1=1e-30, scalar2=None, op0=ALU.max))
                                dsync(dve.reciprocal(out=rz[:, 0, :], in_=rz[:, 0, :]))
                                gv = gt[:, 12 * g:12 * g + 12].rearrange("p (r k) -> p r k", k=3)
                                dsync(dve.tensor_tensor(out=fac[:, 0, :], in0=rz[:, 0, :], in1=gv[:, :, 0], op=ALU.mult))
                                dsync(dve.tensor_scalar(out=imp[:, 1:64], in0=tv_[:, 0, 65:128], scalar1=rz[:, 0, 0:1], scalar2=None, op0=ALU.mult))
                                for r in range(1, 4):
                                    dsync(dve.scalar_tensor_tensor(out=imp[:, 1:64], in0=tv_[:, r, 65:128], scalar=rz[:, 0, r:r + 1], in1=imp[:, 1:64],
                                                                   op0=ALU.mult, op1=ALU.add))
                                tlast = None
                                for r in range(4):
                                    tlast = cx.sig("dve", dve.tensor_scalar(out=ot[:, (4 * g + r) * 64:(4 * g + r + 1) * 64], in0=tv_[:, r, 0:64],
                                                                            scalar1=fac[:, 0, r:r + 1], scalar2=None, op0=ALU.mult))
                                fr_["tm"] = [tlast]
                                j0 = 64 - 2 * qb
                                dsync(dve.tensor_tensor(out=sc[:], in0=imp[:], in1=cms[:, j0:j0 + 64], op=ALU.mult))
                                dsync(dve.tensor_tensor(out=sc[:], in0=sc[:], in1=ads[:, j0:j0 + 64], op=ALU.add))
                                dsync(dve.memset(sc[:, 0:1], 1e4))
                                dsync(dve.max(out=m8[:, 0, :], in_=sc[:]))
                                dsync(dve.match_replace(out=sc2[:], in_to_replace=m8[:, 0, :], in_values=sc[:], imm_value=-3e4))
                                dsync(dve.max(out=m8[:, 1, :], in_=sc2[:]))
                                cx.wait("dve", *fr_["mb"])
                                tmb = dsync(dve.tensor_scalar(out=MB[:, 64:128], in0=sc[:], scalar1=m8[:, 1, 7:8], scalar2=NEG, op0=ALU.is_lt, op1=ALU.mult))
                                cx.wait("pe", tmb, *fr_["tp"])
                                ttp = cx.sig("pe", pe.transpose(tpb_bf[:, 0:128], MB[:], ident[:]))
                                fr_["mb"] = [ttp]
                                cx.wait("act", ttp, tq)
                                tcp = None
                                for r in range(4):
                                    tcp = cx.sig("act", act.copy(out=qp[64:128, (4 * g + r) * 128:(4 * g + r + 1) * 128], in_=tpb_bf[64:128, 0:128]))
                                fr_["tp"] = [tcp]
                                sel_ready[g] = tcp

                            def evac_acc(g, tpv, bank, slot, gcol, key):
                                ov = evac_T(g, tpv, bank, 65, key)
                                dsync(dve.reciprocal(out=rz[:, slot, :], in_=ov[:, :, 64]))
                                gv = gt[:, 12 * g:12 * g + 12].rearrange("p (r k) -> p r k", k=3)
                                dsync(dve.tensor_tensor(out=fac[:, slot, :], in0=rz[:, slot, :], in1=gv[:, :, gcol], op=ALU.mult))
                                tl_ = None
                                for r in range(4):
                                    osl = ot[:, (4 * g + r) * 64:(4 * g + r + 1) * 64]
                                    tl_ = dsync(dve.scalar_tensor_tensor(out=osl, in0=ov[:, r, 0:64], scalar=fac[:, slot, r:r + 1], in1=osl,
                                                                         op0=ALU.mult, op1=ALU.add))
                                fr_["tm"] = [tl_]
                                st["tl"] = tl_

                            for g in range(NG):
                                qg = qp[0:64, 4 * g * 128:(4 * g + 4) * 128]
                                nts = [nt for nt in range(2) if 128 * nt < 8 * qb + 7]
                                for nt in nts:
                                    full = (128 * nt + 127 <= 8 * qb - 2)
                                    extra = None if full else cmask[:, (8 * qb - 128 * nt) // 8, :]

                                    def pv_c(pi, pt, te, g=g, nt=nt, nts=nts):
                                        cx.wait("pe", te, *(fr_["cmpacc"] if nt == nts[0] else []))
                                        tpv = cx.sig("pe", pe.matmul(banks[2][:, 0:512], vcx[:, b, nt, g, :], pt[:, 0:512], start=(nt == nts[0]), stop=(nt == nts[-1])))
                                        ptring.release(pi, tpv)
                                        if nt == nts[-1]:
                                            q_last_readers.append(tpv)
                                            evac_cmp(g, tpv)

                                    issue(kcmpT[:, b, g, nt * 128:(nt + 1) * 128], qg, extra,
                                          (LXC[:, nt, :], exq[:, 1, 4 * g * 128:(4 * g + 4) * 128]), pv_c)
                            for g in range(NG):
                                qg = qp[0:64, 4 * g * 128:(4 * g + 4) * 128]
                                qg2 = qp[:, 4 * g * 128:(4 * g + 4) * 128]
                                kts = [kt for kt in range(qb - 4, qb + 1) if kt >= 0]
                                for kt in kts:
                                    extra = None
                                    if kt == qb:
                                        extra = negtri[:, 0, :]
                                    elif kt == qb - 4:
                                        extra = negtri[:, 1, :]

                                    def pv_w(pi, pt, te, g=g, kt=kt, kts=kts):
                                        cx.wait("pe", te, *(fr_["winacc"] if kt == kts[0] else []))
                                        tpv = cx.sig("pe", pe.matmul(banks[3][0:65, 0:512], VW[:, kt, g, :], pt[:, 0:512], start=(kt == kts[0]), stop=(kt == kts[-1])))
                                        ptring.release(pi, tpv)
                                        if kt == kts[-1]:
                                            q_last_readers.append(tpv)
                                            evac_acc(g, tpv, banks[3], 1, 2, "winacc")

                                    issue(KW[:, g, kt * 128:(kt + 1) * 128], qg, extra,
                                          (LX[:, kt, :], exq[:, 0, 4 * g * 128:(4 * g + 4) * 128]), pv_w)
                                flush()
                                cx.wait("pe", sel_ready[g])
                                for kt in range(qb + 1):
                                    extra = negtri[:, 0, :] if kt == qb else None

                                    def pv_s(pi, pt, te, g=g, kt=kt):
                                        cx.wait("pe", te, *(fr_["selacc"] if kt == 0 else []))
                                        tpv = cx.sig("pe", pe.matmul(banks[4][0:65, 0:512], VS[:, kt, g, :], pt[:, 0:512], start=(kt == 0), stop=(kt == qb)))
                                        ptring.release(pi, tpv)
                                        if kt == qb:
                                            q_last_readers.append(tpv)
                                            evac_acc(g, tpv, banks[4], 2, 1, "selacc")

                                    issue(KA[:, g, kt * 128:(kt + 1) * 128], qg2, extra,
                                          (LX[:, kt, :], exq[:, 0, 4 * g * 128:(4 * g + 4) * 128]), pv_s)
                            flush()
                            tl_ = st["tl"]
                            qring.release(qi, *q_last_readers, tl_)
                            exring.release(exi, *q_last_readers)

                            cx.wait("act", tl_, *otb_free)
                            tob = cx.sig("act", act.copy(out=otb[:], in_=ot[:]))
                            oring.release(oi, tob)
                            ti_, oT, fr = otring.next()
                            cx.wait("pe", tob, *otp_free)
                            ttp = None
                            for m in range(KD):
                                ttp = cx.sig("pe", pe.transpose(otp_bf[:, m * 128:(m + 1) * 128], otb[:, m * 128:(m + 1) * 128], ident[:]))
                            otb_free = [ttp]
                            cx.wait("dve", ttp, *fr)
                            tcp = dsync(dve.tensor_copy(out=oT.rearrange("p m t -> p (m t)"), in_=otp_bf[:, 0:KD * 128]))
                            otp_free = [tcp]
                            cx.wait("sp", tcp)
                            tst = None
                            for m in range(KD):
                                tst = cx.dma("sp", d_o, o_d[b, m * 128:(m + 1) * 128, qb * 128:(qb + 1) * 128], oT[:, m, :])
                            otring.release(ti_, tst)
                            t_ostore = tst
                        kv_free = [Tok("pe", cx.sem["pe"], cx.cnt["pe"])]
                    cx.barrier(extra=[t_ostore])

    def proj_pass(l, src, hid_src, w_src, dst):
        TN = 512
        NT = T // TN
        with ExitStack() as es9:
            wo = es9.enter_context(nc.sbuf_tensor("jwo", [128, KD, D], BF16))
            xbuf = es9.enter_context(nc.sbuf_tensor("jx", [128, 3, KD, TN], F32))
            hbuf = es9.enter_context(nc.sbuf_tensor("jh", [128, 2, KD, TN], BF16))
            d_w = cx.dsem("jw")
            t_w = load_w(d_w, wo, w_src, KD)
            ep = ep_glob.set_tn(TN)
            xring = Ring([xbuf[:, i] for i in range(3)])
            hring = Ring([hbuf[:, i] for i in range(2)])
            yring = Ring(banks[4:6])
            d_x = [cx.dsem(f"jx{i}") for i in range(3)]
            d_h = [cx.dsem(f"jh{i}") for i in range(2)]
            d_st = cx.dsem("jst")
            tiles = [(b, t) for b in range(NB) for t in range(ntiles_dbg or NT)]
            loads = {}

            def issue_load(idx):
                b, t = tiles[idx]
                xi, xs, fr = xring.next()
                hi, hs, frh = hring.next()
                cx.wait("sp", *fr, *frh)
                tk = None
                for m in range(KD):
                    tk = cx.dma("sp", d_x[xi], xs[:, m, :], src[b, m * 128:(m + 1) * 128, t * TN:(t + 1) * TN])
                th = None
                for m in range(KD):
                    th = cx.dma("sp", d_h[hi], hs[:, m, :], hid_src[b, m * 128:(m + 1) * 128, t * TN:(t + 1) * TN])
                loads[idx] = (xi, xs, tk, hi, hs, th)

            issue_load(0)
            t_store = None
            for idx, (b, t) in enumerate(tiles):
                xi, xs, tl, hi, hs, th = loads[idx]
                if idx + 1 < len(tiles):
                    issue_load(idx + 1)
                cx.wait("pe", th, t_w)
                tx = epilogue(ep, l, 0, b, xs, yring,
                              lambda m: [(wo[:, k, m * 128:(m + 1) * 128], hs[:, k, :]) for k in range(KD)],
                              "ln1_g", "ln1_b", tl)
                hring.release(hi, Tok("pe", cx.sem["pe"], cx.cnt["pe"]))
                cx.wait("sp", tx)
                ts = None
                for m in range(KD):
                    ts = cx.dma("sp", d_st, dst[b, m * 128:(m + 1) * 128, t * TN:(t + 1) * TN], xs[:, m, :])
                xring.release(xi, ts)
                t_store = ts
            cx.barrier(extra=[t_store])


    cur = xT
    bufs = [actA, actB]
    nb = 0
    for l in layers:
        mixer = l % 3
        d1 = bufs[nb]
        nb ^= 1
        if mixer == 0:
            with nc.named_scope(f"conv{l}"):
                conv_pass(l, l // 3, cur, d1)
        elif mixer == 1:
            with nc.named_scope(f"pool{l}"):
                pool_pass(l, cur, d1)
        else:
            with nc.named_scope("nsa_proj"):
                nsa_proj_pass(l, cur)
            with nc.named_scope("nsa_attn"):
                nsa_attn_pass()
            with nc.named_scope("nsa_out"):
                proj_pass(l, cur, o_d, nsa_w_out[0], d1)
        d2 = outT if l == layers[-1] else bufs[nb]
        if d2 is not outT:
            nb ^= 1
        with nc.named_scope(f"ffn{l}"):
            ffn_pass(l, d1, d2)
        cur = d2
    return nc


_NSA_C = None


def _nsa_consts():
    global _NSA_C
    if _NSA_C is not None:
        return _NSA_C
    NEG = -30000.0
    c = {}
    c["c_ident"] = np.eye(128, dtype=np.float32)
    a = np.arange(128)[:, None]
    bq = np.arange(128)[None, :]
    nt0 = np.where(a > bq, NEG, 0.0).astype(np.float32)
    nt1 = np.where(a <= bq, NEG, 0.0).astype(np.float32)
    c["c_negtri"] = np.ascontiguousarray(np.stack([np.tile(nt0, (1, 4)), np.tile(nt1, (1, 4))], axis=1))
    cm = np.zeros((128, 17, 128), np.float32)
    for oi in range(17):
        cm[:, oi, :] = np.where(16 * (a - 8 * oi) + 31 > bq, NEG, 0.0)
    c["c_cmask"] = np.ascontiguousarray(np.tile(cm, (1, 1, 4)))
    n = np.arange(256)
    cs, ce = n * 16, n * 16 + 31
    ss = np.arange(64) * 64
    ov = ((cs[:, None] <= ss[None, :] + 63) & (ce[:, None] >= ss[None, :])).astype(np.float32)
    ov[255] = 0.0
    c["c_ovl"] = np.ascontiguousarray(ov.reshape(2, 128, 64).transpose(1, 0, 2))
    btr = (np.arange(128) >= 64).astype(np.int64)[:, None]
    jr = np.arange(128)[None, :] - 64
    c["c_cms"] = (jr <= btr - 2).astype(np.float32)
    ad = np.zeros((128, 128), np.float32)
    ad[jr > btr] = -1e4
    ad[(jr == btr) | (jr == btr - 1)] = 1e4
    c["c_ads"] = ad
    slopes = np.exp2(-np.arange(1, 17, dtype=np.float64) / 2.0)
    dl = np.arange(32) - 31
    bt = slopes[None, None, :] * (128.0 * dl[None, :, None] + np.arange(128)[:, None, None] - 64.0)
    c["c_btab"] = bt.astype(np.float32)
    nl = np.arange(128)[:, None, None, None]
    ntt = np.arange(2)[None, :, None, None]
    qb = np.arange(32)[None, None, :, None]
    bc = slopes[None, None, None, :] * (16.0 * (128 * ntt + nl) + 15.5 - 128.0 * qb - 64.0)
    c["c_bC"] = bc.astype(np.float32)
    import ml_dtypes
    def bf(x):
        return np.asarray(x, np.float32).astype(ml_dtypes.bfloat16).astype(np.float64)
    s_hi = bf(slopes)
    s_lo = bf(slopes - s_hi)
    lx = np.zeros((7, 32, 128), np.float32)
    aa = np.arange(128, dtype=np.float32)
    lx[0] = lx[1] = (aa - 64.0)[None, :]
    lx[2] = lx[3] = (128.0 * np.arange(32, dtype=np.float32))[:, None]
    lx[4:7] = 1.0
    c["c_lx"] = lx
    lxc = np.zeros((7, 2, 128), np.float32)
    lxc[0] = lxc[1] = (16.0 * (aa - 64.0))[None, :]
    lxc[2] = lxc[3] = (2048.0 * np.arange(2, dtype=np.float32))[:, None]
    lxc[4:7] = 1.0
    c["c_lxc"] = lxc
    ext = np.zeros((7, 32, 2, 16), np.float64)
    qbs = np.arange(32, dtype=np.float64)[:, None]
    for v, base in enumerate((0.0, 975.5)):
        cc = slopes[None, :] * (base - 128.0 * qbs)
        c1 = bf(cc)
        c2 = bf(cc - c1)
        c3 = bf(cc - c1 - c2)
        ext[0, :, v, :] = s_hi[None, :]
        ext[1, :, v, :] = s_lo[None, :]
        ext[2, :, v, :] = s_hi[None, :]
        ext[3, :, v, :] = s_lo[None, :]
        ext[4, :, v, :] = c1
        ext[5, :, v, :] = c2
        ext[6, :, v, :] = c3
    c["c_ext"] = ext.astype(np.float32)
    ind = (np.arange(T)[None, :] // 64 == np.arange(64)[:, None]).astype(np.float32)
    c["c_ind"] = np.ascontiguousarray(ind)
    _NSA_C = c
    return c


def host_prep(inputs, core):
    b0 = core * NB
    f = {k: np.asarray(v) for k, v in inputs.items()}
    m = {}
    m["xT"] = np.ascontiguousarray(np.transpose(f["x"][b0:b0 + NB], (0, 2, 1)))
    m["condT"] = np.ascontiguousarray(f["c"][b0:b0 + NB].reshape(NB, KD, 128).transpose(2, 1, 0))
    vecs = np.zeros((128, NV), np.float32)

    def put(name, arr):
        a = _pm(arr.reshape(-1))
        vecs[:, VOFF[name]:VOFF[name] + a.shape[1]] = a

    put("ada_b", f["ada_b"])
    put("ln1_g", f["ln1_g"])
    put("ln1_b", f["ln1_b"])
    put("ln2_g", f["ln2_g"])
    put("ln2_b", f["ln2_b"])
    put("ffn_conv", f["ffn_conv"])
    put("conv_w", f["conv_w"])
    put("pool_scale", f["pool_scale"])
    rc = np.ones((4, 16), np.float32)
    for g, w in enumerate(POOL_W):
        cnt = np.minimum(np.arange(1, 17), w).astype(np.float32)
        rc[g] = (np.float32(1.0) / cnt) * np.float32(w)
    for g, w in enumerate(POOL_W):
        cnt = np.minimum(np.arange(1, 17), w).astype(np.float32)
        rc[g] = np.float32(1.0) / cnt
    vecs[:, VOFF["pool_rc"]:VOFF["pool_rc"] + 64] = rc.reshape(1, 64)
    m["vecs"] = vecs
    m.update(_nsa_consts())
    m["nsa_posT"] = np.ascontiguousarray(np.stack([f["nsa_cmp_pos_k"][0].T, f["nsa_cmp_pos_v"][0].T]).astype(np.float32))
    for k in ("nsa_w_in", "nsa_cmp_k_w1", "nsa_cmp_k_w2", "nsa_cmp_v_w1", "nsa_cmp_v_w2", "nsa_w_out"):
        m[k] = np.ascontiguousarray(f[k], dtype=np.float32)
    for k in ("ada_w", "ffn_w_in", "ffn_w_out", "conv_w_in", "conv_w_out", "pool_w_in", "pool_w_grp", "pool_w_out"):
        m[k] = np.ascontiguousarray(f[k], dtype=np.float32)
    return m


def kernel(**inputs):
    nc = build_program()
    in_maps = [host_prep(inputs, c) for c in range(8)]
    res = run_bass_kernel_spmd(nc, in_maps, core_ids=list(range(8)))
    outs = [np.transpose(r["outT"], (0, 2, 1)) for r in res.results]
    return np.ascontiguousarray(np.concatenate(outs, axis=0), dtype=np.float32)
```

```python
import math
from contextlib import ExitStack
import numpy as np
import concourse.bass as bass
import concourse.mybir as mybir
from concourse.bass_utils import run_bass_kernel_spmd

F32 = mybir.dt.float32
BF16 = mybir.dt.bfloat16
AF = mybir.ActivationFunctionType
ALU = mybir.AluOpType

D = 1024
T = 4096
NB = 2
DEPTH = 4
DFF = 2816
KD = D // 128
KF = DFF // 128
ALPHA = (2.0 * DEPTH) ** 0.25
LN_EPS = 1e-5
EPS_P = LN_EPS / (ALPHA * ALPHA)
NSA_IN = 2608
POOL_W = (2, 4, 8, 16)

VOFF = {}
_nv = 0


def _valloc(name, n):
    global _nv
    VOFF[name] = _nv
    _nv += n


_valloc("ada_b", DEPTH * 48)
_valloc("ln1_g", DEPTH * 8)
_valloc("ln1_b", DEPTH * 8)
_valloc("ln2_g", DEPTH * 8)
_valloc("ln2_b", DEPTH * 8)
_valloc("ffn_conv", DEPTH * 3 * KF)
_valloc("conv_w", 2 * 3 * KD)
_valloc("pool_scale", KD)
_valloc("pool_rc", 4 * 16)
NV = _nv


def _pm(v):
    v = np.asarray(v, np.float32)
    return np.ascontiguousarray(v.reshape(-1, 128).T)


class Tok:
    __slots__ = ("key", "sem", "val")

    def __init__(self, key, sem, val):
        self.key, self.sem, self.val = key, sem, val


class Ctx:
    def __init__(self, nc):
        self.nc = nc
        self.eng = {"pe": nc.tensor, "act": nc.scalar, "dve": nc.vector, "pool": nc.gpsimd, "sp": nc.sync}
        self.sem = {e: nc.alloc_semaphore(name=f"s_{e}") for e in self.eng}
        self.cnt = {e: 0 for e in self.eng}
        self.seen = {e: {} for e in self.eng}
        self.ndma = 0

    def sig(self, e, ins):
        ins.then_inc(self.sem[e], 1)
        self.cnt[e] += 1
        return Tok(e, self.sem[e], self.cnt[e])

    def wait(self, e, *toks):
        for t in toks:
            if t is None:
                continue
            if isinstance(t, (list, tuple)):
                self.wait(e, *t)
                continue
            if t.key == e:
                continue
            if self.seen[e].get(t.key, 0) >= t.val:
                continue
            self.eng[e].wait_ge(t.sem, t.val)
            self.seen[e][t.key] = t.val

    def dsem(self, name):
        self.ndma += 1
        return [f"d{self.ndma}_{name}", self.nc.alloc_semaphore(name=f"d{self.ndma}_{name}"), 0]

    def dma(self, q, ds, out, in_):
        ins = self.eng[q].dma_start(out=out, in_=in_)
        ins.then_inc(ds[1], 16)
        ds[2] += 16
        return Tok(ds[0], ds[1], ds[2])

    def barrier(self, extra=()):
        toks = []
        for e in self.eng:
            self.wait(e, *extra)
        for e in ("pe", "act", "dve", "pool", "sp"):
            toks.append(self.sig(e, self.eng[e].drain()))
        for e in self.eng:
            self.wait(e, *toks)


class Ring:
    def __init__(self, items):
        self.items = list(items)
        self.free = [[] for _ in self.items]
        self.i = -1

    def next(self):
        self.i = (self.i + 1) % len(self.items)
        fr = self.free[self.i]
        self.free[self.i] = []
        return self.i, self.items[self.i], fr

    def release(self, idx, *toks):
        self.free[idx].extend(t for t in toks if t is not None)


def _mm_group(cx, out, pairs, last_sig=True):
    n = len(pairs)
    tok = None
    for i, (l, r) in enumerate(pairs):
        ins = cx.nc.tensor.matmul(out, l, r, start=(i == 0), stop=(i == n - 1))
        if i == n - 1 and last_sig:
            tok = cx.sig("pe", ins)
    return tok


def build_program(layers=(0, 1, 2, 3), ntiles_dbg=None, dbg=False):
    nc = bass.Bass("TRN2", target_bir_lowering=False)
    cx = Ctx(nc)
    pe, act, dve, pool, sp = nc.tensor, nc.scalar, nc.vector, nc.gpsimd, nc.sync

    def din(name, shape, dt=F32):
        return nc.dram_tensor(name, list(shape), dt, kind="ExternalInput").ap()

    xT = din("xT", [NB, D, T])
    condT = din("condT", [128, KD, NB])
    vecs_d = din("vecs", [128, NV])
    ada_w = din("ada_w", [DEPTH, D, 6 * D])
    ffn_w_in = din("ffn_w_in", [DEPTH, D, 2 * DFF])
    ffn_w_out = din("ffn_w_out", [DEPTH, DFF, D])
    conv_w_in = din("conv_w_in", [2, D, 3 * D])
    conv_w_out = din("conv_w_out", [2, D, D])
    pool_w_in = din("pool_w_in", [1, D, D])
    pool_w_grp = din("pool_w_grp", [1, 4, 256, 256])
    pool_w_out = din("pool_w_out", [1, D, D])
    nsa_w_in = din("nsa_w_in", [1, D, NSA_IN])
    nsa_posT = din("nsa_posT", [2, 64, 32])
    nsa_cmp_k_w1 = din("nsa_cmp_k_w1", [1, 32, 64, 256])
    nsa_cmp_k_w2 = din("nsa_cmp_k_w2", [1, 256, 64])
    nsa_cmp_v_w1 = din("nsa_cmp_v_w1", [1, 32, 64, 256])
    nsa_cmp_v_w2 = din("nsa_cmp_v_w2", [1, 256, 64])
    nsa_w_out = din("nsa_w_out", [1, D, D])
    c_ident = din("c_ident", [128, 128])
    c_negtri = din("c_negtri", [128, 2, 512])
    c_cmask = din("c_cmask", [128, 17, 512])
    c_ovl = din("c_ovl", [128, 2, 64])
    c_cms = din("c_cms", [128, 128])
    c_ads = din("c_ads", [128, 128])
    c_btab = din("c_btab", [128, 32, 16])
    c_bC = din("c_bC", [128, 2, 32, 16])
    c_ind = din("c_ind", [64, T])

    def dscr(name, shape, dt):
        return nc.dram_tensor(name, list(shape), dt, kind=("ExternalOutput" if dbg else "Internal")).ap()

    q_d = dscr("q_d", [NB, 16, 64, T], BF16)
    kc_d = dscr("kc_d", [NB, 4, 64, T], BF16)
    vc_d = dscr("vc_d", [NB, 4, 64, T], BF16)
    ks_d = dscr("ks_d", [NB, 4, 64, T], BF16)
    kw_d = dscr("kw_d", [NB, 4, 64, T], BF16)
    vs_d = dscr("vs_d", [NB, T, 260], BF16)
    vw_d = dscr("vw_d", [NB, T, 260], BF16)
    gate_d = dscr("gate_d", [NB, T, 48], F32)
    o_d = dscr("o_d", [NB, D, T], BF16)
    w2bf_d = nc.dram_tensor("w2bf_d", [KD, 128, KF, 128], BF16, kind="Internal").ap()
    outT = nc.dram_tensor("outT", [NB, D, T], F32, kind="ExternalOutput").ap()
    actA = nc.dram_tensor("actA", [NB, D, T], F32, kind=("ExternalOutput" if dbg else "Internal")).ap()
    actB = nc.dram_tensor("actB", [NB, D, T], F32, kind=("ExternalOutput" if dbg else "Internal")).ap()

    def sb(name, shape, dt=F32):
        return nc.sbuf_tensor(name, list(shape), dt).__enter__()

    def ps(name, shape=(128, 512), dt=F32):
        return nc.psum_tensor(name, list(shape), dt).__enter__()

    vecs = sb("vecs_sb", [128, NV])
    modT = sb("modT", [128, DEPTH, 48, NB])
    gsc = sb("gsc", [128, DEPTH, 2, KD, NB])
    ones_bf = sb("ones_bf", [128, 128], BF16)
    eps_col = sb("eps_col", [128, 1])
    banks = [ps(f"bank{i}") for i in range(8)]

    d_const = cx.dsem("const")
    t_vecs = cx.dma("sp", d_const, vecs[:], vecs_d[:, :])
    cond = sb("cond", [128, KD, NB])
    t_cond = cx.dma("sp", d_const, cond[:], condT[:, :, :])
    t_const = t_cond

    cx.wait("dve", t_const)
    dve.memset(ones_bf[:], 1.0 / 1024.0)
    t_c1 = cx.sig("dve", dve.memset(eps_col[:], EPS_P))

    cx.wait("act", t_const)
    t_silu = cx.sig("act", act.activation(out=cond[:], in_=cond[:], func=AF.Silu))
    adw_cm = [nc.sbuf_tensor(f"adw{i}", [128, KD, 768], F32) for i in range(2)]
    adw = [c_.__enter__() for c_ in adw_cm]
    adw_ring = Ring(adw)
    d_adw = [cx.dsem("adw0"), cx.dsem("adw1")]
    mod_ps = banks[0]
    tok_mod_evac = None
    for l in range(DEPTH):
        for piece in range(8):
            i, wt, fr = adw_ring.next()
            cx.wait("sp", *fr)
            tl = None
            for k in range(KD):
                tl = cx.dma("sp", d_adw[i], wt[:, k, :], ada_w[l, k * 128:(k + 1) * 128, piece * 768:(piece + 1) * 768])
            cx.wait("pe", tl, t_silu, tok_mod_evac)
            tk = None
            for j in range(6):
                f = piece * 6 + j
                tk = _mm_group(cx, mod_ps[:, f * NB:(f + 1) * NB],
                               [(wt[:, k, j * 128:(j + 1) * 128], cond[:, k, :]) for k in range(KD)],
                               last_sig=(j == 5))
            adw_ring.release(i, tk)
        cx.wait("dve", tk, t_const)
        tok_mod_evac = cx.sig("dve", dve.tensor_tensor(
            out=modT[:, l, :, :],
            in0=mod_ps[:, 0:48 * NB].rearrange("p (f b) -> p f b", b=NB),
            in1=vecs[:, VOFF["ada_b"] + l * 48: VOFF["ada_b"] + (l + 1) * 48].unsqueeze(2).to_broadcast([128, 48, NB]),
            op=ALU.add))
    for l in range(DEPTH):
        dve.tensor_scalar(out=modT[:, l, 8:16, :], in0=modT[:, l, 8:16, :], scalar1=1.0, scalar2=None, op0=ALU.add)
        dve.tensor_scalar(out=modT[:, l, 32:40, :], in0=modT[:, l, 32:40, :], scalar1=1.0, scalar2=None, op0=ALU.add)
        dve.tensor_scalar(out=gsc[:, l, 0, :, :], in0=modT[:, l, 16:24, :], scalar1=1.0, scalar2=1.0 / ALPHA,
                          op0=ALU.add, op1=ALU.mult)
        tok_mod = cx.sig("dve", dve.tensor_scalar(out=gsc[:, l, 1, :, :], in0=modT[:, l, 40:48, :], scalar1=1.0,
                                                   scalar2=1.0 / ALPHA, op0=ALU.add, op1=ALU.mult))
    cx.barrier()
    for c_ in reversed(adw_cm):
        c_.__exit__(None, None, None)

    def vcol(name, idx):
        o = VOFF[name] + idx
        return vecs[:, o:o + 1]

    class Epi:
        def __init__(self):
            self.TN = 512
            self.zb_t = [sb(f"zb{i}", [128, 512], BF16) for i in range(2)]
            self.zq_t = [sb(f"zq{i}", [128, 512], BF16) for i in range(2)]
            self.m2_t = sb("m2", [128, 512])
            self.rstd_t = sb("rstd", [128, 512])
            self.nmr_t = sb("nmr", [128, 512])
            self.stat_free = []
            self.set_tn(512)

        def set_tn(self, TN):
            self.TN = TN
            fz = getattr(self, "zb", None)
            self.zb = Ring([t[:, 0:TN] for t in self.zb_t])
            self.zq = Ring([t[:, 0:TN] for t in self.zq_t])
            if fz is not None:
                pass
            self.m2 = self.m2_t[:, 0:TN]
            self.rstd = self.rstd_t[:, 0:TN]
            self.nmr = self.nmr_t[:, 0:TN]
            return self

    ep_glob = Epi()

    def epilogue(ep, l, which, b, xs, y_ring, ymm, lng, lnb, x_ready, yrel=None):
        TN = ep.TN
        s1, s2 = banks[6], banks[7]
        cx.wait("pe", *ep.stat_free)
        ep_free_tmp = ep.stat_free
        ep.stat_free = []
        pend = None
        z_toks = []
        for m in range(KD):
            yi, ybank, fr = y_ring.next()
            cx.wait("pe", *fr)
            ty = _mm_group(cx, ybank[:, 0:TN], ymm(m))
            if yrel is not None:
                yrel(m, ty)
            if pend is not None:
                pm, tzb, tzq, zi, qi, zbt, zqt = pend
                cx.wait("pe", tzb, tzq)
                pe.matmul(s1[:, 0:TN], ones_bf[:], zbt, start=(pm == 0), stop=(pm == KD - 1))
                tst = cx.sig("pe", pe.matmul(s2[:, 0:TN], ones_bf[:], zqt, start=(pm == 0), stop=(pm == KD - 1)))
                ep.zb.release(zi, tst)
                ep.zq.release(qi, tst)
            cx.wait("dve", ty, x_ready, *ep_free_tmp)
            tz = cx.sig("dve", dve.scalar_tensor_tensor(out=xs[:, m, :], in0=ybank[:, 0:TN], scalar=gsc[:, l, which, m, b:b + 1],
                                                        in1=xs[:, m, :], op0=ALU.mult, op1=ALU.add))
            y_ring.release(yi, tz)
            z_toks.append(tz)
            zi, zbt, fr1 = ep.zb.next()
            qi, zqt, fr2 = ep.zq.next()
            cx.wait("act", tz, *fr1, *fr2)
            tzb = cx.sig("act", act.activation(out=zbt, in_=xs[:, m, :], func=AF.Identity))
            tzq = cx.sig("act", act.activation(out=zqt, in_=xs[:, m, :], func=AF.Square))
            pend = (m, tzb, tzq, zi, qi, zbt, zqt)
        pm, tzb, tzq, zi, qi, zbt, zqt = pend
        cx.wait("pe", tzb, tzq)
        pe.matmul(s1[:, 0:TN], ones_bf[:], zbt, start=False, stop=True)
        tst = cx.sig("pe", pe.matmul(s2[:, 0:TN], ones_bf[:], zqt, start=False, stop=True))
        ep.zb.release(zi, tst)
        ep.zq.release(qi, tst)
        cx.wait("act", tst)
        tm2 = cx.sig("act", act.activation(out=ep.m2, in_=s1[:, 0:TN], func=AF.Square))
        cx.wait("dve", tm2, tst)
        tv = cx.sig("dve", dve.scalar_tensor_tensor(out=ep.rstd, in0=s2[:, 0:TN], scalar=eps_col[:, 0:1], in1=ep.m2,
                                                    op0=ALU.add, op1=ALU.subtract))
        cx.wait("act", tv)
        tsd = cx.sig("act", act.activation(out=ep.rstd, in_=ep.rstd, func=AF.Sqrt))
        cx.wait("dve", tsd)
        dve.reciprocal(out=ep.rstd, in_=ep.rstd)
        tn = cx.sig("dve", dve.scalar_tensor_tensor(out=ep.nmr, in0=s1[:, 0:TN], scalar=-1.0, in1=ep.rstd,
                                                    op0=ALU.mult, op1=ALU.mult))
        ep.stat_free.append(tn)
        tx = None
        for m in range(KD):
            dve.tensor_tensor(out=xs[:, m, :], in0=xs[:, m, :], in1=ep.rstd, op=ALU.mult)
            tt = cx.sig("dve", dve.tensor_tensor(out=xs[:, m, :], in0=xs[:, m, :], in1=ep.nmr, op=ALU.add))
            cx.wait("act", tt)
            tx = cx.sig("act", act.activation(out=xs[:, m, :], in_=xs[:, m, :], func=AF.Identity,
                                              scale=vcol(lng, l * 8 + m), bias=vcol(lnb, l * 8 + m)))
        ep.stat_free.append(tt)
        return tx

    def load_w(ds, dst, src, nk, after=()):
        P = dst.shape[0]
        cx.wait("pool", *after)
        tk = None
        for k in range(nk):
            tk = cx.dma("pool", ds, dst[:, k, :], src[k * P:(k + 1) * P, :])
        return tk

    d_ffnw = cx.dsem("ffnw")
    ffn_w_free = []


    def make_h(dst, xs, l, sc0, sh0, b, x_ready, free):
        cx.wait("pool", x_ready, tok_mod, *free)
        tk = None
        for m in range(KD):
            tk = cx.sig("pool", pool.tensor_scalar(out=dst[:, m, :], in0=xs[:, m, :], scalar1=modT[:, l, sc0 + m, b:b + 1],
                                                   scalar2=modT[:, l, sh0 + m, b:b + 1], op0=ALU.mult, op1=ALU.add))
        return tk

    def ffn_pass(l, src, dst):
        TN = 512
        NT = T // TN
        cw = VOFF["ffn_conv"] + l * 3 * KF
        d_s1 = cx.dsem("w2s1")
        d_s2 = cx.dsem("w2s2")
        with nc.sbuf_tensor(f"w2stage{l}", [128, KF, D], BF16) as w2st:
            t1 = load_w(d_s1, w2st, ffn_w_out[l], KF)
            cx.wait("sp", t1)
            t_w2bf = None
            for m in range(KD):
                t_w2bf = cx.dma("sp", d_s2, w2bf_d[m], w2st[:, :, m * 128:(m + 1) * 128])
        with ExitStack() as es1:
            w1s = es1.enter_context(nc.sbuf_tensor(f"ffn_w1_{l}", [128, KD, 2 * DFF], BF16))
            w2r = es1.enter_context(nc.sbuf_tensor(f"ffn_w2r_{l}", [128, 4, KF, 128], BF16))
            xbuf = es1.enter_context(nc.sbuf_tensor(f"fx{l}", [128, 2, KD, TN], F32))
            hbuf = es1.enter_context(nc.sbuf_tensor(f"fh{l}", [128, 1, KD, TN], BF16))
            hid = es1.enter_context(nc.sbuf_tensor(f"fhid{l}", [128, KF, TN], BF16))
            abuf = es1.enter_context(nc.sbuf_tensor(f"fab{l}", [128, 2, TN + 2], F32))
            accb = es1.enter_context(nc.sbuf_tensor(f"facc{l}", [128, 2, TN], F32))
            glb = es1.enter_context(nc.sbuf_tensor(f"fgl{l}", [128, 2, TN], F32))
            carry = es1.enter_context(nc.sbuf_tensor(f"fcar{l}", [128, KF, 2], F32))
            for e_ in ("pool", "sp", "act", "dve", "pe"):
                cx.wait(e_, t_w2bf)
            t_w = load_w(d_ffnw, w1s, ffn_w_in[l], KD)
            w2ring = Ring([w2r[:, i] for i in range(4)])
            d_w2 = [cx.dsem(f"w2r{i}") for i in range(4)]
            w2tok = {}

            def issue_w2(m):
                wi_, wt_, fr = w2ring.next()
                cx.wait("sp", *fr)
                w2tok[m] = (wi_, wt_, cx.dma("sp", d_w2[wi_], wt_.rearrange("p c n -> p (c n)"), w2bf_d[m].rearrange("p c n -> p (c n)")))

            def ymm_ffn(m):
                wi_, wt_, tk_ = w2tok[m]
                cx.wait("pe", tk_)
                return [(wt_[:, c, :], hid[:, c, :]) for c in range(KF)]

            def yrel_ffn(m, ty):
                wi_, wt_, tk_ = w2tok[m]
                w2ring.release(wi_, ty)
                if m + 4 < KD:
                    issue_w2(m + 4)
            ep = ep_glob.set_tn(TN)
            xring = Ring([xbuf[:, i] for i in range(2)])
            hring = Ring([hbuf[:, i] for i in range(1)])
            p1 = Ring(banks[0:4])
            yring = Ring(banks[4:6])
            aring = Ring([abuf[:, i] for i in range(2)])
            cring = Ring([accb[:, i] for i in range(2)])
            gring = Ring([glb[:, i] for i in range(2)])
            d_x = [cx.dsem(f"fx{i}") for i in range(2)]
            d_st = cx.dsem("fst")
            tiles = [(b, t) for b in range(NB) for t in range(NT)]
            if ntiles_dbg:
                tiles = [(b, t) for b in range(NB) for t in range(ntiles_dbg)]
            loads = {}

            def issue_load(idx):
                b, t = tiles[idx]
                xi, xs, fr = xring.next()
                cx.wait("sp", *fr)
                tk = None
                for m in range(KD):
                    tk = cx.dma("sp", d_x[xi], xs[:, m, :], src[b, m * 128:(m + 1) * 128, t * TN:(t + 1) * TN])
                loads[idx] = (xi, xs, tk)

            issue_load(0)
            hid_free = []
            hinfo = {}

            def issue_h(idx):
                b, t = tiles[idx]
                xi, xs, tl = loads[idx]
                hi, hs, fr = hring.next()
                th = make_h(hs, xs, l, 32, 24, b, tl, fr)
                hinfo[idx] = (hi, hs, th)

            issue_h(0)
            t_store = None
            for idx, (b, t) in enumerate(tiles):
                xi, xs, tl = loads[idx]
                hi, hs, th = hinfo[idx]
                if idx + 1 < len(tiles):
                    issue_load(idx + 1)
                for m_ in range(4):
                    issue_w2(m_)
                if t == 0:
                    cx.wait("act", *hid_free)
                    act.memzero(carry[:])
                cx.wait("pe", th, t_w)
                last_hid = None
                for c in range(KF):
                    ai, ab_, fra = p1.next()
                    cx.wait("pe", *fra)
                    ta = _mm_group(cx, ab_[:, 0:TN], [(w1s[:, k, c * 128:(c + 1) * 128], hs[:, k, :]) for k in range(KD)])
                    vi, vb_, frv = p1.next()
                    cx.wait("pe", *frv)
                    tv = _mm_group(cx, vb_[:, 0:TN], [(w1s[:, k, DFF + c * 128:DFF + (c + 1) * 128], hs[:, k, :]) for k in range(KD)])
                    bi, A, frb = aring.next()
                    cx.wait("act", ta, *frb)
                    act.copy(out=A[:, 0:2], in_=carry[:, c, :])
                    act.copy(out=A[:, 2:TN + 2], in_=ab_[:, 0:TN])
                    tcp = cx.sig("act", act.copy(out=carry[:, c, :], in_=ab_[:, TN - 2:TN]))
                    p1.release(ai, tcp)
                    ci, acc, frc = cring.next()
                    cx.wait("dve", tcp, *frc)
                    dve.tensor_scalar(out=acc, in0=A[:, 2:TN + 2], scalar1=vecs[:, cw + 2 * KF + c:cw + 2 * KF + c + 1], scalar2=None, op0=ALU.mult)
                    dve.scalar_tensor_tensor(out=acc, in0=A[:, 1:TN + 1], scalar=vecs[:, cw + KF + c:cw + KF + c + 1], in1=acc, op0=ALU.mult, op1=ALU.add)
                    tcv = cx.sig("dve", dve.scalar_tensor_tensor(out=acc, in0=A[:, 0:TN], scalar=vecs[:, cw + c:cw + c + 1], in1=acc, op0=ALU.mult, op1=ALU.add))
                    aring.release(bi, tcv)
                    gi, gl, frg = gring.next()
                    cx.wait("act", tcv, *frg)
                    tg = cx.sig("act", act.activation(out=gl, in_=acc, func=AF.Gelu_apprx_tanh))
                    cring.release(ci, tg)
                    cx.wait("dve", tg, tv, *hid_free)
                    thd = cx.sig("dve", dve.tensor_tensor(out=hid[:, c, :], in0=gl, in1=vb_[:, 0:TN], op=ALU.mult))
                    p1.release(vi, thd)
                    gring.release(gi, thd)
                    last_hid = thd
                hid_free = []
                hring.release(hi, ta, tv)
                if idx + 1 < len(tiles):
                    issue_h(idx + 1)
                cx.wait("pe", last_hid)
                tx = epilogue(ep, l, 1, b, xs, yring, ymm_ffn, "ln2_g", "ln2_b", tl, yrel=yrel_ffn)
                hid_free = [Tok("pe", cx.sem["pe"], cx.cnt["pe"])]
                cx.wait("sp", tx)
                ts = None
                for m in range(KD):
                    ts = cx.dma("sp", d_st, dst[b, m * 128:(m + 1) * 128, t * TN:(t + 1) * TN], xs[:, m, :])
                xring.release(xi, ts)
                t_store = ts
            ffn_w_free.clear()
            ffn_w_free.append(Tok("pe", cx.sem["pe"], cx.cnt["pe"]))
            cx.barrier(extra=[t_store])

    def conv_pass(l, j, src, dst):
        TN = 512
        NT = T // TN
        cwo = VOFF["conv_w"] + j * 3 * KD
        with ExitStack() as es2:
            wi = es2.enter_context(nc.sbuf_tensor(f"cwi{l}", [128, KD, 3 * D], BF16))
            wo = es2.enter_context(nc.sbuf_tensor(f"cwo{l}", [128, KD, D], BF16))
            xbuf = es2.enter_context(nc.sbuf_tensor(f"cx{l}", [128, 3, KD, TN], F32))
            hbuf = es2.enter_context(nc.sbuf_tensor(f"ch{l}", [128, 2, KD, TN], BF16))
            hid = es2.enter_context(nc.sbuf_tensor(f"chid{l}", [128, KD, TN], BF16))
            ubuf = es2.enter_context(nc.sbuf_tensor(f"cu{l}", [128, 2, TN], F32))
            cubuf = es2.enter_context(nc.sbuf_tensor(f"ccu{l}", [128, 2, TN + 2], F32))
            accb = es2.enter_context(nc.sbuf_tensor(f"cacc{l}", [128, 2, TN], F32))
            carry = es2.enter_context(nc.sbuf_tensor(f"ccar{l}", [128, KD, 2], F32))
            d_w = cx.dsem("cw")
            load_w(d_w, wi, conv_w_in[j], KD)
            t_w = load_w(d_w, wo, conv_w_out[j], KD)
            ep = ep_glob.set_tn(TN)
            xring = Ring([xbuf[:, i] for i in range(3)])
            hring = Ring([hbuf[:, i] for i in range(2)])
            p1 = Ring(banks[0:4])
            yring = Ring(banks[4:6])
            uring = Ring([ubuf[:, i] for i in range(2)])
            curing = Ring([cubuf[:, i] for i in range(2)])
            cring = Ring([accb[:, i] for i in range(2)])
            d_x = [cx.dsem(f"cx{i}") for i in range(3)]
            d_st = cx.dsem("cst")
            tiles = [(b, t) for b in range(NB) for t in range(ntiles_dbg or NT)]
            loads, hinfo = {}, {}

            def issue_load(idx):
                b, t = tiles[idx]
                xi, xs, fr = xring.next()
                cx.wait("sp", *fr)
                tk = None
                for m in range(KD):
                    tk = cx.dma("sp", d_x[xi], xs[:, m, :], src[b, m * 128:(m + 1) * 128, t * TN:(t + 1) * TN])
                loads[idx] = (xi, xs, tk)

            def issue_h(idx):
                b, t = tiles[idx]
                xi, xs, tl = loads[idx]
                hi, hs, fr = hring.next()
                hinfo[idx] = (hi, hs, make_h(hs, xs, l, 8, 0, b, tl, fr))

            issue_load(0)
            issue_h(0)
            hid_free = []
            t_store = None
            for idx, (b, t) in enumerate(tiles):
                xi, xs, tl = loads[idx]
                hi, hs, th = hinfo[idx]
                if idx + 1 < len(tiles):
                    issue_load(idx + 1)
                if t == 0:
                    cx.wait("act", *hid_free)
                    act.memzero(carry[:])
                cx.wait("pe", th, t_w)
                last_hid = None
                for m in range(KD):
                    def grp(off):
                        return [(wi[:, k, off + m * 128:off + (m + 1) * 128], hs[:, k, :]) for k in range(KD)]
                    ui_, ub_, fr = p1.next()
                    cx.wait("pe", *fr)
                    tu = _mm_group(cx, ub_[:, 0:TN], grp(2 * D))
                    gi_, gb_, fr = p1.next()
                    cx.wait("pe", *fr)
                    tcg = _mm_group(cx, gb_[:, 0:TN], grp(D))
                    bi_, bb_, fr = p1.next()
                    cx.wait("pe", *fr)
                    tbg = _mm_group(cx, bb_[:, 0:TN], grp(0))
                    si, us, fr = uring.next()
                    cx.wait("act", tu, *fr)
                    tus = cx.sig("act", act.copy(out=us, in_=ub_[:, 0:TN]))
                    p1.release(ui_, tus)
                    qi, CU, fr = curing.next()
                    cx.wait("dve", tus, tcg, *fr)
                    tcu = cx.sig("dve", dve.tensor_tensor(out=CU[:, 2:TN + 2], in0=us, in1=gb_[:, 0:TN], op=ALU.mult))
                    p1.release(gi_, tcu)
                    uring.release(si, tcu)
                    cx.wait("act", tcu)
                    act.copy(out=CU[:, 0:2], in_=carry[:, m, :])
                    tcar = cx.sig("act", act.copy(out=carry[:, m, :], in_=CU[:, TN:TN + 2]))
                    ci, acc, fr = cring.next()
                    cx.wait("dve", tcar, *fr)
                    dve.tensor_scalar(out=acc, in0=CU[:, 2:TN + 2], scalar1=vecs[:, cwo + 2 * KD + m:cwo + 2 * KD + m + 1], scalar2=None, op0=ALU.mult)
                    dve.scalar_tensor_tensor(out=acc, in0=CU[:, 1:TN + 1], scalar=vecs[:, cwo + KD + m:cwo + KD + m + 1], in1=acc, op0=ALU.mult, op1=ALU.add)
                    dve.scalar_tensor_tensor(out=acc, in0=CU[:, 0:TN], scalar=vecs[:, cwo + m:cwo + m + 1], in1=acc, op0=ALU.mult, op1=ALU.add)
                    cx.wait("dve", tbg, *hid_free)
                    thd = cx.sig("dve", dve.tensor_tensor(out=hid[:, m, :], in0=acc, in1=bb_[:, 0:TN], op=ALU.mult))
                    curing.release(qi, thd)
                    cring.release(ci, thd)
                    p1.release(bi_, thd)
                    last_hid = thd
                hid_free = []
                hring.release(hi, tbg)
                if idx + 1 < len(tiles):
                    issue_h(idx + 1)
                cx.wait("pe", last_hid)
                tx = epilogue(ep, l, 0, b, xs, yring,
                              lambda m: [(wo[:, k, m * 128:(m + 1) * 128], hid[:, k, :]) for k in range(KD)],
                              "ln1_g", "ln1_b", tl)
                hid_free = [Tok("pe", cx.sem["pe"], cx.cnt["pe"])]
                cx.wait("sp", tx)
                ts = None
                for m in range(KD):
                    ts = cx.dma("sp", d_st, dst[b, m * 128:(m + 1) * 128, t * TN:(t + 1) * TN], xs[:, m, :])
                xring.release(xi, ts)
                t_store = ts
            cx.barrier(extra=[t_store])

    def pool_pass(l, src, dst):
        TN = 512
        NT = T // TN
        H = 16
        with ExitStack() as es3:
            wi = es3.enter_context(nc.sbuf_tensor(f"pwi{l}", [128, KD, D], BF16))
            wg = es3.enter_context(nc.sbuf_tensor(f"pwg{l}", [128, 4, 2, 256], BF16))
            wo = es3.enter_context(nc.sbuf_tensor(f"pwo{l}", [128, KD, D], BF16))
            xbuf = es3.enter_context(nc.sbuf_tensor(f"px{l}", [128, 3, KD, TN], F32))
            hbuf = es3.enter_context(nc.sbuf_tensor(f"ph{l}", [128, 2, KD, TN], BF16))
            pooled = es3.enter_context(nc.sbuf_tensor(f"ppl{l}", [128, KD, TN], BF16))
            zs = es3.enter_context(nc.sbuf_tensor(f"pz{l}", [128, KD, TN], BF16))
            ubuf = es3.enter_context(nc.sbuf_tensor(f"pu{l}", [128, 2, TN + H], F32))
            sbuf2 = es3.enter_context(nc.sbuf_tensor(f"ps{l}", [128, 2, 2, TN + H], F32))
            carry = es3.enter_context(nc.sbuf_tensor(f"pcar{l}", [128, KD, H], F32))
            d_w = cx.dsem("pw")
            load_w(d_w, wi, pool_w_in[0], KD)
            tk = None
            for g in range(4):
                for k in range(2):
                    tk = cx.dma("pool", d_w, wg[:, g, k, :], pool_w_grp[0, g, k * 128:(k + 1) * 128, :])
            t_w = load_w(d_w, wo, pool_w_out[0], KD)
            ep = ep_glob.set_tn(TN)
            xring = Ring([xbuf[:, i] for i in range(3)])
            hring = Ring([hbuf[:, i] for i in range(2)])
            p1 = Ring(banks[0:4])
            yring = Ring(banks[4:6])
            uring = Ring([ubuf[:, i] for i in range(2)])
            sring = Ring([sbuf2[:, i] for i in range(2)])
            d_x = [cx.dsem(f"px{i}") for i in range(3)]
            d_st = cx.dsem("pst")
            tiles = [(b, t) for b in range(NB) for t in range(ntiles_dbg or NT)]
            loads, hinfo = {}, {}

            def issue_load(idx):
                b, t = tiles[idx]
                xi, xs, fr = xring.next()
                cx.wait("sp", *fr)
                tk = None
                for m in range(KD):
                    tk = cx.dma("sp", d_x[xi], xs[:, m, :], src[b, m * 128:(m + 1) * 128, t * TN:(t + 1) * TN])
                loads[idx] = (xi, xs, tk)

            def issue_h(idx):
                b, t = tiles[idx]
                xi, xs, tl = loads[idx]
                hi, hs, fr = hring.next()
                hinfo[idx] = (hi, hs, make_h(hs, xs, l, 8, 0, b, tl, fr))

            issue_load(0)
            issue_h(0)
            pooled_free, z_free = [], []
            t_store = None
            rc0 = VOFF["pool_rc"]
            for idx, (b, t) in enumerate(tiles):
                xi, xs, tl = loads[idx]
                hi, hs, th = hinfo[idx]
                if idx + 1 < len(tiles):
                    issue_load(idx + 1)
                if t == 0:
                    act.memzero(carry[:])
                cx.wait("pe", th, t_w)
                last_p = None
                for m in range(KD):
                    g = m // 2
                    w = POOL_W[g]
                    ui_, ub_, fr = p1.next()
                    cx.wait("pe", *fr)
                    tu = _mm_group(cx, ub_[:, 0:TN], [(wi[:, k, m * 128:(m + 1) * 128], hs[:, k, :]) for k in range(KD)])
                    si, U, fr = uring.next()
                    cx.wait("act", tu, *fr)
                    act.copy(out=U[:, 0:H], in_=carry[:, m, :])
                    act.copy(out=U[:, H:H + TN], in_=ub_[:, 0:TN])
                    tcar = cx.sig("act", act.copy(out=carry[:, m, :], in_=ub_[:, TN - H:TN]))
                    p1.release(ui_, tcar)
                    ri, S, fr = sring.next()
                    cx.wait("dve", tcar, *fr)
                    L = TN + H
                    cur = U
                    step = 1
                    n = 0
                    while step < w:
                        nxt = S[:, n % 2]
                        dve.tensor_tensor(out=nxt[:, step:L], in0=cur[:, step:L], in1=cur[:, 0:L - step], op=ALU.add)
                        cur = nxt
                        step *= 2
                        n += 1
                    cx.wait("dve", *pooled_free)
                    tp = cx.sig("dve", dve.scalar_tensor_tensor(out=pooled[:, m, :], in0=cur[:, H:H + TN], scalar=1.0 / w,
                                                                in1=U[:, H:H + TN], op0=ALU.mult, op1=ALU.subtract))
                    if t == 0:
                        tfx = cx.sig("dve", dve.tensor_tensor(out=cur[:, H:H + 16], in0=cur[:, H:H + 16], in1=vecs[:, rc0 + g * 16:rc0 + (g + 1) * 16], op=ALU.mult))
                        dve.wait_ge(tfx.sem, tfx.val)
                        tp = cx.sig("dve", dve.tensor_tensor(out=pooled[:, m, 0:16], in0=cur[:, H:H + 16], in1=U[:, H:H + 16], op=ALU.subtract))
                    uring.release(si, tp)
                    sring.release(ri, tp)
                    last_p = tp
                pooled_free = []
                hring.release(hi, tu)
                if idx + 1 < len(tiles):
                    issue_h(idx + 1)
                cx.wait("pe", last_p)
                last_z = None
                for m in range(KD):
                    g, e = m // 2, m % 2
                    zi_, zb_, fr = p1.next()
                    cx.wait("pe", *fr)
                    tzm = _mm_group(cx, zb_[:, 0:TN], [(wg[:, g, k, e * 128:(e + 1) * 128], pooled[:, 2 * g + k, :]) for k in range(2)])
                    cx.wait("act", tzm, *z_free)
                    tze = cx.sig("act", act.activation(out=zs[:, m, :], in_=zb_[:, 0:TN], func=AF.Identity,
                                                       scale=vecs[:, VOFF["pool_scale"] + m:VOFF["pool_scale"] + m + 1]))
                    p1.release(zi_, tze)
                    last_z = tze
                z_free = []
                pooled_free = [Tok("pe", cx.sem["pe"], cx.cnt["pe"])]
                cx.wait("pe", last_z)
                tx = epilogue(ep, l, 0, b, xs, yring,
                              lambda m: [(wo[:, k, m * 128:(m + 1) * 128], zs[:, k, :]) for k in range(KD)],
                              "ln1_g", "ln1_b", tl)
                z_free = [Tok("pe", cx.sem["pe"], cx.cnt["pe"])]
                cx.wait("sp", tx)
                ts = None
                for m in range(KD):
                    ts = cx.dma("sp", d_st, dst[b, m * 128:(m + 1) * 128, t * TN:(t + 1) * TN], xs[:, m, :])
                xring.release(xi, ts)
                t_store = ts
            cx.barrier(extra=[t_store])

    NH, HD, NG = 16, 64, 4
    NEG = -30000.0

    def dsync(ins):
        t = cx.sig("dve", ins)
        dve.wait_ge(t.sem, t.val)
        return t

    def asyncw(ins):
        t = cx.sig("act", ins)
        act.wait_ge(t.sem, t.val)
        return t

    def nsa_proj_pass(l, src):
        TN = 512
        NT = T // TN
        with ExitStack() as es4:
            wi = es4.enter_context(nc.sbuf_tensor("nwi", [128, KD, NSA_IN], BF16))
            xbuf = es4.enter_context(nc.sbuf_tensor("nx", [128, 2, KD, TN], F32))
            hbuf = es4.enter_context(nc.sbuf_tensor("nh", [128, 2, KD, TN], BF16))
            evb = es4.enter_context(nc.sbuf_tensor("nev", [128, 4, TN], BF16))
            evb2 = es4.enter_context(nc.sbuf_tensor("nev2", [128, 2, 2, 4, 65], BF16))
            gtb = es4.enter_context(nc.sbuf_tensor("ngt", [128, 2, 48], F32))
            d_w = cx.dsem("nw")
            t_w = load_w(d_w, wi, nsa_w_in[0], KD)
            xring = Ring([xbuf[:, i] for i in range(2)])
            hring = Ring([hbuf[:, i] for i in range(2)])
            ering = Ring([evb[:, i] for i in range(4)])
            e2ring = Ring([evb2[:, i] for i in range(2)])
            t_e2 = cx.sig("dve", dve.memset(evb2[:], 1.0))
            gring = Ring([gtb[:, i] for i in range(2)])
            p1 = Ring(banks[0:4])
            p2 = Ring(banks[4:6])
            p3 = Ring(banks[6:8])
            d_x = [cx.dsem(f"nx{i}") for i in range(2)]
            d_st = cx.dsem("nst")
            tiles = [(b, t) for b in range(NB) for t in range(ntiles_dbg or NT)]
            loads, hinfo = {}, {}

            def issue_load(idx):
                b, t = tiles[idx]
                xi, xs, fr = xring.next()
                cx.wait("sp", *fr)
                tk = None
                for m in range(KD):
                    tk = cx.dma("sp", d_x[xi], xs[:, m, :], src[b, m * 128:(m + 1) * 128, t * TN:(t + 1) * TN])
                loads[idx] = (xi, xs, tk)

            def issue_h(idx):
                b, t = tiles[idx]
                xi, xs, tl = loads[idx]
                hi, hs, fr = hring.next()
                th = make_h(hs, xs, l, 8, 0, b, tl, fr)
                xring.release(xi, th)
                hinfo[idx] = (hi, hs, th)

            issue_load(0)
            issue_h(0)
            t_store = None
            fm = [(m * 128, ("q", m), 0.125) for m in range(8)]
            for nm, c0 in (("kc", 1024), ("vc", 1280), ("ks", 1536), ("kw", 2048)):
                fm += [(c0, (nm, 0), 1.0), (c0 + 128, (nm, 1), 1.0)]
            dsts = {"q": q_d, "kc": kc_d, "vc": vc_d, "ks": ks_d, "kw": kw_d}
            for idx, (b, t) in enumerate(tiles):
                hi, hs, th = hinfo[idx]
                if idx + 1 < len(tiles):
                    issue_load(idx + 1)
                cx.wait("pe", th, t_w)
                tlast = None
                for n, (c0, (nm, ci), scl) in enumerate(fm):
                    pi, pb, fr = p1.next()
                    cx.wait("pe", *fr)
                    tm = _mm_group(cx, pb[:, 0:TN], [(wi[:, k, c0:c0 + 128], hs[:, k, :]) for k in range(KD)])
                    tlast = tm
                    ei, eb, fr = ering.next()
                    if n % 2 == 0:
                        cx.wait("act", tm, *fr)
                        te = cx.sig("act", act.activation(out=eb, in_=pb[:, 0:TN], func=AF.Copy, scale=scl))
                    else:
                        cx.wait("dve", tm, *fr)
                        te = cx.sig("dve", dve.tensor_scalar(out=eb, in0=pb[:, 0:TN], scalar1=scl, scalar2=None, op0=ALU.mult))
                    p1.release(pi, te)
                    cx.wait("sp", te)
                    dd = dsts[nm]
                    ts = cx.dma("sp", d_st, dd[b, 2 * ci:2 * ci + 2, :, t * TN:(t + 1) * TN].rearrange("h d t -> (h d) t"), eb)
                    ering.release(ei, ts)
                    t_store = ts
                for tb in range(TN // 128):
                    tok0 = t * TN + tb * 128
                    pi, pb, fr = p2.next()
                    cx.wait("pe", *fr)
                    lh = [hs[:, k, tb * 128:(tb + 1) * 128] for k in range(KD)]
                    _mm_group(cx, pb[:, 0:256], [(lh[k], wi[:, k, 1792:2048]) for k in range(KD)], last_sig=False)
                    tv = _mm_group(cx, pb[:, 256:512], [(lh[k], wi[:, k, 2304:2560]) for k in range(KD)])
                    gi, gb, fr2 = p3.next()
                    cx.wait("pe", *fr2)
                    tg = _mm_group(cx, gb[:, 0:48], [(lh[k], wi[:, k, 2560:2608]) for k in range(KD)])
                    tlast = tg
                    ei, eb, fr = e2ring.next()
                    cx.wait("dve", tv, *fr)
                    te = cx.sig("dve", dve.tensor_copy(out=eb[:, :, :, 0:64], in_=pb[:, 0:512].rearrange("p (s g d) -> p s g d", s=2, g=4)))
                    p2.release(pi, te)
                    cx.wait("sp", te)
                    cx.dma("sp", d_st, vs_d[b, tok0:tok0 + 128, :], eb[:, 0].rearrange("p g d -> p (g d)"))
                    ts = cx.dma("sp", d_st, vw_d[b, tok0:tok0 + 128, :], eb[:, 1].rearrange("p g d -> p (g d)"))
                    e2ring.release(ei, ts)
                    si, sgt, fr = gring.next()
                    cx.wait("act", tg, *fr)
                    tsg = cx.sig("act", act.activation(out=sgt, in_=gb[:, 0:48], func=AF.Sigmoid))
                    p3.release(gi, tsg)
                    cx.wait("sp", tsg)
                    ts = cx.dma("sp", d_st, gate_d[b, tok0:tok0 + 128, :], sgt)
                    gring.release(si, ts)
                    t_store = ts
                hring.release(hi, tlast)
                if idx + 1 < len(tiles):
                    issue_h(idx + 1)
            cx.barrier(extra=[t_store])

    def nsa_attn_pass():
        NQB = T // 128
        with ExitStack() as es5:
            ident = es5.enter_context(nc.sbuf_tensor("a_ident", [128, 128], BF16))
            negtri = es5.enter_context(nc.sbuf_tensor("a_negtri", [128, 2, 512], BF16))
            cmask = es5.enter_context(nc.sbuf_tensor("a_cmask", [128, 17, 512], BF16))
            btab = es5.enter_context(nc.sbuf_tensor("a_btab", [128, 32, 16], F32))
            bC = es5.enter_context(nc.sbuf_tensor("a_bC", [128, 2, 32, 16], F32))
            cms = es5.enter_context(nc.sbuf_tensor("a_cms", [128, 128], F32))
            ads = es5.enter_context(nc.sbuf_tensor("a_ads", [128, 128], F32))
            ovl = es5.enter_context(nc.sbuf_tensor("a_ovl", [128, 2, 64], BF16))
            kcmpT = es5.enter_context(nc.sbuf_tensor("a_kcmp", [64, NB, NG, 256], BF16))
            vcmp = es5.enter_context(nc.sbuf_tensor("a_vcmp", [128, NB, 2, NG, 65], BF16))
            w2sb = es5.enter_context(nc.sbuf_tensor("a_w2", [128, 2, 2, 64], BF16))
            posT = es5.enter_context(nc.sbuf_tensor("a_posT", [64, 2, 32], BF16))
            d_c = cx.dsem("ac")
            cx.dma("pool", d_c, ident[:], c_ident[:, :])
            cx.dma("pool", d_c, negtri[:], c_negtri[:, :, :])
            cx.dma("pool", d_c, cmask[:], c_cmask[:, :, :])
            cx.dma("pool", d_c, ovl[:], c_ovl[:, :, :])
            cx.dma("pool", d_c, posT[:, 0, :], nsa_posT[0])
            cx.dma("pool", d_c, posT[:, 1, :], nsa_posT[1])
            for kv, wsrc in enumerate((nsa_cmp_k_w2, nsa_cmp_v_w2)):
                for ec in range(2):
                    cx.dma("pool", d_c, w2sb[:, kv, ec, :], wsrc[0, ec * 128:(ec + 1) * 128, :])
            t_c1 = cx.dma("pool", d_c, cms[:], c_cms[:, :])
            cx.dma("sp", d_c, btab[:], c_btab[:, :, :])
            cx.dma("sp", d_c, bC[:], c_bC[:, :, :, :])
            t_c2 = cx.dma("sp", d_c, ads[:], c_ads[:, :])
            t_consts = t_c2
            cx.wait("dve", t_consts)
            dve.memset(kcmpT[:], 0.0)
            dve.memset(vcmp[:], 0.0)
            t_z = dsync(dve.memset(vcmp[:, :, :, :, 64:65], 1.0))

            with ExitStack() as es6:
                w1sb = es6.enter_context(nc.sbuf_tensor("c_w1", [64, 32, 256], BF16))
                csrc = es6.enter_context(nc.sbuf_tensor("c_src", [64, NG, T], BF16))
                hidT = es6.enter_context(nc.sbuf_tensor("c_hid", [128, 2, 256], BF16))
                posb = es6.enter_context(nc.sbuf_tensor("c_posb", [128, 2], F32))
                d_w1 = cx.dsem("cw1")
                d_src = cx.dsem("csrc")
                src_free = []
                w1_free = []
                t_hz = dsync(dve.memset(hidT[:], 0.0))
                hid_free = []
                for kv, (w1src, sdram) in enumerate(((nsa_cmp_k_w1, kc_d), (nsa_cmp_v_w1, vc_d))):
                    cx.wait("pool", *w1_free)
                    tw1 = None
                    for l0 in range(0, 32, 8):
                        tw1 = cx.dma("pool", d_w1, w1sb[:, l0:l0 + 8, :], w1src[0, l0:l0 + 8].rearrange("l d e -> d l e"))
                    cx.wait("pe", tw1, t_consts, *hid_free)
                    pbk = banks[7]
                    for ec in range(2):
                        tpb = _mm_group(cx, pbk[:, ec:ec + 1], [(w1sb[:, l_, ec * 128:(ec + 1) * 128], posT[:, kv, l_:l_ + 1]) for l_ in range(32)])
                    cx.wait("dve", tpb)
                    t_posb = dsync(dve.tensor_copy(out=posb[:], in_=pbk[:, 0:2]))
                    for b in range(NB):
                        cx.wait("sp", *src_free)
                        tsrc = cx.dma("sp", d_src, csrc[:], sdram[b].rearrange("g d t -> d g t"))
                        src_free = []
                        cx.wait("pe", tsrc, t_posb)
                        for g in range(NG):
                            hb = [banks[0], banks[1]]
                            thid = []
                            for ec in range(2):
                                cx.wait("pe", *hid_free)
                                th_ = _mm_group(cx, hb[ec][:, 0:255],
                                                [(w1sb[:, l_, ec * 128:(ec + 1) * 128], csrc[:, g, l_:l_ + 16 * 254 + 1:16]) for l_ in range(32)])
                                cx.wait("act", th_, t_posb, t_hz, *hid_free)
                                thid.append(cx.sig("act", act.activation(out=hidT[:, ec, 0:255], in_=hb[ec][:, 0:255], func=AF.Gelu_apprx_tanh,
                                                                         bias=posb[:, ec:ec + 1], scale=1.0)))
                            hid_free = []
                            cx.wait("pe", *thid)
                            if kv == 0:
                                ob = banks[2]
                                to = _mm_group(cx, ob[0:64, 0:255], [(w2sb[:, 0, ec, :], hidT[:, ec, 0:255]) for ec in range(2)])
                                cx.wait("dve", to, t_z)
                                te = cx.sig("dve", dve.tensor_copy(out=kcmpT[:, b, g, 0:255], in_=ob[0:64, 0:255]))
                            else:
                                ob = banks[2]
                                for nt in range(2):
                                    to = _mm_group(cx, ob[:, nt * 64:(nt + 1) * 64], [(hidT[:, ec, nt * 128:(nt + 1) * 128], w2sb[:, 1, ec, :]) for ec in range(2)])
                                cx.wait("dve", to, t_z)
                                te = cx.sig("dve", dve.tensor_copy(out=vcmp[:, b, :, g, 0:64], in_=ob[:, 0:128].rearrange("p (n d) -> p n d", n=2)))
                            hid_free = [te, Tok("pe", cx.sem["pe"], cx.cnt["pe"])]
                        src_free = [Tok("pe", cx.sem["pe"], cx.cnt["pe"])]
                    w1_free = [Tok("pe", cx.sem["pe"], cx.cnt["pe"])]
                cx.barrier()

            with ExitStack() as es7:
                KA = es7.enter_context(nc.sbuf_tensor("KA", [128, NG, T], BF16))
                KW = es7.enter_context(nc.sbuf_tensor("KW", [64, NG, T], BF16))
                VS = es7.enter_context(nc.sbuf_tensor("VS", [128, 32, NG, 65], BF16))
                VW = es7.enter_context(nc.sbuf_tensor("VW", [128, 32, NG, 65], BF16))
                QP = es7.enter_context(nc.sbuf_tensor("QP", [128, 3, NH * 128], BF16))
                GT = es7.enter_context(nc.sbuf_tensor("GT", [128, 3, 48], F32))
                PTb = es7.enter_context(nc.sbuf_tensor("PT", [128, 4, 512], BF16))
                otok = es7.enter_context(nc.sbuf_tensor("otok", [128, 2, D], F32))
                otb = es7.enter_context(nc.sbuf_tensor("otb", [128, D], BF16))
                oTs = es7.enter_context(nc.sbuf_tensor("oTs", [128, 2, KD, 128], BF16))
                with ExitStack() as es8:
                    rz = es8.enter_context(nc.sbuf_tensor("rz", [128, 3, 4], F32))
                    fac = es8.enter_context(nc.sbuf_tensor("fac", [128, 3, 4], F32))
                    imp = es8.enter_context(nc.sbuf_tensor("imp", [128, 64], F32))
                    sc = es8.enter_context(nc.sbuf_tensor("sc", [128, 64], F32))
                    sc2 = es8.enter_context(nc.sbuf_tensor("sc2", [128, 64], F32))
                    m8 = es8.enter_context(nc.sbuf_tensor("m8", [128, 2, 8], F32))
                    MB = es8.enter_context(nc.sbuf_tensor("MB", [128, 128], BF16))
                    d_kv = cx.dsem("kv")
                    d_q = [cx.dsem("q0"), cx.dsem("q1"), cx.dsem("q2")]
                    d_o = cx.dsem("ost")
                    cx.wait("dve", t_consts)
                    t_ones = dsync(dve.memset(MB[:], 0.0))
                    tpb_bf = banks[6][:, :].bitcast(BF16)
                    otp_bf = banks[7][:, :].bitcast(BF16)
                    otp_free = []
                    fr_ = {"cmp": [], "mb": [], "tp": [], "ow": [], "os": []}
                    sel_ready = {}
                    t_ind = None
                    for g in range(NG):
                        t_ind = cx.dma("pool", d_c, KA[64:128, g, :], c_ind[:, :])
                    qring = Ring([(QP[:, i], GT[:, i]) for i in range(3)])
                    sring = Ring(banks[0:2])
                    ptring = Ring([PTb[:, i] for i in range(4)])
                    oring = Ring([otok[:, i] for i in range(2)])
                    otring = Ring([oTs[:, i] for i in range(2)])
                    kv_free = []
                    t_ostore = None
                    otb_free = []
                    for b in range(NB):
                        cx.wait("sp", *kv_free, t_ones)
                        cx.dma("sp", d_kv, KA[0:64, :, :], ks_d[b].rearrange("g d t -> d g t"))
                        cx.dma("sp", d_kv, KW[:], kw_d[b].rearrange("g d t -> d g t"))
                        for k0 in range(0, 32, 8):
                            cx.dma("sp", d_kv, VS[:, k0:k0 + 8].rearrange("p k g d -> p k (g d)"),
                                   vs_d[b, k0 * 128:(k0 + 8) * 128, :].rearrange("(k p) c -> p k c", p=128))
                            t_kv = cx.dma("sp", d_kv, VW[:, k0:k0 + 8].rearrange("p k g d -> p k (g d)"),
                                          vw_d[b, k0 * 128:(k0 + 8) * 128, :].rearrange("(k p) c -> p k c", p=128))
                        qinfo = {}

                        def issue_q(qb):
                            qi, (qp, gt), fr = qring.next()
                            cx.wait("sp", *fr)
                            cx.dma("sp", d_q[qi], qp[0:64, :].rearrange("p (h t) -> p h t", t=128), q_d[b, :, :, qb * 128:(qb + 1) * 128].rearrange("h d t -> d h t"))
                            tq = cx.dma("sp", d_q[qi], gt, gate_d[b, qb * 128:(qb + 1) * 128, :])
                            qinfo[qb] = (qi, qp, gt, tq)


                        nqb = NQB if not ntiles_dbg else ntiles_dbg * 4
                        Q = {}

                        def prep_q(qbn):
                            qi_, (qp_, gt_), fr = qring.next()
                            cx.wait("sp", *fr)
                            cx.dma("sp", d_q[qi_], qp_[0:64, :].rearrange("p (h t) -> p h t", t=128), q_d[b, :, :, qbn * 128:(qbn + 1) * 128].rearrange("h d t -> d h t"))
                            tq_ = cx.dma("sp", d_q[qi_], gt_, gate_d[b, qbn * 128:(qbn + 1) * 128, :])
                            Q[qbn] = dict(qi=qi_, qp=qp_, gt=gt_, tq=tq_, readers=[], part2={})

                        def prep_o(qbn):
                            oi_, ot_, ofr_ = oring.next()
                            Q[qbn].update(oi=oi_, ot=ot_, ofr=ofr_)

                        def qk_tile(lhsT, rhs, extra=None):
                            si, sbk, fr = sring.next()
                            cx.wait("pe", *fr)
                            prs = [(lhsT, rhs)]
                            if extra is not None:
                                prs.append((ident[:], extra))
                            return si, sbk, _mm_group(cx, sbk[:, :], prs)

                        def exp_tile(si, sbk, ts, bias_of_r):
                            pi, pt, fr = ptring.next()
                            cx.wait("act", ts, *fr)
                            te = None
                            for r in range(4):
                                te = cx.sig("act", act.activation(out=pt[:, r * 128:(r + 1) * 128], in_=sbk[:, r * 128:(r + 1) * 128],
                                                                  func=AF.Exp, bias=bias_of_r(r), scale=1.0))
                            sring.release(si, te)
                            return pi, pt, te

                        pend = []
                        st = {"tl": None}

                        def flush():
                            while pend:
                                pend.pop(0)()

                        def issue(lhsT, rhs, extra, bias_fn, pv_fn):
                            si, sbk, ts = qk_tile(lhsT, rhs, extra)
                            pi, pt, te = exp_tile(si, sbk, ts, bias_fn)
                            flush()
                            pend.append(lambda: pv_fn(pi, pt, te))

                        def evac_cmp(qbn, g, tpv):
                            C_ = Q[qbn]
                            qp, gt, tq, ot = C_["qp"], C_["gt"], C_["tq"], C_["ot"]
                            ocb, impb = banks[2], banks[3]
                            cx.wait("dve", tpv, tq, *C_["ofr"])
                            ocv = ocb[:, 0:260].rearrange("p (r c) -> p r c", c=65)
                            dsync(dve.tensor_scalar(out=rz[:, 0, :], in0=ocv[:, :, 64], scalar1=1e-30, scalar2=None, op0=ALU.max))
                            dsync(dve.reciprocal(out=rz[:, 0, :], in_=rz[:, 0, :]))
                            gv = gt[:, 12 * g:12 * g + 12].rearrange("p (r k) -> p r k", k=3)
                            dsync(dve.tensor_tensor(out=fac[:, 0, :], in0=rz[:, 0, :], in1=gv[:, :, 0], op=ALU.mult))
                            dsync(dve.tensor_scalar(out=imp[:], in0=impb[:, 0:64], scalar1=rz[:, 0, 0:1], scalar2=None, op0=ALU.mult))
                            for r in range(1, 4):
                                dsync(dve.scalar_tensor_tensor(out=imp[:], in0=impb[:, r * 64:(r + 1) * 64], scalar=rz[:, 0, r:r + 1], in1=imp[:],
                                                               op0=ALU.mult, op1=ALU.add))
                            tlast = None
                            for r in range(4):
                                tlast = cx.sig("dve", dve.tensor_scalar(out=ot[:, (4 * g + r) * 64:(4 * g + r + 1) * 64], in0=ocv[:, r, 0:64],
                                                                        scalar1=fac[:, 0, r:r + 1], scalar2=None, op0=ALU.mult))
                            fr_["cmp"] = [tlast]
                            j0 = 64 - 2 * qbn
                            dsync(dve.tensor_tensor(out=sc[:], in0=imp[:], in1=cms[:, j0:j0 + 64], op=ALU.mult))
                            dsync(dve.tensor_tensor(out=sc[:], in0=sc[:], in1=ads[:, j0:j0 + 64], op=ALU.add))
                            dsync(dve.memset(sc[:, 0:1], 1e4))
                            dsync(dve.max(out=m8[:, 0, :], in_=sc[:]))
                            dsync(dve.match_replace(out=sc2[:], in_to_replace=m8[:, 0, :], in_values=sc[:], imm_value=-3e4))
                            dsync(dve.max(out=m8[:, 1, :], in_=sc2[:]))
                            cx.wait("dve", *fr_["mb"])
                            tmb = dsync(dve.tensor_scalar(out=MB[:, 64:128], in0=sc[:], scalar1=m8[:, 1, 7:8], scalar2=NEG, op0=ALU.is_lt, op1=ALU.mult))

                            def part2():
                                cx.wait("pe", tmb, *fr_["tp"])
                                ttp = cx.sig("pe", pe.transpose(tpb_bf[:, 0:128], MB[:], ident[:]))
                                fr_["mb"] = [ttp]
                                cx.wait("dve", ttp, tq)
                                tcp = dsync(dve.tensor_copy(out=qp[64:128, 4 * g * 128:(4 * g + 4) * 128].rearrange("p (r t) -> p r t", t=128),
                                                            in_=tpb_bf[64:128, 0:128].unsqueeze(1).to_broadcast([64, 4, 128])))
                                fr_["tp"] = [tcp]
                                sel_ready[(qbn, g)] = tcp
                            C_["part2"][g] = part2

                        def evac_acc(qbn, g, tpv, bank, slot, gcol, key):
                            C_ = Q[qbn]
                            gt, tq, ot = C_["gt"], C_["tq"], C_["ot"]
                            cx.wait("dve", tpv, tq)
                            ov = bank[:, 0:260].rearrange("p (r c) -> p r c", c=65)
                            dsync(dve.reciprocal(out=rz[:, slot, :], in_=ov[:, :, 64]))
                            gv = gt[:, 12 * g:12 * g + 12].rearrange("p (r k) -> p r k", k=3)
                            dsync(dve.tensor_tensor(out=fac[:, slot, :], in0=rz[:, slot, :], in1=gv[:, :, gcol], op=ALU.mult))
                            tl_ = None
                            for r in range(4):
                                osl = ot[:, (4 * g + r) * 64:(4 * g + r + 1) * 64]
                                tl_ = dsync(dve.scalar_tensor_tensor(out=osl, in0=ov[:, r, 0:64], scalar=fac[:, slot, r:r + 1], in1=osl,
                                                                     op0=ALU.mult, op1=ALU.add))
                            fr_[key] = [tl_]
                            st["tl"] = tl_

                        def phase_A(qbn, g):
                            C_ = Q[qbn]
                            qp, tq = C_["qp"], C_["tq"]
                            cx.wait("pe", tq, t_kv, t_ind)
                            qg = qp[0:64, 4 * g * 128:(4 * g + 4) * 128]
                            nts = [nt for nt in range(2) if 128 * nt < 8 * qbn + 7]
                            for nt in nts:
                                full = (128 * nt + 127 <= 8 * qbn - 2)
                                extra = None if full else cmask[:, (8 * qbn - 128 * nt) // 8, :]

                                def pv_c(pi, pt, te, g=g, nt=nt, nts=nts, qbn=qbn):
                                    ocb, impb = banks[2], banks[3]
                                    cx.wait("pe", te, *(fr_["cmp"] if nt == nts[0] else []))
                                    tpv = None
                                    for r in range(4):
                                        lt = pt[:, r * 128:(r + 1) * 128]
                                        pe.matmul(ocb[:, r * 65:(r + 1) * 65], lt, vcmp[:, b, nt, g, :], start=(nt == nts[0] and r == 0), stop=(nt == nts[-1]))
                                        tpv = cx.sig("pe", pe.matmul(impb[:, r * 64:(r + 1) * 64], lt, ovl[:, nt, :], start=(nt == nts[0] and r == 0), stop=(nt == nts[-1])))
                                    ptring.release(pi, tpv)
                                    if nt == nts[-1]:
                                        Q[qbn]["readers"].append(tpv)
                                        evac_cmp(qbn, g, tpv)

                                issue(kcmpT[:, b, g, nt * 128:(nt + 1) * 128], qg, extra,
                                      (lambda r, g=g, nt=nt, qbn=qbn: bC[:, nt, qbn, 4 * g + r:4 * g + r + 1]), pv_c)

                        def run_part2(qbn, g):
                            flush()
                            Q[qbn]["part2"].pop(g)()

                        def phase_BC(qb, g):
                            C_ = Q[qb]
                            qp = C_["qp"]
                            qg = qp[0:64, 4 * g * 128:(4 * g + 4) * 128]
                            qg2 = qp[:, 4 * g * 128:(4 * g + 4) * 128]
                            kts = [kt for kt in range(qb - 4, qb + 1) if kt >= 0]
                            for kt in kts:
                                extra = None
                                if kt == qb:
                                    extra = negtri[:, 0, :]
                                elif kt == qb - 4:
                                    extra = negtri[:, 1, :]

                                def pv_w(pi, pt, te, g=g, kt=kt, kts=kts):
                                    owb = banks[4]
                                    cx.wait("pe", te, *(fr_["ow"] if kt == kts[0] else []))
                                    tpv = None
                                    for r in range(4):
                                        tpv = cx.sig("pe", pe.matmul(owb[:, r * 65:(r + 1) * 65], pt[:, r * 128:(r + 1) * 128], VW[:, kt, g, :],
                                                                     start=(kt == kts[0] and r == 0), stop=(kt == kts[-1])))
                                    ptring.release(pi, tpv)
                                    if kt == kts[-1]:
                                        Q[qb]["readers"].append(tpv)
                                        evac_acc(qb, g, tpv, owb, 1, 2, "ow")

                                issue(KW[:, g, kt * 128:(kt + 1) * 128], qg, extra,
                                      (lambda r, g=g, kt=kt: btab[:, kt - qb + 31, 4 * g + r:4 * g + r + 1]), pv_w)
                            cx.wait("pe", sel_ready[(qb, g)])
                            for kt in range(qb + 1):
                                extra = negtri[:, 0, :] if kt == qb else None

                                def pv_s(pi, pt, te, g=g, kt=kt):
                                    osb = banks[5]
                                    cx.wait("pe", te, *(fr_["os"] if kt == 0 else []))
                                    tpv = None
                                    for r in range(4):
                                        tpv = cx.sig("pe", pe.matmul(osb[:, r * 65:(r + 1) * 65], pt[:, r * 128:(r + 1) * 128], VS[:, kt, g, :],
                                                                     start=(kt == 0 and r == 0), stop=(kt == qb)))
                                    ptring.release(pi, tpv)
                                    if kt == qb:
                                        Q[qb]["readers"].append(tpv)
                                        evac_acc(qb, g, tpv, osb, 2, 1, "os")

                                issue(KA[:, g, kt * 128:(kt + 1) * 128], qg2, extra,
                                      (lambda r, g=g, kt=kt: btab[:, kt - qb + 31, 4 * g + r:4 * g + r + 1]), pv_s)

                        prep_q(0)
                        if nqb > 1:
                            prep_q(1)
                        prep_o(0)
                        for g in range(NG):
                            phase_A(0, g)
                            run_part2(0, g)
                        for qb in range(nqb):
                            C0 = Q[qb]
                            qi, qp, gt, tq, oi, ot = C0["qi"], C0["qp"], C0["gt"], C0["tq"], C0["oi"], C0["ot"]
                            if qb + 2 < nqb:
                                prep_q(qb + 2)
                            if qb + 1 < nqb:
                                prep_o(qb + 1)
                            for g in range(NG):
                                if qb + 1 < nqb:
                                    phase_A(qb + 1, g)
                                phase_BC(qb, g)
                                if qb + 1 < nqb:
                                    run_part2(qb + 1, g)
                            flush()
                            tl_ = st["tl"]
                            qring.release(qi, *C0["readers"], tl_)


                            cx.wait("act", tl_, *otb_free)
                            tob = cx.sig("act", act.copy(out=otb[:], in_=ot[:]))
                            oring.release(oi, tob)
                            ti_, oT, fr = otring.next()
                            cx.wait("pe", tob, *otp_free)
                            ttp = None
                            for m in range(KD):
                                ttp = cx.sig("pe", pe.transpose(otp_bf[:, m * 128:(m + 1) * 128], otb[:, m * 128:(m + 1) * 128], ident[:]))
                            otb_free = [ttp]
                            cx.wait("dve", ttp, *fr)
                            tcp = dsync(dve.tensor_copy(out=oT.rearrange("p m t -> p (m t)"), in_=otp_bf[:, 0:KD * 128]))
                            otp_free = [tcp]
                            cx.wait("sp", tcp)
                            tst = None
                            for m in range(KD):
                                tst = cx.dma("sp", d_o, o_d[b, m * 128:(m + 1) * 128, qb * 128:(qb + 1) * 128], oT[:, m, :])
                            otring.release(ti_, tst)
                            t_ostore = tst
                        kv_free = [Tok("pe", cx.sem["pe"], cx.cnt["pe"])]
                    cx.barrier(extra=[t_ostore])

    def proj_pass(l, src, hid_src, w_src, dst):
        TN = 512
        NT = T // TN
        with ExitStack() as es9:
            wo = es9.enter_context(nc.sbuf_tensor("jwo", [128, KD, D], BF16))
            xbuf = es9.enter_context(nc.sbuf_tensor("jx", [128, 3, KD, TN], F32))
            hbuf = es9.enter_context(nc.sbuf_tensor("jh", [128, 2, KD, TN], BF16))
            d_w = cx.dsem("jw")
            t_w = load_w(d_w, wo, w_src, KD)
            ep = ep_glob.set_tn(TN)
            xring = Ring([xbuf[:, i] for i in range(3)])
            hring = Ring([hbuf[:, i] for i in range(2)])
            yring = Ring(banks[4:6])
            d_x = [cx.dsem(f"jx{i}") for i in range(3)]
            d_h = [cx.dsem(f"jh{i}") for i in range(2)]
            d_st = cx.dsem("jst")
            tiles = [(b, t) for b in range(NB) for t in range(ntiles_dbg or NT)]
            loads = {}

            def issue_load(idx):
                b, t = tiles[idx]
                xi, xs, fr = xring.next()
                hi, hs, frh = hring.next()
                cx.wait("sp", *fr, *frh)
                tk = None
                for m in range(KD):
                    tk = cx.dma("sp", d_x[xi], xs[:, m, :], src[b, m * 128:(m + 1) * 128, t * TN:(t + 1) * TN])
                th = None
                for m in range(KD):
                    th = cx.dma("sp", d_h[hi], hs[:, m, :], hid_src[b, m * 128:(m + 1) * 128, t * TN:(t + 1) * TN])
                loads[idx] = (xi, xs, tk, hi, hs, th)

            issue_load(0)
            t_store = None
            for idx, (b, t) in enumerate(tiles):
                xi, xs, tl, hi, hs, th = loads[idx]
                if idx + 1 < len(tiles):
                    issue_load(idx + 1)
                cx.wait("pe", th, t_w)
                tx = epilogue(ep, l, 0, b, xs, yring,
                              lambda m: [(wo[:, k, m * 128:(m + 1) * 128], hs[:, k, :]) for k in range(KD)],
                              "ln1_g", "ln1_b", tl)
                hring.release(hi, Tok("pe", cx.sem["pe"], cx.cnt["pe"]))
                cx.wait("sp", tx)
                ts = None
                for m in range(KD):
                    ts = cx.dma("sp", d_st, dst[b, m * 128:(m + 1) * 128, t * TN:(t + 1) * TN], xs[:, m, :])
                xring.release(xi, ts)
                t_store = ts
            cx.barrier(extra=[t_store])


    cur = xT
    bufs = [actA, actB]
    nb = 0
    for l in layers:
        mixer = l % 3
        d1 = bufs[nb]
        nb ^= 1
        if mixer == 0:
            with nc.named_scope(f"conv{l}"):
                conv_pass(l, l // 3, cur, d1)
        elif mixer == 1:
            with nc.named_scope(f"pool{l}"):
                pool_pass(l, cur, d1)
        else:
            with nc.named_scope("nsa_proj"):
                nsa_proj_pass(l, cur)
            with nc.named_scope("nsa_attn"):
                nsa_attn_pass()
            with nc.named_scope("nsa_out"):
                proj_pass(l, cur, o_d, nsa_w_out[0], d1)
        d2 = outT if l == layers[-1] else bufs[nb]
        if d2 is not outT:
            nb ^= 1
        with nc.named_scope(f"ffn{l}"):
            ffn_pass(l, d1, d2)
        cur = d2
    return nc


_NSA_C = None


def _nsa_consts():
    global _NSA_C
    if _NSA_C is not None:
        return _NSA_C
    NEG = -30000.0
    c = {}
    c["c_ident"] = np.eye(128, dtype=np.float32)
    a = np.arange(128)[:, None]
    bq = np.arange(128)[None, :]
    nt0 = np.where(a > bq, NEG, 0.0).astype(np.float32)
    nt1 = np.where(a <= bq, NEG, 0.0).astype(np.float32)
    c["c_negtri"] = np.ascontiguousarray(np.stack([np.tile(nt0, (1, 4)), np.tile(nt1, (1, 4))], axis=1))
    cm = np.zeros((128, 17, 128), np.float32)
    for oi in range(17):
        cm[:, oi, :] = np.where(16 * (a - 8 * oi) + 31 > bq, NEG, 0.0)
    c["c_cmask"] = np.ascontiguousarray(np.tile(cm, (1, 1, 4)))
    n = np.arange(256)
    cs, ce = n * 16, n * 16 + 31
    ss = np.arange(64) * 64
    ov = ((cs[:, None] <= ss[None, :] + 63) & (ce[:, None] >= ss[None, :])).astype(np.float32)
    ov[255] = 0.0
    c["c_ovl"] = np.ascontiguousarray(ov.reshape(2, 128, 64).transpose(1, 0, 2))
    btr = (np.arange(128) >= 64).astype(np.int64)[:, None]
    jr = np.arange(128)[None, :] - 64
    c["c_cms"] = (jr <= btr - 2).astype(np.float32)
    ad = np.zeros((128, 128), np.float32)
    ad[jr > btr] = -1e4
    ad[(jr == btr) | (jr == btr - 1)] = 1e4
    c["c_ads"] = ad
    slopes = np.exp2(-np.arange(1, 17, dtype=np.float64) / 2.0)
    dl = np.arange(32) - 31
    bt = slopes[None, None, :] * (128.0 * dl[None, :, None] + np.arange(128)[:, None, None] - 64.0)
    c["c_btab"] = bt.astype(np.float32)
    nl = np.arange(128)[:, None, None, None]
    ntt = np.arange(2)[None, :, None, None]
    qb = np.arange(32)[None, None, :, None]
    bc = slopes[None, None, None, :] * (16.0 * (128 * ntt + nl) + 15.5 - 128.0 * qb - 64.0)
    c["c_bC"] = bc.astype(np.float32)
    ind = (np.arange(T)[None, :] // 64 == np.arange(64)[:, None]).astype(np.float32)
    c["c_ind"] = np.ascontiguousarray(ind)
    _NSA_C = c
    return c


def host_prep(inputs, core):
    b0 = core * NB
    f = {k: np.asarray(v) for k, v in inputs.items()}
    m = {}
    m["xT"] = np.ascontiguousarray(np.transpose(f["x"][b0:b0 + NB], (0, 2, 1)))
    m["condT"] = np.ascontiguousarray(f["c"][b0:b0 + NB].reshape(NB, KD, 128).transpose(2, 1, 0))
    vecs = np.zeros((128, NV), np.float32)

    def put(name, arr):
        a = _pm(arr.reshape(-1))
        vecs[:, VOFF[name]:VOFF[name] + a.shape[1]] = a

    put("ada_b", f["ada_b"])
    put("ln1_g", f["ln1_g"])
    put("ln1_b", f["ln1_b"])
    put("ln2_g", f["ln2_g"])
    put("ln2_b", f["ln2_b"])
    put("ffn_conv", f["ffn_conv"])
    put("conv_w", f["conv_w"])
    put("pool_scale", f["pool_scale"])
    rc = np.ones((4, 16), np.float32)
    for g, w in enumerate(POOL_W):
        cnt = np.minimum(np.arange(1, 17), w).astype(np.float32)
        rc[g] = (np.float32(1.0) / cnt) * np.float32(w)
    for g, w in enumerate(POOL_W):
        cnt = np.minimum(np.arange(1, 17), w).astype(np.float32)
        rc[g] = np.float32(1.0) / cnt
    vecs[:, VOFF["pool_rc"]:VOFF["pool_rc"] + 64] = rc.reshape(1, 64)
    m["vecs"] = vecs
    m.update(_nsa_consts())
    m["nsa_posT"] = np.ascontiguousarray(np.stack([f["nsa_cmp_pos_k"][0].T, f["nsa_cmp_pos_v"][0].T]).astype(np.float32))
    for k in ("nsa_w_in", "nsa_cmp_k_w1", "nsa_cmp_k_w2", "nsa_cmp_v_w1", "nsa_cmp_v_w2", "nsa_w_out"):
        m[k] = np.ascontiguousarray(f[k], dtype=np.float32)
    for k in ("ada_w", "ffn_w_in", "ffn_w_out", "conv_w_in", "conv_w_out", "pool_w_in", "pool_w_grp", "pool_w_out"):
        m[k] = np.ascontiguousarray(f[k], dtype=np.float32)
    return m


def kernel(**inputs):
    nc = build_program()
    in_maps = [host_prep(inputs, c) for c in range(8)]
    res = run_bass_kernel_spmd(nc, in_maps, core_ids=list(range(8)))
    outs = [np.transpose(r["outT"], (0, 2, 1)) for r in res.results]
    return np.ascontiguousarray(np.concatenate(outs, axis=0), dtype=np.float32)
```

```python
import math
from contextlib import ExitStack
import numpy as np
import concourse.bass as bass
import concourse.mybir as mybir
from concourse.bass_utils import run_bass_kernel_spmd

F32 = mybir.dt.float32
BF16 = mybir.dt.bfloat16
AF = mybir.ActivationFunctionType
ALU = mybir.AluOpType

D = 1024
T = 4096
NB = 2
DEPTH = 4
DFF = 2816
KD = D // 128
KF = DFF // 128
ALPHA = (2.0 * DEPTH) ** 0.25
LN_EPS = 1e-5
EPS_P = LN_EPS / (ALPHA * ALPHA)
NSA_IN = 2608
POOL_W = (2, 4, 8, 16)

VOFF = {}
_nv = 0


def _valloc(name, n):
    global _nv
    VOFF[name] = _nv
    _nv += n


_valloc("ada_b", DEPTH * 48)
_valloc("ln1_g", DEPTH * 8)
_valloc("ln1_b", DEPTH * 8)
_valloc("ln2_g", DEPTH * 8)
_valloc("ln2_b", DEPTH * 8)
_valloc("ffn_conv", DEPTH * 3 * KF)
_valloc("conv_w", 2 * 3 * KD)
_valloc("pool_scale", KD)
_valloc("pool_rc", 4 * 16)
NV = _nv


def _pm(v):
    v = np.asarray(v, np.float32)
    return np.ascontiguousarray(v.reshape(-1, 128).T)


class Tok:
    __slots__ = ("key", "sem", "val")

    def __init__(self, key, sem, val):
        self.key, self.sem, self.val = key, sem, val


class Ctx:
    def __init__(self, nc):
        self.nc = nc
        self.eng = {"pe": nc.tensor, "act": nc.scalar, "dve": nc.vector, "pool": nc.gpsimd, "sp": nc.sync}
        self.sem = {e: nc.alloc_semaphore(name=f"s_{e}") for e in self.eng}
        self.cnt = {e: 0 for e in self.eng}
        self.seen = {e: {} for e in self.eng}
        self.ndma = 0

    def sig(self, e, ins):
        ins.then_inc(self.sem[e], 1)
        self.cnt[e] += 1
        return Tok(e, self.sem[e], self.cnt[e])

    def wait(self, e, *toks):
        for t in toks:
            if t is None:
                continue
            if isinstance(t, (list, tuple)):
                self.wait(e, *t)
                continue
            if t.key == e:
                continue
            if self.seen[e].get(t.key, 0) >= t.val:
                continue
            self.eng[e].wait_ge(t.sem, t.val)
            self.seen[e][t.key] = t.val

    def dsem(self, name):
        self.ndma += 1
        return [f"d{self.ndma}_{name}", self.nc.alloc_semaphore(name=f"d{self.ndma}_{name}"), 0]

    def dma(self, q, ds, out, in_):
        ins = self.eng[q].dma_start(out=out, in_=in_)
        ins.then_inc(ds[1], 16)
        ds[2] += 16
        return Tok(ds[0], ds[1], ds[2])

    def barrier(self, extra=()):
        toks = []
        for e in self.eng:
            self.wait(e, *extra)
        for e in ("pe", "act", "dve", "pool", "sp"):
            toks.append(self.sig(e, self.eng[e].drain()))
        for e in self.eng:
            self.wait(e, *toks)


class Ring:
    def __init__(self, items):
        self.items = list(items)
        self.free = [[] for _ in self.items]
        self.i = -1

    def next(self):
        self.i = (self.i + 1) % len(self.items)
        fr = self.free[self.i]
        self.free[self.i] = []
        return self.i, self.items[self.i], fr

    def release(self, idx, *toks):
        self.free[idx].extend(t for t in toks if t is not None)


def _mm_group(cx, out, pairs, last_sig=True):
    n = len(pairs)
    tok = None
    for i, (l, r) in enumerate(pairs):
        ins = cx.nc.tensor.matmul(out, l, r, start=(i == 0), stop=(i == n - 1))
        if i == n - 1 and last_sig:
            tok = cx.sig("pe", ins)
    return tok


def build_program(layers=(0, 1, 2, 3), ntiles_dbg=None, dbg=False):
    nc = bass.Bass("TRN2", target_bir_lowering=False)
    cx = Ctx(nc)
    pe, act, dve, pool, sp = nc.tensor, nc.scalar, nc.vector, nc.gpsimd, nc.sync

    def din(name, shape, dt=F32):
        return nc.dram_tensor(name, list(shape), dt, kind="ExternalInput").ap()

    xT = din("xT", [NB, D, T])
    condT = din("condT", [128, KD, NB])
    vecs_d = din("vecs", [128, NV])
    ada_w = din("ada_w", [DEPTH, D, 6 * D])
    ffn_w_in = din("ffn_w_in", [DEPTH, D, 2 * DFF])
    ffn_w_out = din("ffn_w_out", [DEPTH, DFF, D])
    conv_w_in = din("conv_w_in", [2, D, 3 * D])
    conv_w_out = din("conv_w_out", [2, D, D])
    pool_w_in = din("pool_w_in", [1, D, D])
    pool_w_grp = din("pool_w_grp", [1, 4, 256, 256])
    pool_w_out = din("pool_w_out", [1, D, D])
    nsa_w_in = din("nsa_w_in", [1, D, NSA_IN])
    nsa_posT = din("nsa_posT", [2, 64, 32])
    nsa_cmp_k_w1 = din("nsa_cmp_k_w1", [1, 32, 64, 256])
    nsa_cmp_k_w2 = din("nsa_cmp_k_w2", [1, 256, 64])
    nsa_cmp_v_w1 = din("nsa_cmp_v_w1", [1, 32, 64, 256])
    nsa_cmp_v_w2 = din("nsa_cmp_v_w2", [1, 256, 64])
    nsa_w_out = din("nsa_w_out", [1, D, D])
    c_ident = din("c_ident", [128, 128])
    c_negtri = din("c_negtri", [128, 2, 512])
    c_cmask = din("c_cmask", [128, 17, 512])
    c_ovl = din("c_ovl", [128, 2, 64])
    c_cms = din("c_cms", [128, 128])
    c_ads = din("c_ads", [128, 128])
    c_btab = din("c_btab", [128, 32, 16])
    c_bC = din("c_bC", [128, 2, 32, 16])
    c_ind = din("c_ind", [64, T])

    def dscr(name, shape, dt):
        return nc.dram_tensor(name, list(shape), dt, kind=("ExternalOutput" if dbg else "Internal")).ap()

    q_d = dscr("q_d", [NB, 16, 64, T], BF16)
    kc_d = dscr("kc_d", [NB, 4, 64, T], BF16)
    vc_d = dscr("vc_d", [NB, 4, 64, T], BF16)
    ks_d = dscr("ks_d", [NB, 4, 64, T], BF16)
    kw_d = dscr("kw_d", [NB, 4, 64, T], BF16)
    vs_d = dscr("vs_d", [NB, T, 260], BF16)
    vw_d = dscr("vw_d", [NB, T, 260], BF16)
    gate_d = dscr("gate_d", [NB, T, 48], F32)
    o_d = dscr("o_d", [NB, D, T], BF16)
    w2bf_d = nc.dram_tensor("w2bf_d", [KD, 128, KF, 128], BF16, kind="Internal").ap()
    outT = nc.dram_tensor("outT", [NB, D, T], F32, kind="ExternalOutput").ap()
    actA = nc.dram_tensor("actA", [NB, D, T], F32, kind=("ExternalOutput" if dbg else "Internal")).ap()
    actB = nc.dram_tensor("actB", [NB, D, T], F32, kind=("ExternalOutput" if dbg else "Internal")).ap()

    def sb(name, shape, dt=F32):
        return nc.sbuf_tensor(name, list(shape), dt).__enter__()

    def ps(name, shape=(128, 512), dt=F32):
        return nc.psum_tensor(name, list(shape), dt).__enter__()

    vecs = sb("vecs_sb", [128, NV])
    modT = sb("modT", [128, DEPTH, 48, NB])
    gsc = sb("gsc", [128, DEPTH, 2, KD, NB])
    ones_bf = sb("ones_bf", [128, 128], BF16)
    eps_col = sb("eps_col", [128, 1])
    banks = [ps(f"bank{i}") for i in range(8)]

    d_const = cx.dsem("const")
    t_vecs = cx.dma("sp", d_const, vecs[:], vecs_d[:, :])
    cond = sb("cond", [128, KD, NB])
    t_cond = cx.dma("sp", d_const, cond[:], condT[:, :, :])
    t_const = t_cond

    cx.wait("dve", t_const)
    dve.memset(ones_bf[:], 1.0 / 1024.0)
    t_c1 = cx.sig("dve", dve.memset(eps_col[:], EPS_P))

    cx.wait("act", t_const)
    t_silu = cx.sig("act", act.activation(out=cond[:], in_=cond[:], func=AF.Silu))
    adw_cm = [nc.sbuf_tensor(f"adw{i}", [128, KD, 768], F32) for i in range(2)]
    adw = [c_.__enter__() for c_ in adw_cm]
    adw_ring = Ring(adw)
    d_adw = [cx.dsem("adw0"), cx.dsem("adw1")]
    mod_ps = banks[0]
    tok_mod_evac = None
    for l in range(DEPTH):
        for piece in range(8):
            i, wt, fr = adw_ring.next()
            cx.wait("sp", *fr)
            tl = None
            for k in range(KD):
                tl = cx.dma("sp", d_adw[i], wt[:, k, :], ada_w[l, k * 128:(k + 1) * 128, piece * 768:(piece + 1) * 768])
            cx.wait("pe", tl, t_silu, tok_mod_evac)
            tk = None
            for j in range(6):
                f = piece * 6 + j
                tk = _mm_group(cx, mod_ps[:, f * NB:(f + 1) * NB],
                               [(wt[:, k, j * 128:(j + 1) * 128], cond[:, k, :]) for k in range(KD)],
                               last_sig=(j == 5))
            adw_ring.release(i, tk)
        cx.wait("dve", tk, t_const)
        tok_mod_evac = cx.sig("dve", dve.tensor_tensor(
            out=modT[:, l, :, :],
            in0=mod_ps[:, 0:48 * NB].rearrange("p (f b) -> p f b", b=NB),
            in1=vecs[:, VOFF["ada_b"] + l * 48: VOFF["ada_b"] + (l + 1) * 48].unsqueeze(2).to_broadcast([128, 48, NB]),
            op=ALU.add))
    for l in range(DEPTH):
        dve.tensor_scalar(out=modT[:, l, 8:16, :], in0=modT[:, l, 8:16, :], scalar1=1.0, scalar2=None, op0=ALU.add)
        dve.tensor_scalar(out=modT[:, l, 32:40, :], in0=modT[:, l, 32:40, :], scalar1=1.0, scalar2=None, op0=ALU.add)
        dve.tensor_scalar(out=gsc[:, l, 0, :, :], in0=modT[:, l, 16:24, :], scalar1=1.0, scalar2=1.0 / ALPHA,
                          op0=ALU.add, op1=ALU.mult)
        tok_mod = cx.sig("dve", dve.tensor_scalar(out=gsc[:, l, 1, :, :], in0=modT[:, l, 40:48, :], scalar1=1.0,
                                                   scalar2=1.0 / ALPHA, op0=ALU.add, op1=ALU.mult))
    cx.barrier()
    for c_ in reversed(adw_cm):
        c_.__exit__(None, None, None)

    def vcol(name, idx):
        o = VOFF[name] + idx
        return vecs[:, o:o + 1]

    class Epi:
        def __init__(self):
            self.TN = 512
            self.zb_t = [sb(f"zb{i}", [128, 512], BF16) for i in range(2)]
            self.zq_t = [sb(f"zq{i}", [128, 512], BF16) for i in range(2)]
            self.m2_t = sb("m2", [128, 512])
            self.rstd_t = sb("rstd", [128, 512])
            self.nmr_t = sb("nmr", [128, 512])
            self.stat_free = []
            self.set_tn(512)

        def set_tn(self, TN):
            self.TN = TN
            fz = getattr(self, "zb", None)
            self.zb = Ring([t[:, 0:TN] for t in self.zb_t])
            self.zq = Ring([t[:, 0:TN] for t in self.zq_t])
            if fz is not None:
                pass
            self.m2 = self.m2_t[:, 0:TN]
            self.rstd = self.rstd_t[:, 0:TN]
            self.nmr = self.nmr_t[:, 0:TN]
            return self

    ep_glob = Epi()

    def epilogue(ep, l, which, b, xs, y_ring, ymm, lng, lnb, x_ready, yrel=None):
        TN = ep.TN
        s1, s2 = banks[6], banks[7]
        cx.wait("pe", *ep.stat_free)
        ep_free_tmp = ep.stat_free
        ep.stat_free = []
        pend = None
        z_toks = []
        for m in range(KD):
            yi, ybank, fr = y_ring.next()
            cx.wait("pe", *fr)
            ty = _mm_group(cx, ybank[:, 0:TN], ymm(m))
            if yrel is not None:
                yrel(m, ty)
            if pend is not None:
                pm, tzb, tzq, zi, qi, zbt, zqt = pend
                cx.wait("pe", tzb, tzq)
                pe.matmul(s1[:, 0:TN], ones_bf[:], zbt, start=(pm == 0), stop=(pm == KD - 1))
                tst = cx.sig("pe", pe.matmul(s2[:, 0:TN], ones_bf[:], zqt, start=(pm == 0), stop=(pm == KD - 1)))
                ep.zb.release(zi, tst)
                ep.zq.release(qi, tst)
            cx.wait("dve", ty, x_ready, *ep_free_tmp)
            tz = cx.sig("dve", dve.scalar_tensor_tensor(out=xs[:, m, :], in0=ybank[:, 0:TN], scalar=gsc[:, l, which, m, b:b + 1],
                                                        in1=xs[:, m, :], op0=ALU.mult, op1=ALU.add))
            y_ring.release(yi, tz)
            z_toks.append(tz)
            zi, zbt, fr1 = ep.zb.next()
            qi, zqt, fr2 = ep.zq.next()
            cx.wait("act", tz, *fr1, *fr2)
            tzb = cx.sig("act", act.activation(out=zbt, in_=xs[:, m, :], func=AF.Identity))
            tzq = cx.sig("act", act.activation(out=zqt, in_=xs[:, m, :], func=AF.Square))
            pend = (m, tzb, tzq, zi, qi, zbt, zqt)
        pm, tzb, tzq, zi, qi, zbt, zqt = pend
        cx.wait("pe", tzb, tzq)
        pe.matmul(s1[:, 0:TN], ones_bf[:], zbt, start=False, stop=True)
        tst = cx.sig("pe", pe.matmul(s2[:, 0:TN], ones_bf[:], zqt, start=False, stop=True))
        ep.zb.release(zi, tst)
        ep.zq.release(qi, tst)
        cx.wait("act", tst)
        tm2 = cx.sig("act", act.activation(out=ep.m2, in_=s1[:, 0:TN], func=AF.Square))
        cx.wait("dve", tm2, tst)
        tv = cx.sig("dve", dve.scalar_tensor_tensor(out=ep.rstd, in0=s2[:, 0:TN], scalar=eps_col[:, 0:1], in1=ep.m2,
                                                    op0=ALU.add, op1=ALU.subtract))
        cx.wait("act", tv)
        tsd = cx.sig("act", act.activation(out=ep.rstd, in_=ep.rstd, func=AF.Sqrt))
        cx.wait("dve", tsd)
        dve.reciprocal(out=ep.rstd, in_=ep.rstd)
        tn = cx.sig("dve", dve.scalar_tensor_tensor(out=ep.nmr, in0=s1[:, 0:TN], scalar=-1.0, in1=ep.rstd,
                                                    op0=ALU.mult, op1=ALU.mult))
        ep.stat_free.append(tn)
        tx = None
        for m in range(KD):
            dve.tensor_tensor(out=xs[:, m, :], in0=xs[:, m, :], in1=ep.rstd, op=ALU.mult)
            tt = cx.sig("dve", dve.tensor_tensor(out=xs[:, m, :], in0=xs[:, m, :], in1=ep.nmr, op=ALU.add))
            cx.wait("act", tt)
            tx = cx.sig("act", act.activation(out=xs[:, m, :], in_=xs[:, m, :], func=AF.Identity,
                                              scale=vcol(lng, l * 8 + m), bias=vcol(lnb, l * 8 + m)))
        ep.stat_free.append(tt)
        return tx

    def load_w(ds, dst, src, nk, after=()):
        P = dst.shape[0]
        cx.wait("pool", *after)
        tk = None
        for k in range(nk):
            tk = cx.dma("pool", ds, dst[:, k, :], src[k * P:(k + 1) * P, :])
        return tk

    d_ffnw = cx.dsem("ffnw")
    ffn_w_free = []


    def make_h(dst, xs, l, sc0, sh0, b, x_ready, free):
        cx.wait("pool", x_ready, tok_mod, *free)
        tk = None
        for m in range(KD):
            tk = cx.sig("pool", pool.tensor_scalar(out=dst[:, m, :], in0=xs[:, m, :], scalar1=modT[:, l, sc0 + m, b:b + 1],
                                                   scalar2=modT[:, l, sh0 + m, b:b + 1], op0=ALU.mult, op1=ALU.add))
        return tk

    def stage_begin(es_, l):
        w2st = es_.enter_context(nc.sbuf_tensor(f"w2stage{l}", [128, KF, D], BF16))
        t1 = load_w(cx.dsem("w2s1"), w2st, ffn_w_out[l], KF)
        return w2st, t1

    def stage_end(w2st, t1):
        ds2 = cx.dsem("w2s2")
        cx.wait("sp", t1)
        tk = None
        for m in range(KD):
            tk = cx.dma("sp", ds2, w2bf_d[m], w2st[:, :, m * 128:(m + 1) * 128])
        return tk

    def ffn_pass(l, src, dst):
        TN = 512
        NT = T // TN
        cw = VOFF["ffn_conv"] + l * 3 * KF
        with ExitStack() as es1:
            w1s = es1.enter_context(nc.sbuf_tensor(f"ffn_w1_{l}", [128, KD, 2 * DFF], BF16))
            w2r = es1.enter_context(nc.sbuf_tensor(f"ffn_w2r_{l}", [128, 4, KF, 128], BF16))
            xbuf = es1.enter_context(nc.sbuf_tensor(f"fx{l}", [128, 2, KD, TN], F32))
            hbuf = es1.enter_context(nc.sbuf_tensor(f"fh{l}", [128, 1, KD, TN], BF16))
            hid = es1.enter_context(nc.sbuf_tensor(f"fhid{l}", [128, KF, TN], BF16))
            abuf = es1.enter_context(nc.sbuf_tensor(f"fab{l}", [128, 2, TN + 2], F32))
            accb = es1.enter_context(nc.sbuf_tensor(f"facc{l}", [128, 2, TN], F32))
            glb = es1.enter_context(nc.sbuf_tensor(f"fgl{l}", [128, 2, TN], F32))
            carry = es1.enter_context(nc.sbuf_tensor(f"fcar{l}", [128, KF, 2], F32))
            t_w = load_w(d_ffnw, w1s, ffn_w_in[l], KD)
            w2ring = Ring([w2r[:, i] for i in range(4)])
            d_w2 = [cx.dsem(f"w2r{i}") for i in range(4)]
            w2tok = {}

            def issue_w2(m):
                wi_, wt_, fr = w2ring.next()
                cx.wait("sp", *fr)
                w2tok[m] = (wi_, wt_, cx.dma("sp", d_w2[wi_], wt_.rearrange("p c n -> p (c n)"), w2bf_d[m].rearrange("p c n -> p (c n)")))

            def ymm_ffn(m):
                wi_, wt_, tk_ = w2tok[m]
                cx.wait("pe", tk_)
                return [(wt_[:, c, :], hid[:, c, :]) for c in range(KF)]

            def yrel_ffn(m, ty):
                wi_, wt_, tk_ = w2tok[m]
                w2ring.release(wi_, ty)
                if m + 4 < KD:
                    issue_w2(m + 4)
            ep = ep_glob.set_tn(TN)
            xring = Ring([xbuf[:, i] for i in range(2)])
            hring = Ring([hbuf[:, i] for i in range(1)])
            p1 = Ring(banks[0:4])
            yring = Ring(banks[4:6])
            aring = Ring([abuf[:, i] for i in range(2)])
            cring = Ring([accb[:, i] for i in range(2)])
            gring = Ring([glb[:, i] for i in range(2)])
            d_x = [cx.dsem(f"fx{i}") for i in range(2)]
            d_st = cx.dsem("fst")
            tiles = [(b, t) for b in range(NB) for t in range(NT)]
            if ntiles_dbg:
                tiles = [(b, t) for b in range(NB) for t in range(ntiles_dbg)]
            loads = {}

            def issue_load(idx):
                b, t = tiles[idx]
                xi, xs, fr = xring.next()
                cx.wait("sp", *fr)
                tk = None
                for m in range(KD):
                    tk = cx.dma("sp", d_x[xi], xs[:, m, :], src[b, m * 128:(m + 1) * 128, t * TN:(t + 1) * TN])
                loads[idx] = (xi, xs, tk)

            issue_load(0)
            hid_free = []
            hinfo = {}

            def issue_h(idx):
                b, t = tiles[idx]
                xi, xs, tl = loads[idx]
                hi, hs, fr = hring.next()
                th = make_h(hs, xs, l, 32, 24, b, tl, fr)
                hinfo[idx] = (hi, hs, th)

            issue_h(0)
            t_store = None
            for idx, (b, t) in enumerate(tiles):
                xi, xs, tl = loads[idx]
                hi, hs, th = hinfo[idx]
                if idx + 1 < len(tiles):
                    issue_load(idx + 1)
                for m_ in range(4):
                    issue_w2(m_)
                if t == 0:
                    cx.wait("act", *hid_free)
                    act.memzero(carry[:])
                cx.wait("pe", th, t_w)
                last_hid = None
                for c in range(KF):
                    ai, ab_, fra = p1.next()
                    cx.wait("pe", *fra)
                    ta = _mm_group(cx, ab_[:, 0:TN], [(w1s[:, k, c * 128:(c + 1) * 128], hs[:, k, :]) for k in range(KD)])
                    vi, vb_, frv = p1.next()
                    cx.wait("pe", *frv)
                    tv = _mm_group(cx, vb_[:, 0:TN], [(w1s[:, k, DFF + c * 128:DFF + (c + 1) * 128], hs[:, k, :]) for k in range(KD)])
                    bi, A, frb = aring.next()
                    cx.wait("act", ta, *frb)
                    act.copy(out=A[:, 0:2], in_=carry[:, c, :])
                    act.copy(out=A[:, 2:TN + 2], in_=ab_[:, 0:TN])
                    tcp = cx.sig("act", act.copy(out=carry[:, c, :], in_=ab_[:, TN - 2:TN]))
                    p1.release(ai, tcp)
                    ci, acc, frc = cring.next()
                    cx.wait("dve", tcp, *frc)
                    dve.tensor_scalar(out=acc, in0=A[:, 2:TN + 2], scalar1=vecs[:, cw + 2 * KF + c:cw + 2 * KF + c + 1], scalar2=None, op0=ALU.mult)
                    dve.scalar_tensor_tensor(out=acc, in0=A[:, 1:TN + 1], scalar=vecs[:, cw + KF + c:cw + KF + c + 1], in1=acc, op0=ALU.mult, op1=ALU.add)
                    tcv = cx.sig("dve", dve.scalar_tensor_tensor(out=acc, in0=A[:, 0:TN], scalar=vecs[:, cw + c:cw + c + 1], in1=acc, op0=ALU.mult, op1=ALU.add))
                    aring.release(bi, tcv)
                    gi, gl, frg = gring.next()
                    cx.wait("act", tcv, *frg)
                    tg = cx.sig("act", act.activation(out=gl, in_=acc, func=AF.Gelu_apprx_tanh))
                    cring.release(ci, tg)
                    cx.wait("dve", tg, tv, *hid_free)
                    thd = cx.sig("dve", dve.tensor_tensor(out=hid[:, c, :], in0=gl, in1=vb_[:, 0:TN], op=ALU.mult))
                    p1.release(vi, thd)
                    gring.release(gi, thd)
                    last_hid = thd
                hid_free = []
                hring.release(hi, ta, tv)
                if idx + 1 < len(tiles):
                    issue_h(idx + 1)
                cx.wait("pe", last_hid)
                tx = epilogue(ep, l, 1, b, xs, yring, ymm_ffn, "ln2_g", "ln2_b", tl, yrel=yrel_ffn)
                hid_free = [Tok("pe", cx.sem["pe"], cx.cnt["pe"])]
                cx.wait("sp", tx)
                ts = None
                for m in range(KD):
                    ts = cx.dma("sp", d_st, dst[b, m * 128:(m + 1) * 128, t * TN:(t + 1) * TN], xs[:, m, :])
                xring.release(xi, ts)
                t_store = ts
            ffn_w_free.clear()
            ffn_w_free.append(Tok("pe", cx.sem["pe"], cx.cnt["pe"]))
            cx.barrier(extra=[t_store])

    def conv_pass(l, j, src, dst):
        TN = 512
        NT = T // TN
        cwo = VOFF["conv_w"] + j * 3 * KD
        with ExitStack() as es2:
            wi = es2.enter_context(nc.sbuf_tensor(f"cwi{l}", [128, KD, 3 * D], BF16))
            wo = es2.enter_context(nc.sbuf_tensor(f"cwo{l}", [128, KD, D], BF16))
            xbuf = es2.enter_context(nc.sbuf_tensor(f"cx{l}", [128, 2, KD, TN], F32))
            hbuf = es2.enter_context(nc.sbuf_tensor(f"ch{l}", [128, 2, KD, TN], BF16))
            hid = es2.enter_context(nc.sbuf_tensor(f"chid{l}", [128, KD, TN], BF16))
            ubuf = es2.enter_context(nc.sbuf_tensor(f"cu{l}", [128, 2, TN], F32))
            cubuf = es2.enter_context(nc.sbuf_tensor(f"ccu{l}", [128, 2, TN + 2], F32))
            accb = es2.enter_context(nc.sbuf_tensor(f"cacc{l}", [128, 2, TN], F32))
            carry = es2.enter_context(nc.sbuf_tensor(f"ccar{l}", [128, KD, 2], F32))
            d_w = cx.dsem("cw")
            load_w(d_w, wi, conv_w_in[j], KD)
            t_w = load_w(d_w, wo, conv_w_out[j], KD)
            w2st, t_stg = stage_begin(es2, l)
            ep = ep_glob.set_tn(TN)
            xring = Ring([xbuf[:, i] for i in range(2)])
            hring = Ring([hbuf[:, i] for i in range(2)])
            p1 = Ring(banks[0:4])
            yring = Ring(banks[4:6])
            uring = Ring([ubuf[:, i] for i in range(2)])
            curing = Ring([cubuf[:, i] for i in range(2)])
            cring = Ring([accb[:, i] for i in range(2)])
            d_x = [cx.dsem(f"cx{i}") for i in range(2)]
            d_st = cx.dsem("cst")
            tiles = [(b, t) for b in range(NB) for t in range(ntiles_dbg or NT)]
            loads, hinfo = {}, {}

            def issue_load(idx):
                b, t = tiles[idx]
                xi, xs, fr = xring.next()
                cx.wait("sp", *fr)
                tk = None
                for m in range(KD):
                    tk = cx.dma("sp", d_x[xi], xs[:, m, :], src[b, m * 128:(m + 1) * 128, t * TN:(t + 1) * TN])
                loads[idx] = (xi, xs, tk)

            def issue_h(idx):
                b, t = tiles[idx]
                xi, xs, tl = loads[idx]
                hi, hs, fr = hring.next()
                hinfo[idx] = (hi, hs, make_h(hs, xs, l, 8, 0, b, tl, fr))

            issue_load(0)
            issue_h(0)
            hid_free = []
            t_store = None
            for idx, (b, t) in enumerate(tiles):
                xi, xs, tl = loads[idx]
                hi, hs, th = hinfo[idx]
                if idx + 1 < len(tiles):
                    issue_load(idx + 1)
                if t == 0:
                    cx.wait("act", *hid_free)
                    act.memzero(carry[:])
                cx.wait("pe", th, t_w)
                last_hid = None
                for m in range(KD):
                    def grp(off):
                        return [(wi[:, k, off + m * 128:off + (m + 1) * 128], hs[:, k, :]) for k in range(KD)]
                    ui_, ub_, fr = p1.next()
                    cx.wait("pe", *fr)
                    tu = _mm_group(cx, ub_[:, 0:TN], grp(2 * D))
                    gi_, gb_, fr = p1.next()
                    cx.wait("pe", *fr)
                    tcg = _mm_group(cx, gb_[:, 0:TN], grp(D))
                    bi_, bb_, fr = p1.next()
                    cx.wait("pe", *fr)
                    tbg = _mm_group(cx, bb_[:, 0:TN], grp(0))
                    si, us, fr = uring.next()
                    cx.wait("act", tu, *fr)
                    tus = cx.sig("act", act.copy(out=us, in_=ub_[:, 0:TN]))
                    p1.release(ui_, tus)
                    qi, CU, fr = curing.next()
                    cx.wait("dve", tus, tcg, *fr)
                    tcu = cx.sig("dve", dve.tensor_tensor(out=CU[:, 2:TN + 2], in0=us, in1=gb_[:, 0:TN], op=ALU.mult))
                    p1.release(gi_, tcu)
                    uring.release(si, tcu)
                    cx.wait("act", tcu)
                    act.copy(out=CU[:, 0:2], in_=carry[:, m, :])
                    tcar = cx.sig("act", act.copy(out=carry[:, m, :], in_=CU[:, TN:TN + 2]))
                    ci, acc, fr = cring.next()
                    cx.wait("dve", tcar, *fr)
                    dve.tensor_scalar(out=acc, in0=CU[:, 2:TN + 2], scalar1=vecs[:, cwo + 2 * KD + m:cwo + 2 * KD + m + 1], scalar2=None, op0=ALU.mult)
                    dve.scalar_tensor_tensor(out=acc, in0=CU[:, 1:TN + 1], scalar=vecs[:, cwo + KD + m:cwo + KD + m + 1], in1=acc, op0=ALU.mult, op1=ALU.add)
                    dve.scalar_tensor_tensor(out=acc, in0=CU[:, 0:TN], scalar=vecs[:, cwo + m:cwo + m + 1], in1=acc, op0=ALU.mult, op1=ALU.add)
                    cx.wait("dve", tbg, *hid_free)
                    thd = cx.sig("dve", dve.tensor_tensor(out=hid[:, m, :], in0=acc, in1=bb_[:, 0:TN], op=ALU.mult))
                    curing.release(qi, thd)
                    cring.release(ci, thd)
                    p1.release(bi_, thd)
                    last_hid = thd
                hid_free = []
                hring.release(hi, tbg)
                if idx + 1 < len(tiles):
                    issue_h(idx + 1)
                cx.wait("pe", last_hid)
                tx = epilogue(ep, l, 0, b, xs, yring,
                              lambda m: [(wo[:, k, m * 128:(m + 1) * 128], hid[:, k, :]) for k in range(KD)],
                              "ln1_g", "ln1_b", tl)
                hid_free = [Tok("pe", cx.sem["pe"], cx.cnt["pe"])]
                cx.wait("sp", tx)
                ts = None
                for m in range(KD):
                    ts = cx.dma("sp", d_st, dst[b, m * 128:(m + 1) * 128, t * TN:(t + 1) * TN], xs[:, m, :])
                xring.release(xi, ts)
                t_store = ts
            cx.barrier(extra=[t_store, stage_end(w2st, t_stg)])

    def pool_pass(l, src, dst):
        TN = 512
        NT = T // TN
        H = 16
        with ExitStack() as es3:
            wi = es3.enter_context(nc.sbuf_tensor(f"pwi{l}", [128, KD, D], BF16))
            wg = es3.enter_context(nc.sbuf_tensor(f"pwg{l}", [128, 4, 2, 256], BF16))
            wo = es3.enter_context(nc.sbuf_tensor(f"pwo{l}", [128, KD, D], BF16))
            xbuf = es3.enter_context(nc.sbuf_tensor(f"px{l}", [128, 3, KD, TN], F32))
            hbuf = es3.enter_context(nc.sbuf_tensor(f"ph{l}", [128, 2, KD, TN], BF16))
            pooled = es3.enter_context(nc.sbuf_tensor(f"ppl{l}", [128, KD, TN], BF16))
            zs = es3.enter_context(nc.sbuf_tensor(f"pz{l}", [128, KD, TN], BF16))
            ubuf = es3.enter_context(nc.sbuf_tensor(f"pu{l}", [128, 2, TN + H], F32))
            sbuf2 = es3.enter_context(nc.sbuf_tensor(f"ps{l}", [128, 2, 2, TN + H], F32))
            carry = es3.enter_context(nc.sbuf_tensor(f"pcar{l}", [128, KD, H], F32))
            d_w = cx.dsem("pw")
            load_w(d_w, wi, pool_w_in[0], KD)
            tk = None
            for g in range(4):
                for k in range(2):
                    tk = cx.dma("pool", d_w, wg[:, g, k, :], pool_w_grp[0, g, k * 128:(k + 1) * 128, :])
            t_w = load_w(d_w, wo, pool_w_out[0], KD)
            w2st, t_stg = stage_begin(es3, l)
            ep = ep_glob.set_tn(TN)
            xring = Ring([xbuf[:, i] for i in range(3)])
            hring = Ring([hbuf[:, i] for i in range(2)])
            p1 = Ring(banks[0:4])
            yring = Ring(banks[4:6])
            uring = Ring([ubuf[:, i] for i in range(2)])
            sring = Ring([sbuf2[:, i] for i in range(2)])
            d_x = [cx.dsem(f"px{i}") for i in range(3)]
            d_st = cx.dsem("pst")
            tiles = [(b, t) for b in range(NB) for t in range(ntiles_dbg or NT)]
            loads, hinfo = {}, {}

            def issue_load(idx):
                b, t = tiles[idx]
                xi, xs, fr = xring.next()
                cx.wait("sp", *fr)
                tk = None
                for m in range(KD):
                    tk = cx.dma("sp", d_x[xi], xs[:, m, :], src[b, m * 128:(m + 1) * 128, t * TN:(t + 1) * TN])
                loads[idx] = (xi, xs, tk)

            def issue_h(idx):
                b, t = tiles[idx]
                xi, xs, tl = loads[idx]
                hi, hs, fr = hring.next()
                hinfo[idx] = (hi, hs, make_h(hs, xs, l, 8, 0, b, tl, fr))

            issue_load(0)
            issue_h(0)
            pooled_free, z_free = [], []
            t_store = None
            rc0 = VOFF["pool_rc"]
            for idx, (b, t) in enumerate(tiles):
                xi, xs, tl = loads[idx]
                hi, hs, th = hinfo[idx]
                if idx + 1 < len(tiles):
                    issue_load(idx + 1)
                if t == 0:
                    act.memzero(carry[:])
                cx.wait("pe", th, t_w)
                last_p = None
                for m in range(KD):
                    g = m // 2
                    w = POOL_W[g]
                    ui_, ub_, fr = p1.next()
                    cx.wait("pe", *fr)
                    tu = _mm_group(cx, ub_[:, 0:TN], [(wi[:, k, m * 128:(m + 1) * 128], hs[:, k, :]) for k in range(KD)])
                    si, U, fr = uring.next()
                    cx.wait("act", tu, *fr)
                    act.copy(out=U[:, 0:H], in_=carry[:, m, :])
                    act.copy(out=U[:, H:H + TN], in_=ub_[:, 0:TN])
                    tcar = cx.sig("act", act.copy(out=carry[:, m, :], in_=ub_[:, TN - H:TN]))
                    p1.release(ui_, tcar)
                    ri, S, fr = sring.next()
                    cx.wait("dve", tcar, *fr)
                    L = TN + H
                    cur = U
                    step = 1
                    n = 0
                    while step < w:
                        nxt = S[:, n % 2]
                        dve.tensor_tensor(out=nxt[:, step:L], in0=cur[:, step:L], in1=cur[:, 0:L - step], op=ALU.add)
                        cur = nxt
                        step *= 2
                        n += 1
                    cx.wait("dve", *pooled_free)
                    tp = cx.sig("dve", dve.scalar_tensor_tensor(out=pooled[:, m, :], in0=cur[:, H:H + TN], scalar=1.0 / w,
                                                                in1=U[:, H:H + TN], op0=ALU.mult, op1=ALU.subtract))
                    if t == 0:
                        tfx = cx.sig("dve", dve.tensor_tensor(out=cur[:, H:H + 16], in0=cur[:, H:H + 16], in1=vecs[:, rc0 + g * 16:rc0 + (g + 1) * 16], op=ALU.mult))
                        dve.wait_ge(tfx.sem, tfx.val)
                        tp = cx.sig("dve", dve.tensor_tensor(out=pooled[:, m, 0:16], in0=cur[:, H:H + 16], in1=U[:, H:H + 16], op=ALU.subtract))
                    uring.release(si, tp)
                    sring.release(ri, tp)
                    last_p = tp
                pooled_free = []
                hring.release(hi, tu)
                if idx + 1 < len(tiles):
                    issue_h(idx + 1)
                cx.wait("pe", last_p)
                last_z = None
                for m in range(KD):
                    g, e = m // 2, m % 2
                    zi_, zb_, fr = p1.next()
                    cx.wait("pe", *fr)
                    tzm = _mm_group(cx, zb_[:, 0:TN], [(wg[:, g, k, e * 128:(e + 1) * 128], pooled[:, 2 * g + k, :]) for k in range(2)])
                    cx.wait("act", tzm, *z_free)
                    tze = cx.sig("act", act.activation(out=zs[:, m, :], in_=zb_[:, 0:TN], func=AF.Identity,
                                                       scale=vecs[:, VOFF["pool_scale"] + m:VOFF["pool_scale"] + m + 1]))
                    p1.release(zi_, tze)
                    last_z = tze
                z_free = []
                pooled_free = [Tok("pe", cx.sem["pe"], cx.cnt["pe"])]
                cx.wait("pe", last_z)
                tx = epilogue(ep, l, 0, b, xs, yring,
                              lambda m: [(wo[:, k, m * 128:(m + 1) * 128], zs[:, k, :]) for k in range(KD)],
                              "ln1_g", "ln1_b", tl)
                z_free = [Tok("pe", cx.sem["pe"], cx.cnt["pe"])]
                cx.wait("sp", tx)
                ts = None
                for m in range(KD):
                    ts = cx.dma("sp", d_st, dst[b, m * 128:(m + 1) * 128, t * TN:(t + 1) * TN], xs[:, m, :])
                xring.release(xi, ts)
                t_store = ts
            cx.barrier(extra=[t_store, stage_end(w2st, t_stg)])

    NH, HD, NG = 16, 64, 4
    NEG = -30000.0

    def dsync(ins):
        t = cx.sig("dve", ins)
        dve.wait_ge(t.sem, t.val)
        return t

    def asyncw(ins):
        t = cx.sig("act", ins)
        act.wait_ge(t.sem, t.val)
        return t

    def nsa_proj_pass(l, src):
        TN = 512
        NT = T // TN
        with ExitStack() as es4:
            wi = es4.enter_context(nc.sbuf_tensor("nwi", [128, KD, NSA_IN], BF16))
            xbuf = es4.enter_context(nc.sbuf_tensor("nx", [128, 2, KD, TN], F32))
            hbuf = es4.enter_context(nc.sbuf_tensor("nh", [128, 2, KD, TN], BF16))
            evb = es4.enter_context(nc.sbuf_tensor("nev", [128, 4, TN], BF16))
            evb2 = es4.enter_context(nc.sbuf_tensor("nev2", [128, 2, 2, 4, 65], BF16))
            gtb = es4.enter_context(nc.sbuf_tensor("ngt", [128, 2, 48], F32))
            d_w = cx.dsem("nw")
            t_w = load_w(d_w, wi, nsa_w_in[0], KD)
            xring = Ring([xbuf[:, i] for i in range(2)])
            hring = Ring([hbuf[:, i] for i in range(2)])
            ering = Ring([evb[:, i] for i in range(4)])
            e2ring = Ring([evb2[:, i] for i in range(2)])
            t_e2 = cx.sig("dve", dve.memset(evb2[:], 1.0))
            gring = Ring([gtb[:, i] for i in range(2)])
            p1 = Ring(banks[0:4])
            p2 = Ring(banks[4:6])
            p3 = Ring(banks[6:8])
            d_x = [cx.dsem(f"nx{i}") for i in range(2)]
            d_st = cx.dsem("nst")
            tiles = [(b, t) for b in range(NB) for t in range(ntiles_dbg or NT)]
            loads, hinfo = {}, {}

            def issue_load(idx):
                b, t = tiles[idx]
                xi, xs, fr = xring.next()
                cx.wait("sp", *fr)
                tk = None
                for m in range(KD):
                    tk = cx.dma("sp", d_x[xi], xs[:, m, :], src[b, m * 128:(m + 1) * 128, t * TN:(t + 1) * TN])
                loads[idx] = (xi, xs, tk)

            def issue_h(idx):
                b, t = tiles[idx]
                xi, xs, tl = loads[idx]
                hi, hs, fr = hring.next()
                th = make_h(hs, xs, l, 8, 0, b, tl, fr)
                xring.release(xi, th)
                hinfo[idx] = (hi, hs, th)

            issue_load(0)
            issue_h(0)
            t_store = None
            fm = [(m * 128, ("q", m), 0.125) for m in range(8)]
            for nm, c0 in (("kc", 1024), ("vc", 1280), ("ks", 1536), ("kw", 2048)):
                fm += [(c0, (nm, 0), 1.0), (c0 + 128, (nm, 1), 1.0)]
            dsts = {"q": q_d, "kc": kc_d, "vc": vc_d, "ks": ks_d, "kw": kw_d}
            for idx, (b, t) in enumerate(tiles):
                hi, hs, th = hinfo[idx]
                if idx + 1 < len(tiles):
                    issue_load(idx + 1)
                cx.wait("pe", th, t_w)
                tlast = None
                for n, (c0, (nm, ci), scl) in enumerate(fm):
                    pi, pb, fr = p1.next()
                    cx.wait("pe", *fr)
                    tm = _mm_group(cx, pb[:, 0:TN], [(wi[:, k, c0:c0 + 128], hs[:, k, :]) for k in range(KD)])
                    tlast = tm
                    ei, eb, fr = ering.next()
                    if n % 2 == 0:
                        cx.wait("act", tm, *fr)
                        te = cx.sig("act", act.activation(out=eb, in_=pb[:, 0:TN], func=AF.Copy, scale=scl))
                    else:
                        cx.wait("dve", tm, *fr)
                        te = cx.sig("dve", dve.tensor_scalar(out=eb, in0=pb[:, 0:TN], scalar1=scl, scalar2=None, op0=ALU.mult))
                    p1.release(pi, te)
                    cx.wait("sp", te)
                    dd = dsts[nm]
                    ts = cx.dma("sp", d_st, dd[b, 2 * ci:2 * ci + 2, :, t * TN:(t + 1) * TN].rearrange("h d t -> (h d) t"), eb)
                    ering.release(ei, ts)
                    t_store = ts
                for tb in range(TN // 128):
                    tok0 = t * TN + tb * 128
                    pi, pb, fr = p2.next()
                    cx.wait("pe", *fr)
                    lh = [hs[:, k, tb * 128:(tb + 1) * 128] for k in range(KD)]
                    _mm_group(cx, pb[:, 0:256], [(lh[k], wi[:, k, 1792:2048]) for k in range(KD)], last_sig=False)
                    tv = _mm_group(cx, pb[:, 256:512], [(lh[k], wi[:, k, 2304:2560]) for k in range(KD)])
                    gi, gb, fr2 = p3.next()
                    cx.wait("pe", *fr2)
                    tg = _mm_group(cx, gb[:, 0:48], [(lh[k], wi[:, k, 2560:2608]) for k in range(KD)])
                    tlast = tg
                    ei, eb, fr = e2ring.next()
                    cx.wait("dve", tv, *fr)
                    te = cx.sig("dve", dve.tensor_copy(out=eb[:, :, :, 0:64], in_=pb[:, 0:512].rearrange("p (s g d) -> p s g d", s=2, g=4)))
                    p2.release(pi, te)
                    cx.wait("sp", te)
                    cx.dma("sp", d_st, vs_d[b, tok0:tok0 + 128, :], eb[:, 0].rearrange("p g d -> p (g d)"))
                    ts = cx.dma("sp", d_st, vw_d[b, tok0:tok0 + 128, :], eb[:, 1].rearrange("p g d -> p (g d)"))
                    e2ring.release(ei, ts)
                    si, sgt, fr = gring.next()
                    cx.wait("act", tg, *fr)
                    tsg = cx.sig("act", act.activation(out=sgt, in_=gb[:, 0:48], func=AF.Sigmoid))
                    p3.release(gi, tsg)
                    cx.wait("sp", tsg)
                    ts = cx.dma("sp", d_st, gate_d[b, tok0:tok0 + 128, :], sgt)
                    gring.release(si, ts)
                    t_store = ts
                hring.release(hi, tlast)
                if idx + 1 < len(tiles):
                    issue_h(idx + 1)
            cx.barrier(extra=[t_store])

    def nsa_attn_pass():
        NQB = T // 128
        with ExitStack() as es5:
            ident = es5.enter_context(nc.sbuf_tensor("a_ident", [128, 128], BF16))
            negtri = es5.enter_context(nc.sbuf_tensor("a_negtri", [128, 2, 512], BF16))
            cmask = es5.enter_context(nc.sbuf_tensor("a_cmask", [128, 17, 512], BF16))
            btab = es5.enter_context(nc.sbuf_tensor("a_btab", [128, 32, 16], F32))
            bC = es5.enter_context(nc.sbuf_tensor("a_bC", [128, 2, 32, 16], F32))
            cms = es5.enter_context(nc.sbuf_tensor("a_cms", [128, 128], F32))
            ads = es5.enter_context(nc.sbuf_tensor("a_ads", [128, 128], F32))
            ovl = es5.enter_context(nc.sbuf_tensor("a_ovl", [128, 2, 64], BF16))
            kcmpT = es5.enter_context(nc.sbuf_tensor("a_kcmp", [64, NB, NG, 256], BF16))
            vcmp = es5.enter_context(nc.sbuf_tensor("a_vcmp", [128, NB, 2, NG, 65], BF16))
            w2sb = es5.enter_context(nc.sbuf_tensor("a_w2", [128, 2, 2, 64], BF16))
            posT = es5.enter_context(nc.sbuf_tensor("a_posT", [64, 2, 32], BF16))
            d_c = cx.dsem("ac")
            cx.dma("pool", d_c, ident[:], c_ident[:, :])
            cx.dma("pool", d_c, negtri[:], c_negtri[:, :, :])
            cx.dma("pool", d_c, cmask[:], c_cmask[:, :, :])
            cx.dma("pool", d_c, ovl[:], c_ovl[:, :, :])
            cx.dma("pool", d_c, posT[:, 0, :], nsa_posT[0])
            cx.dma("pool", d_c, posT[:, 1, :], nsa_posT[1])
            for kv, wsrc in enumerate((nsa_cmp_k_w2, nsa_cmp_v_w2)):
                for ec in range(2):
                    cx.dma("pool", d_c, w2sb[:, kv, ec, :], wsrc[0, ec * 128:(ec + 1) * 128, :])
            t_c1 = cx.dma("pool", d_c, cms[:], c_cms[:, :])
            cx.dma("sp", d_c, btab[:], c_btab[:, :, :])
            cx.dma("sp", d_c, bC[:], c_bC[:, :, :, :])
            t_c2 = cx.dma("sp", d_c, ads[:], c_ads[:, :])
            t_consts = t_c2
            cx.wait("dve", t_consts)
            dve.memset(kcmpT[:], 0.0)
            dve.memset(vcmp[:], 0.0)
            t_z = dsync(dve.memset(vcmp[:, :, :, :, 64:65], 1.0))

            with ExitStack() as es6:
                w1sb = es6.enter_context(nc.sbuf_tensor("c_w1", [64, 32, 256], BF16))
                csrc = es6.enter_context(nc.sbuf_tensor("c_src", [64, NG, T], BF16))
                hidT = es6.enter_context(nc.sbuf_tensor("c_hid", [128, 2, 256], BF16))
                posb = es6.enter_context(nc.sbuf_tensor("c_posb", [128, 2], F32))
                d_w1 = cx.dsem("cw1")
                d_src = cx.dsem("csrc")
                src_free = []
                w1_free = []
                t_hz = dsync(dve.memset(hidT[:], 0.0))
                hid_free = []
                for kv, (w1src, sdram) in enumerate(((nsa_cmp_k_w1, kc_d), (nsa_cmp_v_w1, vc_d))):
                    cx.wait("pool", *w1_free)
                    tw1 = None
                    for l0 in range(0, 32, 8):
                        tw1 = cx.dma("pool", d_w1, w1sb[:, l0:l0 + 8, :], w1src[0, l0:l0 + 8].rearrange("l d e -> d l e"))
                    cx.wait("pe", tw1, t_consts, *hid_free)
                    pbk = banks[7]
                    for ec in range(2):
                        tpb = _mm_group(cx, pbk[:, ec:ec + 1], [(w1sb[:, l_, ec * 128:(ec + 1) * 128], posT[:, kv, l_:l_ + 1]) for l_ in range(32)])
                    cx.wait("dve", tpb)
                    t_posb = dsync(dve.tensor_copy(out=posb[:], in_=pbk[:, 0:2]))
                    for b in range(NB):
                        cx.wait("sp", *src_free)
                        tsrc = cx.dma("sp", d_src, csrc[:], sdram[b].rearrange("g d t -> d g t"))
                        src_free = []
                        cx.wait("pe", tsrc, t_posb)
                        for g in range(NG):
                            hb = [banks[0], banks[1]]
                            thid = []
                            for ec in range(2):
                                cx.wait("pe", *hid_free)
                                th_ = _mm_group(cx, hb[ec][:, 0:255],
                                                [(w1sb[:, l_, ec * 128:(ec + 1) * 128], csrc[:, g, l_:l_ + 16 * 254 + 1:16]) for l_ in range(32)])
                                cx.wait("act", th_, t_posb, t_hz, *hid_free)
                                thid.append(cx.sig("act", act.activation(out=hidT[:, ec, 0:255], in_=hb[ec][:, 0:255], func=AF.Gelu_apprx_tanh,
                                                                         bias=posb[:, ec:ec + 1], scale=1.0)))
                            hid_free = []
                            cx.wait("pe", *thid)
                            if kv == 0:
                                ob = banks[2]
                                to = _mm_group(cx, ob[0:64, 0:255], [(w2sb[:, 0, ec, :], hidT[:, ec, 0:255]) for ec in range(2)])
                                cx.wait("dve", to, t_z)
                                te = cx.sig("dve", dve.tensor_copy(out=kcmpT[:, b, g, 0:255], in_=ob[0:64, 0:255]))
                            else:
                                ob = banks[2]
                                for nt in range(2):
                                    to = _mm_group(cx, ob[:, nt * 64:(nt + 1) * 64], [(hidT[:, ec, nt * 128:(nt + 1) * 128], w2sb[:, 1, ec, :]) for ec in range(2)])
                                cx.wait("dve", to, t_z)
                                te = cx.sig("dve", dve.tensor_copy(out=vcmp[:, b, :, g, 0:64], in_=ob[:, 0:128].rearrange("p (n d) -> p n d", n=2)))
                            hid_free = [te, Tok("pe", cx.sem["pe"], cx.cnt["pe"])]
                        src_free = [Tok("pe", cx.sem["pe"], cx.cnt["pe"])]
                    w1_free = [Tok("pe", cx.sem["pe"], cx.cnt["pe"])]
                cx.barrier()

            with ExitStack() as es7:
                KA = es7.enter_context(nc.sbuf_tensor("KA", [128, NG, T], BF16))
                KW = es7.enter_context(nc.sbuf_tensor("KW", [64, NG, T], BF16))
                VS = es7.enter_context(nc.sbuf_tensor("VS", [128, 32, NG, 65], BF16))
                VW = es7.enter_context(nc.sbuf_tensor("VW", [128, 32, NG, 65], BF16))
                QP = es7.enter_context(nc.sbuf_tensor("QP", [128, 3, NH * 128], BF16))
                GT = es7.enter_context(nc.sbuf_tensor("GT", [128, 3, 48], F32))
                PTb = es7.enter_context(nc.sbuf_tensor("PT", [128, 4, 512], BF16))
                otok = es7.enter_context(nc.sbuf_tensor("otok", [128, 2, D], F32))
                otb = es7.enter_context(nc.sbuf_tensor("otb", [128, D], BF16))
                oTs = es7.enter_context(nc.sbuf_tensor("oTs", [128, 2, KD, 128], BF16))
                with ExitStack() as es8:
                    rz = es8.enter_context(nc.sbuf_tensor("rz", [128, 3, 4], F32))
                    fac = es8.enter_context(nc.sbuf_tensor("fac", [128, 3, 4], F32))
                    imp = es8.enter_context(nc.sbuf_tensor("imp", [128, 64], F32))
                    sc = es8.enter_context(nc.sbuf_tensor("sc", [128, 64], F32))
                    sc2 = es8.enter_context(nc.sbuf_tensor("sc2", [128, 64], F32))
                    m8 = es8.enter_context(nc.sbuf_tensor("m8", [128, 2, 8], F32))
                    MB = es8.enter_context(nc.sbuf_tensor("MB", [128, 128], BF16))
                    d_kv = cx.dsem("kv")
                    d_q = [cx.dsem("q0"), cx.dsem("q1"), cx.dsem("q2")]
                    d_o = cx.dsem("ost")
                    cx.wait("dve", t_consts)
                    t_ones = dsync(dve.memset(MB[:], 0.0))
                    tpb_bf = banks[6][:, :].bitcast(BF16)
                    otp_bf = banks[7][:, :].bitcast(BF16)
                    otp_free = []
                    fr_ = {"cmp": [], "mb": [], "tp": [], "ow": [], "os": []}
                    sel_ready = {}
                    t_ind = None
                    for g in range(NG):
                        t_ind = cx.dma("pool", d_c, KA[64:128, g, :], c_ind[:, :])
                    qring = Ring([(QP[:, i], GT[:, i]) for i in range(3)])
                    sring = Ring(banks[0:2])
                    ptring = Ring([PTb[:, i] for i in range(4)])
                    oring = Ring([otok[:, i] for i in range(2)])
                    otring = Ring([oTs[:, i] for i in range(2)])
                    kv_free = []
                    t_ostore = None
                    otb_free = []
                    for b in range(NB):
                        cx.wait("sp", *kv_free, t_ones)
                        cx.dma("sp", d_kv, KA[0:64, :, :], ks_d[b].rearrange("g d t -> d g t"))
                        cx.dma("sp", d_kv, KW[:], kw_d[b].rearrange("g d t -> d g t"))
                        for k0 in range(0, 32, 8):
                            cx.dma("sp", d_kv, VS[:, k0:k0 + 8].rearrange("p k g d -> p k (g d)"),
                                   vs_d[b, k0 * 128:(k0 + 8) * 128, :].rearrange("(k p) c -> p k c", p=128))
                            t_kv = cx.dma("sp", d_kv, VW[:, k0:k0 + 8].rearrange("p k g d -> p k (g d)"),
                                          vw_d[b, k0 * 128:(k0 + 8) * 128, :].rearrange("(k p) c -> p k c", p=128))
                        qinfo = {}

                        def issue_q(qb):
                            qi, (qp, gt), fr = qring.next()
                            cx.wait("sp", *fr)
                            cx.dma("sp", d_q[qi], qp[0:64, :].rearrange("p (h t) -> p h t", t=128), q_d[b, :, :, qb * 128:(qb + 1) * 128].rearrange("h d t -> d h t"))
                            tq = cx.dma("sp", d_q[qi], gt, gate_d[b, qb * 128:(qb + 1) * 128, :])
                            qinfo[qb] = (qi, qp, gt, tq)


                        nqb = NQB if not ntiles_dbg else ntiles_dbg * 4
                        Q = {}

                        def prep_q(qbn):
                            qi_, (qp_, gt_), fr = qring.next()
                            cx.wait("sp", *fr)
                            cx.dma("sp", d_q[qi_], qp_[0:64, :].rearrange("p (h t) -> p h t", t=128), q_d[b, :, :, qbn * 128:(qbn + 1) * 128].rearrange("h d t -> d h t"))
                            tq_ = cx.dma("sp", d_q[qi_], gt_, gate_d[b, qbn * 128:(qbn + 1) * 128, :])
                            Q[qbn] = dict(qi=qi_, qp=qp_, gt=gt_, tq=tq_, readers=[], part2={})

                        def prep_o(qbn):
                            oi_, ot_, ofr_ = oring.next()
                            Q[qbn].update(oi=oi_, ot=ot_, ofr=ofr_)

                        def qk_tile(lhsT, rhs, extra=None):
                            si, sbk, fr = sring.next()
                            cx.wait("pe", *fr)
                            prs = [(lhsT, rhs)]
                            if extra is not None:
                                prs.append((ident[:], extra))
                            return si, sbk, _mm_group(cx, sbk[:, :], prs)

                        def exp_tile(si, sbk, ts, bias_of_r):
                            pi, pt, fr = ptring.next()
                            cx.wait("act", ts, *fr)
                            te = None
                            for r in range(4):
                                te = cx.sig("act", act.activation(out=pt[:, r * 128:(r + 1) * 128], in_=sbk[:, r * 128:(r + 1) * 128],
                                                                  func=AF.Exp, bias=bias_of_r(r), scale=1.0))
                            sring.release(si, te)
                            return pi, pt, te

                        pend = []
                        st = {"tl": None}

                        def flush():
                            while pend:
                                pend.pop(0)()

                        def issue(lhsT, rhs, extra, bias_fn, pv_fn):
                            si, sbk, ts = qk_tile(lhsT, rhs, extra)
                            pi, pt, te = exp_tile(si, sbk, ts, bias_fn)
                            flush()
                            pend.append(lambda: pv_fn(pi, pt, te))

                        def evac_cmp(qbn, g, tpv):
                            C_ = Q[qbn]
                            qp, gt, tq, ot = C_["qp"], C_["gt"], C_["tq"], C_["ot"]
                            ocb, impb = banks[2], banks[3]
                            cx.wait("dve", tpv, tq, *C_["ofr"])
                            ocv = ocb[:, 0:260].rearrange("p (r c) -> p r c", c=65)
                            dsync(dve.tensor_scalar(out=rz[:, 0, :], in0=ocv[:, :, 64], scalar1=1e-30, scalar2=None, op0=ALU.max))
                            dsync(dve.reciprocal(out=rz[:, 0, :], in_=rz[:, 0, :]))
                            gv = gt[:, 12 * g:12 * g + 12].rearrange("p (r k) -> p r k", k=3)
                            dsync(dve.tensor_tensor(out=fac[:, 0, :], in0=rz[:, 0, :], in1=gv[:, :, 0], op=ALU.mult))
                            dsync(dve.tensor_scalar(out=imp[:], in0=impb[:, 0:64], scalar1=rz[:, 0, 0:1], scalar2=None, op0=ALU.mult))
                            for r in range(1, 4):
                                dsync(dve.scalar_tensor_tensor(out=imp[:], in0=impb[:, r * 64:(r + 1) * 64], scalar=rz[:, 0, r:r + 1], in1=imp[:],
                                                               op0=ALU.mult, op1=ALU.add))
                            tlast = None
                            for r in range(4):
                                tlast = cx.sig("dve", dve.tensor_scalar(out=ot[:, (4 * g + r) * 64:(4 * g + r + 1) * 64], in0=ocv[:, r, 0:64],
                                                                        scalar1=fac[:, 0, r:r + 1], scalar2=None, op0=ALU.mult))
                            fr_["cmp"] = [tlast]
                            j0 = 64 - 2 * qbn
                            dsync(dve.tensor_tensor(out=sc[:], in0=imp[:], in1=cms[:, j0:j0 + 64], op=ALU.mult))
                            dsync(dve.tensor_tensor(out=sc[:], in0=sc[:], in1=ads[:, j0:j0 + 64], op=ALU.add))
                            dsync(dve.memset(sc[:, 0:1], 1e4))
                            dsync(dve.max(out=m8[:, 0, :], in_=sc[:]))
                            dsync(dve.match_replace(out=sc2[:], in_to_replace=m8[:, 0, :], in_values=sc[:], imm_value=-3e4))
                            dsync(dve.max(out=m8[:, 1, :], in_=sc2[:]))
                            cx.wait("dve", *fr_["mb"])
                            tmb = dsync(dve.tensor_scalar(out=MB[:, 64:128], in0=sc[:], scalar1=m8[:, 1, 7:8], scalar2=NEG, op0=ALU.is_lt, op1=ALU.mult))

                            def part2():
                                cx.wait("pe", tmb, *fr_["tp"])
                                ttp = cx.sig("pe", pe.transpose(tpb_bf[:, 0:128], MB[:], ident[:]))
                                fr_["mb"] = [ttp]
                                cx.wait("dve", ttp, tq)
                                tcp = dsync(dve.tensor_copy(out=qp[64:128, 4 * g * 128:(4 * g + 4) * 128].rearrange("p (r t) -> p r t", t=128),
                                                            in_=tpb_bf[64:128, 0:128].unsqueeze(1).to_broadcast([64, 4, 128])))
                                fr_["tp"] = [tcp]
                                sel_ready[(qbn, g)] = tcp
                            C_["part2"][g] = part2

                        def evac_acc(qbn, g, tpv, bank, slot, gcol, key):
                            C_ = Q[qbn]
                            gt, tq, ot = C_["gt"], C_["tq"], C_["ot"]
                            cx.wait("dve", tpv, tq)
                            ov = bank[:, 0:260].rearrange("p (r c) -> p r c", c=65)
                            dsync(dve.reciprocal(out=rz[:, slot, :], in_=ov[:, :, 64]))
                            gv = gt[:, 12 * g:12 * g + 12].rearrange("p (r k) -> p r k", k=3)
                            dsync(dve.tensor_tensor(out=fac[:, slot, :], in0=rz[:, slot, :], in1=gv[:, :, gcol], op=ALU.mult))
                            tl_ = None
                            for r in range(4):
                                osl = ot[:, (4 * g + r) * 64:(4 * g + r + 1) * 64]
                                tl_ = dsync(dve.scalar_tensor_tensor(out=osl, in0=ov[:, r, 0:64], scalar=fac[:, slot, r:r + 1], in1=osl,
                                                                     op0=ALU.mult, op1=ALU.add))
                            fr_[key] = [tl_]
                            st["tl"] = tl_

                        def phase_A(qbn, g):
                            C_ = Q[qbn]
                            qp, tq = C_["qp"], C_["tq"]
                            cx.wait("pe", tq, t_kv, t_ind)
                            qg = qp[0:64, 4 * g * 128:(4 * g + 4) * 128]
                            nts = [nt for nt in range(2) if 128 * nt < 8 * qbn + 7]
                            for nt in nts:
                                full = (128 * nt + 127 <= 8 * qbn - 2)
                                extra = None if full else cmask[:, (8 * qbn - 128 * nt) // 8, :]

                                def pv_c(pi, pt, te, g=g, nt=nt, nts=nts, qbn=qbn):
                                    ocb, impb = banks[2], banks[3]
                                    cx.wait("pe", te, *(fr_["cmp"] if nt == nts[0] else []))
                                    tpv = None
                                    for r in range(4):
                                        lt = pt[:, r * 128:(r + 1) * 128]
                                        pe.matmul(ocb[:, r * 65:(r + 1) * 65], lt, vcmp[:, b, nt, g, :], start=(nt == nts[0] and r == 0), stop=(nt == nts[-1]))
                                        tpv = cx.sig("pe", pe.matmul(impb[:, r * 64:(r + 1) * 64], lt, ovl[:, nt, :], start=(nt == nts[0] and r == 0), stop=(nt == nts[-1])))
                                    ptring.release(pi, tpv)
                                    if nt == nts[-1]:
                                        Q[qbn]["readers"].append(tpv)
                                        evac_cmp(qbn, g, tpv)

                                issue(kcmpT[:, b, g, nt * 128:(nt + 1) * 128], qg, extra,
                                      (lambda r, g=g, nt=nt, qbn=qbn: bC[:, nt, qbn, 4 * g + r:4 * g + r + 1]), pv_c)

                        def run_part2(qbn, g):
                            flush()
                            Q[qbn]["part2"].pop(g)()

                        def phase_BC(qb, g):
                            C_ = Q[qb]
                            qp = C_["qp"]
                            qg = qp[0:64, 4 * g * 128:(4 * g + 4) * 128]
                            qg2 = qp[:, 4 * g * 128:(4 * g + 4) * 128]
                            kts = [kt for kt in range(qb - 4, qb + 1) if kt >= 0]
                            for kt in kts:
                                extra = None
                                if kt == qb:
                                    extra = negtri[:, 0, :]
                                elif kt == qb - 4:
                                    extra = negtri[:, 1, :]

                                def pv_w(pi, pt, te, g=g, kt=kt, kts=kts):
                                    owb = banks[4]
                                    cx.wait("pe", te, *(fr_["ow"] if kt == kts[0] else []))
                                    tpv = None
                                    for r in range(4):
                                        tpv = cx.sig("pe", pe.matmul(owb[:, r * 65:(r + 1) * 65], pt[:, r * 128:(r + 1) * 128], VW[:, kt, g, :],
                                                                     start=(kt == kts[0] and r == 0), stop=(kt == kts[-1])))
                                    ptring.release(pi, tpv)
                                    if kt == kts[-1]:
                                        Q[qb]["readers"].append(tpv)
                                        evac_acc(qb, g, tpv, owb, 1, 2, "ow")

                                issue(KW[:, g, kt * 128:(kt + 1) * 128], qg, extra,
                                      (lambda r, g=g, kt=kt: btab[:, kt - qb + 31, 4 * g + r:4 * g + r + 1]), pv_w)
                            cx.wait("pe", sel_ready[(qb, g)])
                            for kt in range(qb + 1):
                                extra = negtri[:, 0, :] if kt == qb else None

                                def pv_s(pi, pt, te, g=g, kt=kt):
                                    osb = banks[5]
                                    cx.wait("pe", te, *(fr_["os"] if kt == 0 else []))
                                    tpv = None
                                    for r in range(4):
                                        tpv = cx.sig("pe", pe.matmul(osb[:, r * 65:(r + 1) * 65], pt[:, r * 128:(r + 1) * 128], VS[:, kt, g, :],
                                                                     start=(kt == 0 and r == 0), stop=(kt == qb)))
                                    ptring.release(pi, tpv)
                                    if kt == qb:
                                        Q[qb]["readers"].append(tpv)
                                        evac_acc(qb, g, tpv, osb, 2, 1, "os")

                                issue(KA[:, g, kt * 128:(kt + 1) * 128], qg2, extra,
                                      (lambda r, g=g, kt=kt: btab[:, kt - qb + 31, 4 * g + r:4 * g + r + 1]), pv_s)

                        prep_q(0)
                        if nqb > 1:
                            prep_q(1)
                        prep_o(0)
                        for g in range(NG):
                            phase_A(0, g)
                            run_part2(0, g)
                        for qb in range(nqb):
                            C0 = Q[qb]
                            qi, qp, gt, tq, oi, ot = C0["qi"], C0["qp"], C0["gt"], C0["tq"], C0["oi"], C0["ot"]
                            if qb + 2 < nqb:
                                prep_q(qb + 2)
                            if qb + 1 < nqb:
                                prep_o(qb + 1)
                            for g in range(NG):
                                if qb + 1 < nqb:
                                    phase_A(qb + 1, g)
                                phase_BC(qb, g)
                                if qb + 1 < nqb:
                                    run_part2(qb + 1, g)
                            flush()
                            tl_ = st["tl"]
                            qring.release(qi, *C0["readers"], tl_)


                            cx.wait("act", tl_, *otb_free)
                            tob = cx.sig("act", act.copy(out=otb[:], in_=ot[:]))
                            oring.release(oi, tob)
                            ti_, oT, fr = otring.next()
                            cx.wait("pe", tob, *otp_free)
                            ttp = None
                            for m in range(KD):
                                ttp = cx.sig("pe", pe.transpose(otp_bf[:, m * 128:(m + 1) * 128], otb[:, m * 128:(m + 1) * 128], ident[:]))
                            otb_free = [ttp]
                            cx.wait("dve", ttp, *fr)
                            tcp = dsync(dve.tensor_copy(out=oT.rearrange("p m t -> p (m t)"), in_=otp_bf[:, 0:KD * 128]))
                            otp_free = [tcp]
                            cx.wait("sp", tcp)
                            tst = None
                            for m in range(KD):
                                tst = cx.dma("sp", d_o, o_d[b, m * 128:(m + 1) * 128, qb * 128:(qb + 1) * 128], oT[:, m, :])
                            otring.release(ti_, tst)
                            t_ostore = tst
                        kv_free = [Tok("pe", cx.sem["pe"], cx.cnt["pe"])]
                    cx.barrier(extra=[t_ostore])

    def proj_pass(l, src, hid_src, w_src, dst):
        TN = 512
        NT = T // TN
        with ExitStack() as es9:
            wo = es9.enter_context(nc.sbuf_tensor("jwo", [128, KD, D], BF16))
            xbuf = es9.enter_context(nc.sbuf_tensor("jx", [128, 3, KD, TN], F32))
            hbuf = es9.enter_context(nc.sbuf_tensor("jh", [128, 2, KD, TN], BF16))
            d_w = cx.dsem("jw")
            t_w = load_w(d_w, wo, w_src, KD)
            w2st, t_stg = stage_begin(es9, l)
            ep = ep_glob.set_tn(TN)
            xring = Ring([xbuf[:, i] for i in range(3)])
            hring = Ring([hbuf[:, i] for i in range(2)])
            yring = Ring(banks[4:6])
            d_x = [cx.dsem(f"jx{i}") for i in range(3)]
            d_h = [cx.dsem(f"jh{i}") for i in range(2)]
            d_st = cx.dsem("jst")
            tiles = [(b, t) for b in range(NB) for t in range(ntiles_dbg or NT)]
            loads = {}

            def issue_load(idx):
                b, t = tiles[idx]
                xi, xs, fr = xring.next()
                hi, hs, frh = hring.next()
                cx.wait("sp", *fr, *frh)
                tk = None
                for m in range(KD):
                    tk = cx.dma("sp", d_x[xi], xs[:, m, :], src[b, m * 128:(m + 1) * 128, t * TN:(t + 1) * TN])
                th = None
                for m in range(KD):
                    th = cx.dma("sp", d_h[hi], hs[:, m, :], hid_src[b, m * 128:(m + 1) * 128, t * TN:(t + 1) * TN])
                loads[idx] = (xi, xs, tk, hi, hs, th)

            issue_load(0)
            t_store = None
            for idx, (b, t) in enumerate(tiles):
                xi, xs, tl, hi, hs, th = loads[idx]
                if idx + 1 < len(tiles):
                    issue_load(idx + 1)
                cx.wait("pe", th, t_w)
                tx = epilogue(ep, l, 0, b, xs, yring,
                              lambda m: [(wo[:, k, m * 128:(m + 1) * 128], hs[:, k, :]) for k in range(KD)],
                              "ln1_g", "ln1_b", tl)
                hring.release(hi, Tok("pe", cx.sem["pe"], cx.cnt["pe"]))
                cx.wait("sp", tx)
                ts = None
                for m in range(KD):
                    ts = cx.dma("sp", d_st, dst[b, m * 128:(m + 1) * 128, t * TN:(t + 1) * TN], xs[:, m, :])
                xring.release(xi, ts)
                t_store = ts
            cx.barrier(extra=[t_store, stage_end(w2st, t_stg)])


    cur = xT
    bufs = [actA, actB]
    nb = 0
    for l in layers:
        mixer = l % 3
        d1 = bufs[nb]
        nb ^= 1
        if mixer == 0:
            with nc.named_scope(f"conv{l}"):
                conv_pass(l, l // 3, cur, d1)
        elif mixer == 1:
            with nc.named_scope(f"pool{l}"):
                pool_pass(l, cur, d1)
        else:
            with nc.named_scope("nsa_proj"):
                nsa_proj_pass(l, cur)
            with nc.named_scope("nsa_attn"):
                nsa_attn_pass()
            with nc.named_scope("nsa_out"):
                proj_pass(l, cur, o_d, nsa_w_out[0], d1)
        d2 = outT if l == layers[-1] else bufs[nb]
        if d2 is not outT:
            nb ^= 1
        with nc.named_scope(f"ffn{l}"):
            ffn_pass(l, d1, d2)
        cur = d2
    return nc


_NSA_C = None


def _nsa_consts():
    global _NSA_C
    if _NSA_C is not None:
        return _NSA_C
    NEG = -30000.0
    c = {}
    c["c_ident"] = np.eye(128, dtype=np.float32)
    a = np.arange(128)[:, None]
    bq = np.arange(128)[None, :]
    nt0 = np.where(a > bq, NEG, 0.0).astype(np.float32)
    nt1 = np.where(a <= bq, NEG, 0.0).astype(np.float32)
    c["c_negtri"] = np.ascontiguousarray(np.stack([np.tile(nt0, (1, 4)), np.tile(nt1, (1, 4))], axis=1))
    cm = np.zeros((128, 17, 128), np.float32)
    for oi in range(17):
        cm[:, oi, :] = np.where(16 * (a - 8 * oi) + 31 > bq, NEG, 0.0)
    c["c_cmask"] = np.ascontiguousarray(np.tile(cm, (1, 1, 4)))
    n = np.arange(256)
    cs, ce = n * 16, n * 16 + 31
    ss = np.arange(64) * 64
    ov = ((cs[:, None] <= ss[None, :] + 63) & (ce[:, None] >= ss[None, :])).astype(np.float32)
    ov[255] = 0.0
    c["c_ovl"] = np.ascontiguousarray(ov.reshape(2, 128, 64).transpose(1, 0, 2))
    btr = (np.arange(128) >= 64).astype(np.int64)[:, None]
    jr = np.arange(128)[None, :] - 64
    c["c_cms"] = (jr <= btr - 2).astype(np.float32)
    ad = np.zeros((128, 128), np.float32)
    ad[jr > btr] = -1e4
    ad[(jr == btr) | (jr == btr - 1)] = 1e4
    c["c_ads"] = ad
    slopes = np.exp2(-np.arange(1, 17, dtype=np.float64) / 2.0)
    dl = np.arange(32) - 31
    bt = slopes[None, None, :] * (128.0 * dl[None, :, None] + np.arange(128)[:, None, None] - 64.0)
    c["c_btab"] = bt.astype(np.float32)
    nl = np.arange(128)[:, None, None, None]
    ntt = np.arange(2)[None, :, None, None]
    qb = np.arange(32)[None, None, :, None]
    bc = slopes[None, None, None, :] * (16.0 * (128 * ntt + nl) + 15.5 - 128.0 * qb - 64.0)
    c["c_bC"] = bc.astype(np.float32)
    ind = (np.arange(T)[None, :] // 64 == np.arange(64)[:, None]).astype(np.float32)
    c["c_ind"] = np.ascontiguousarray(ind)
    _NSA_C = c
    return c


def host_prep(inputs, core):
    b0 = core * NB
    f = {k: np.asarray(v) for k, v in inputs.items()}
    m = {}
    m["xT"] = np.ascontiguousarray(np.transpose(f["x"][b0:b0 + NB], (0, 2, 1)))
    m["condT"] = np.ascontiguousarray(f["c"][b0:b0 + NB].reshape(NB, KD, 128).transpose(2, 1, 0))
    vecs = np.zeros((128, NV), np.float32)

    def put(name, arr):
        a = _pm(arr.reshape(-1))
        vecs[:, VOFF[name]:VOFF[name] + a.shape[1]] = a

    put("ada_b", f["ada_b"])
    put("ln1_g", f["ln1_g"])
    put("ln1_b", f["ln1_b"])
    put("ln2_g", f["ln2_g"])
    put("ln2_b", f["ln2_b"])
    put("ffn_conv", f["ffn_conv"])
    put("conv_w", f["conv_w"])
    put("pool_scale", f["pool_scale"])
    rc = np.ones((4, 16), np.float32)
    for g, w in enumerate(POOL_W):
        cnt = np.minimum(np.arange(1, 17), w).astype(np.float32)
        rc[g] = (np.float32(1.0) / cnt) * np.float32(w)
    for g, w in enumerate(POOL_W):
        cnt = np.minimum(np.arange(1, 17), w).astype(np.float32)
        rc[g] = np.float32(1.0) / cnt
    vecs[:, VOFF["pool_rc"]:VOFF["pool_rc"] + 64] = rc.reshape(1, 64)
    m["vecs"] = vecs
    m.update(_nsa_consts())
    m["nsa_posT"] = np.ascontiguousarray(np.stack([f["nsa_cmp_pos_k"][0].T, f["nsa_cmp_pos_v"][0].T]).astype(np.float32))
    for k in ("nsa_w_in", "nsa_cmp_k_w1", "nsa_cmp_k_w2", "nsa_cmp_v_w1", "nsa_cmp_v_w2", "nsa_w_out"):
        m[k] = np.ascontiguousarray(f[k], dtype=np.float32)
    for k in ("ada_w", "ffn_w_in", "ffn_w_out", "conv_w_in", "conv_w_out", "pool_w_in", "pool_w_grp", "pool_w_out"):
        m[k] = np.ascontiguousarray(f[k], dtype=np.float32)
    return m


def kernel(**inputs):
    nc = build_program()
    in_maps = [host_prep(inputs, c) for c in range(8)]
    res = run_bass_kernel_spmd(nc, in_maps, core_ids=list(range(8)))
    outs = [np.transpose(r["outT"], (0, 2, 1)) for r in res.results]
    return np.ascontiguousarray(np.concatenate(outs, axis=0), dtype=np.float32)
```
